# Optimizing a Trainium2 kernel written in Bass

```python
import jax, jax.numpy as jnp
from jax import lax
import numpy as np

D_MODEL = 1024
BATCH = 16
SEQ = 2048
DEPTH = 2

CONV_WIDTH = D_MODEL // 2
CONV_KSIZE = 31
RWKV_WIDTH = D_MODEL // 2
RWKV_HEAD = 64
RWKV_HEADS = RWKV_WIDTH // RWKV_HEAD
W_LORA = 64
A_LORA = 32
RWKV_IN = 3 * RWKV_WIDTH + 2 * (W_LORA + A_LORA)
EVEN_COLS = 3 * CONV_WIDTH + RWKV_IN + RWKV_WIDTH
GLA_HEADS = 4
GLA_DK = D_MODEL // 2
GLA_DV = D_MODEL
GLA_HK = GLA_DK // GLA_HEADS
GLA_HV = GLA_DV // GLA_HEADS
GLA_RANK = 16
GLA_TAU = 16.0
GLA_CHUNK = 64
ODD_COLS = 2 * GLA_DK + 2 * GLA_DV + 2 * GLA_RANK
DEEPNORM_ALPHA = (2 * DEPTH) ** 0.25
DEEPNORM_BETA = (8 * DEPTH) ** -0.25
LN_EPS = 1e-5
RWKV_GN_EPS = 64e-5
RMS_EPS = 1e-6

kernel_name = "hybrid_conv_rwkv7_gla_deepnorm_encoder"


def _layernorm(x, g, b, eps):
    xf = x.astype(jnp.float32)
    mu = jnp.mean(xf, -1, keepdims=True)
    var = jnp.mean(jnp.square(xf - mu), -1, keepdims=True)
    return ((xf - mu) * lax.rsqrt(var + eps) * g + b).astype(x.dtype)


def _token_shift(z, mu):
    prev = jnp.pad(z[:, :-1], ((0, 0), (1, 0), (0, 0)))
    nxt = jnp.pad(z[:, 1:], ((0, 0), (0, 1), (0, 0)))
    return z + mu * (0.5 * (prev + nxt) - z)


def _conformer_conv(val, glu_gate, conv_w, conv_b, ln_g, ln_b):
    u = val * jax.nn.sigmoid(glu_gate)
    u = lax.conv_general_dilated(
        u, conv_w[:, None, :], window_strides=(1,),
        padding=((CONV_KSIZE // 2, CONV_KSIZE // 2),),
        dimension_numbers=('NWC', 'WIO', 'NWC'),
        feature_group_count=CONV_WIDTH) + conv_b
    u = _layernorm(u, ln_g, ln_b, LN_EPS)
    return jax.nn.silu(u)


def _heads(t, n_heads):
    return t.reshape(t.shape[:-1] + (n_heads, t.shape[-1] // n_heads))


def _rwkv7_bidir(z, mu_shift, w0, w_up, a0, a_up, k_k, k_a, r_k, gn_g, gn_b):
    Bn, T, _ = z.shape
    f32 = jnp.float32
    z = _token_shift(z, mu_shift)
    r, k, v, lora = jnp.split(z, [RWKV_WIDTH, 2 * RWKV_WIDTH, 3 * RWKV_WIDTH], axis=-1)
    lora = lora.reshape(Bn, T, 2, W_LORA + A_LORA)
    wl, al = lora[..., :W_LORA], lora[..., W_LORA:]
    w_pre = (w0 + jnp.einsum('btdr,drc->btdc', jnp.tanh(wl), w_up)).astype(f32)
    log_w = -jnp.exp(-jax.nn.softplus(-w_pre) - 0.5)
    a = jax.nn.sigmoid((a0 + jnp.einsum('btdr,drc->btdc', al, a_up)).astype(f32))
    rf, kf, vf = r.astype(f32), k.astype(f32), v.astype(f32)
    kk = _heads(kf * k_k.astype(f32), RWKV_HEADS)
    kk = kk * lax.rsqrt(jnp.sum(kk * kk, -1, keepdims=True) + 1e-12)
    k_mod = kf[:, :, None] * (1.0 + (a - 1.0) * k_a.astype(f32))
    rh, vh = _heads(rf, RWKV_HEADS), _heads(vf, RWKV_HEADS)
    kmh = _heads(k_mod, RWKV_HEADS)
    wh = _heads(jnp.exp(log_w), RWKV_HEADS)
    ah = _heads(a, RWKV_HEADS)

    def shared(t):
        return jnp.moveaxis(jnp.stack([t, t[:, ::-1]], 0), 2, 0)

    def per_dir(t):
        return jnp.moveaxis(jnp.stack([t[:, :, 0], t[:, ::-1, 1]], 0), 2, 0)

    def step(S, inp):
        r_t, w_t, k_t, v_t, kk_t, a_t = inp
        sa = jnp.einsum('dbhvk,dbhk->dbhv', S, -kk_t)
        S = (S * w_t[..., None, :] + sa[..., :, None] * (kk_t * a_t)[..., None, :]
             + v_t[..., :, None] * k_t[..., None, :])
        return S, jnp.einsum('dbhvk,dbhk->dbhv', S, r_t)

    S0 = jnp.zeros((2, Bn, RWKV_HEADS, RWKV_HEAD, RWKV_HEAD), f32)
    _, ys = lax.scan(step, S0, (shared(rh), per_dir(wh), per_dir(kmh), shared(vh),
                                 shared(kk), per_dir(ah)))
    ys = jnp.moveaxis(ys, 0, 2)
    y = ys[0] + ys[1][:, ::-1]
    mu = jnp.mean(y, -1, keepdims=True)
    var = jnp.mean(jnp.square(y - mu), -1, keepdims=True)
    y = ((y - mu) * lax.rsqrt(var + RWKV_GN_EPS)).reshape(Bn, T, RWKV_WIDTH) * gn_g + gn_b
    k_bonus = jnp.mean(kmh, axis=2)
    bonus = jnp.sum(rh * k_bonus * r_k.astype(f32), -1, keepdims=True) * vh
    return (y + bonus.reshape(Bn, T, RWKV_WIDTH)).astype(z.dtype)


def _even_mixer(x, w_in, conv_w, conv_b, conv_ln_g, conv_ln_b, mu_shift, w0, w_up, a0, a_up,
                k_k, k_a, r_k, gn_g, gn_b, w_out):
    proj = jnp.einsum('btd,de->bte', x, w_in)
    c_val, c_glu, gate_a, rwkv_in, gate_b = jnp.split(
        proj, [CONV_WIDTH, 2 * CONV_WIDTH, 3 * CONV_WIDTH, 3 * CONV_WIDTH + RWKV_IN], axis=-1)
    y_a = _conformer_conv(c_val, c_glu, conv_w, conv_b, conv_ln_g, conv_ln_b) * jax.nn.silu(gate_a)
    y_b = _rwkv7_bidir(rwkv_in, mu_shift, w0, w_up, a0, a_up, k_k, k_a, r_k, gn_g, gn_b) * jax.nn.silu(gate_b)
    return jnp.einsum('btc,cd->btd', jnp.concatenate([y_a, y_b], -1), w_out)


def _gla_chunked(q, k, v, log_a):
    Bn, H, T, dk = q.shape
    dv = v.shape[-1]
    nC = T // GLA_CHUNK
    q = q.reshape(Bn, H, nC, GLA_CHUNK, dk)
    k = k.reshape(Bn, H, nC, GLA_CHUNK, dk)
    v = v.reshape(Bn, H, nC, GLA_CHUNK, dv)
    b = jnp.cumsum(log_a.reshape(Bn, H, nC, GLA_CHUNK, dk), axis=-2)
    b_last = b[..., -1:, :]
    q_t = q * jnp.exp(b)
    k_t = k * jnp.exp(-b)
    mask = jnp.tril(jnp.ones((GLA_CHUNK, GLA_CHUNK), bool))
    scores = jnp.where(mask, jnp.einsum('bhnik,bhnjk->bhnij', q_t, k_t), 0.0)
    o_intra = jnp.einsum('bhnij,bhnjv->bhniv', scores, v)
    U = jnp.einsum('bhnjk,bhnjv->bhnkv', k * jnp.exp(b_last - b), v)
    decay = jnp.exp(b_last[..., 0, :])

    def step(S, inp):
        d_c, u_c = inp
        return d_c[..., None] * S + u_c, S

    S0 = jnp.zeros((Bn, H, dk, dv), q.dtype)
    _, S_prev = lax.scan(step, S0, (jnp.moveaxis(decay, 2, 0), jnp.moveaxis(U, 2, 0)))
    S_prev = jnp.moveaxis(S_prev, 0, 2)
    o_inter = jnp.einsum('bhnik,bhnkv->bhniv', q_t, S_prev)
    return (o_intra + o_inter).reshape(Bn, H, T, dv)


def _odd_mixer(x, w_in, g_up, g_bias, norm_g, w_out):
    Bn, T, _ = x.shape
    f32 = jnp.float32
    proj = jnp.einsum('btd,de->bte', x, w_in)
    q, k, v, gate, lr = jnp.split(
        proj, [GLA_DK, 2 * GLA_DK, 2 * GLA_DK + GLA_DV, 2 * GLA_DK + 2 * GLA_DV], axis=-1)
    lr = lr.reshape(Bn, T, 2, GLA_RANK)
    log_a = jax.nn.log_sigmoid(
        (jnp.einsum('btdr,drc->btdc', lr, g_up) + g_bias).astype(f32)) / GLA_TAU

    def hd(t):
        return t.astype(f32).reshape(Bn, T, GLA_HEADS, -1).transpose(0, 2, 1, 3)

    qh, kh, vh = hd(q) * (GLA_HK ** -0.5), hd(k), hd(v)
    la_f, la_b = hd(log_a[:, :, 0]), hd(log_a[:, :, 1])
    o_f = _gla_chunked(qh, kh, vh, la_f)
    o_b = _gla_chunked(qh[:, :, ::-1], kh[:, :, ::-1], vh[:, :, ::-1], la_b[:, :, ::-1])[:, :, ::-1]
    o = o_f + o_b
    o = o * lax.rsqrt(jnp.mean(o * o, -1, keepdims=True) + RMS_EPS)
    o = o.transpose(0, 2, 1, 3).reshape(Bn, T, GLA_DV) * norm_g
    o = o.astype(x.dtype) * jax.nn.silu(gate)
    return jnp.einsum('btc,cd->btd', o, w_out)


def setup_inputs(seed: int = 0) -> dict:
    key = jax.random.key(seed)
    ks = jax.random.split(key, 32)

    def nrm(i, shape, s):
        return jax.random.normal(ks[i], shape, jnp.float32) * s

    return {
        "x": nrm(0, (BATCH, SEQ, D_MODEL), 1.0),
        "l0_w_in": nrm(1, (D_MODEL, EVEN_COLS), D_MODEL ** -0.5),
        "l0_conv_w": nrm(2, (CONV_KSIZE, CONV_WIDTH), CONV_KSIZE ** -0.5),
        "l0_conv_b": nrm(3, (CONV_WIDTH,), 0.01),
        "l0_conv_ln_g": 1.0 + nrm(4, (CONV_WIDTH,), 0.02),
        "l0_conv_ln_b": nrm(5, (CONV_WIDTH,), 0.02),
        "l0_mu_shift": jax.random.uniform(ks[6], (RWKV_IN,), jnp.float32, 0.2, 0.8),
        "l0_w0": jax.random.uniform(ks[7], (2, RWKV_WIDTH), jnp.float32, -5.0, -1.0),
        "l0_w_up": nrm(8, (2, W_LORA, RWKV_WIDTH), 0.1),
        "l0_a0": nrm(9, (2, RWKV_WIDTH), 0.1),
        "l0_a_up": nrm(10, (2, A_LORA, RWKV_WIDTH), 0.5 * A_LORA ** -0.5),
        "l0_k_k": 0.85 + nrm(11, (RWKV_WIDTH,), 0.02),
        "l0_k_a": 1.0 + nrm(12, (RWKV_WIDTH,), 0.02),
        "l0_r_k": nrm(13, (RWKV_HEADS, RWKV_HEAD), 0.1),
        "l0_gn_g": 1.0 + nrm(14, (RWKV_WIDTH,), 0.02),
        "l0_gn_b": nrm(15, (RWKV_WIDTH,), 0.02),
        "l0_w_out": nrm(16, (CONV_WIDTH + RWKV_WIDTH, D_MODEL), DEEPNORM_BETA * (CONV_WIDTH + RWKV_WIDTH) ** -0.5),
        "l0_ln_g": 1.0 + nrm(17, (D_MODEL,), 0.02),
        "l0_ln_b": nrm(18, (D_MODEL,), 0.02),
        "l1_w_in": nrm(19, (D_MODEL, ODD_COLS), D_MODEL ** -0.5),
        "l1_g_up": nrm(20, (2, GLA_RANK, GLA_DK), 0.5 * GLA_RANK ** -0.5),
        "l1_g_bias": nrm(21, (2, GLA_DK), 0.1),
        "l1_norm_g": 1.0 + nrm(22, (GLA_DV,), 0.02),
        "l1_w_out": nrm(23, (GLA_DV, D_MODEL), DEEPNORM_BETA * GLA_DV ** -0.5),
        "l1_ln_g": 1.0 + nrm(24, (D_MODEL,), 0.02),
        "l1_ln_b": nrm(25, (D_MODEL,), 0.02),
    }


def reference(x, l0_w_in, l0_conv_w, l0_conv_b, l0_conv_ln_g, l0_conv_ln_b, l0_mu_shift,
              l0_w0, l0_w_up, l0_a0, l0_a_up, l0_k_k, l0_k_a, l0_r_k, l0_gn_g, l0_gn_b,
              l0_w_out, l0_ln_g, l0_ln_b, l1_w_in, l1_g_up, l1_g_bias, l1_norm_g, l1_w_out,
              l1_ln_g, l1_ln_b):
    even = (l0_w_in, l0_conv_w, l0_conv_b, l0_conv_ln_g, l0_conv_ln_b, l0_mu_shift, l0_w0,
            l0_w_up, l0_a0, l0_a_up, l0_k_k, l0_k_a, l0_r_k, l0_gn_g, l0_gn_b, l0_w_out)
    odd = (l1_w_in, l1_g_up, l1_g_bias, l1_norm_g, l1_w_out)
    norms = ((l0_ln_g, l0_ln_b), (l1_ln_g, l1_ln_b))
    for i in range(DEPTH):
        if i % 2 == 0:
            h = _even_mixer(x, *even)
        else:
            h = _odd_mixer(x, *odd)
        g, b = norms[i]
        x = _layernorm(DEEPNORM_ALPHA * x + h, g, b, LN_EPS)
    return x
```

```python
import contextlib
import numpy as np
import concourse.bass as bass
import concourse.mybir as mybir
from concourse.bass_utils import run_bass_kernel_spmd

F32 = mybir.dt.float32
BF16 = mybir.dt.bfloat16
AF = mybir.ActivationFunctionType
ALU = mybir.AluOpType
AX = mybir.AxisListType

ENGS = ("tensor", "vector", "scalar", "gpsimd", "sync")

D = 1024
EVEN_COLS = 3776
ODD_COLS = 3104
ALPHA = float((2 * 2) ** 0.25)
CW = -float(np.exp(-0.5))


class Prog:
    NDMA_SEMS = 24

    def __init__(self, nc):
        self.nc = nc
        self.ops = {e: [] for e in ENGS}
        self.cnt = {e: 0 for e in ENGS}
        self.pending = {e: False for e in ENGS}
        self.waited = {e: {} for e in ENGS}
        self.last_w = {}
        self.readers = {}
        self.dma_i = 0
        self.dma_j = 0
        self.dma_cnt = [0] * self.NDMA_SEMS
        self.dma_last = [None] * self.NDMA_SEMS

    def _deps(self, eng, reads, writes):
        need = {}

        def add(t):
            if t is None:
                return
            k, v, te = t
            if te == eng and eng == "tensor":
                return
            if need.get(k, 0) < v:
                need[k] = v

        for r in reads:
            add(self.last_w.get(r))
        for w in writes:
            add(self.last_w.get(w))
            for t in self.readers.get(w, ()):
                if t[2] == eng and t[0][0] == "e":
                    continue
                add(t)
        out = []
        for k, v in need.items():
            if self.waited[eng].get(k, 0) >= v:
                continue
            self.waited[eng][k] = v
            out.append((k, v))
        return out

    def _commit(self, ticket, reads, writes):
        for r in reads:
            self.readers.setdefault(r, []).append(ticket)
        for w in writes:
            self.last_w[w] = ticket
            self.readers[w] = []

    def op(self, eng, fn, reads=(), writes=(), signal=True):
        waits = self._deps(eng, reads, writes)
        if signal:
            self.cnt[eng] += 1
            ticket = (("e", eng), self.cnt[eng], eng)
            self.pending[eng] = False
        else:
            ticket = (("e", eng), self.cnt[eng] + 1, eng)
            self.pending[eng] = True
        self.ops[eng].append((fn, waits, (("e", eng), 1) if signal else None))
        self._commit(ticket, reads, writes)
        return ticket

    def dma(self, fn, reads=(), writes=(), eng="sync"):
        half = self.NDMA_SEMS // 2
        if eng == "sync":
            i = self.dma_i % half
            self.dma_i += 1
        else:
            i = half + self.dma_j % half
            self.dma_j += 1
        waits = self._deps(eng, reads, writes)
        prev = self.dma_last[i]
        if prev is not None and self.waited[eng].get(prev[0], 0) < prev[1]:
            self.waited[eng][prev[0]] = prev[1]
            waits.append((prev[0], prev[1]))
        self.dma_cnt[i] += 16
        ticket = (("d", i), self.dma_cnt[i], eng)
        self.dma_last[i] = ticket
        self.ops[eng].append((fn, waits, (("d", i), 16)))
        self._commit(ticket, reads, writes)
        return ticket

    def emit(self, st, final_tickets):
        nc = self.nc
        for e in ENGS:
            assert not self.pending[e], f"unsignalled trailing op on {e}"
        sems = {}
        for e in ENGS:
            sems[("e", e)] = st.enter_context(nc.semaphore(f"s_{e}"))
        for i in range(self.NDMA_SEMS):
            sems[("d", i)] = st.enter_context(nc.semaphore(f"s_d{i}"))
        block = st.enter_context(nc.Block())

        def run(engname):
            def body(eng):
                for fn, waits, inc in self.ops[engname]:
                    for k, v in waits:
                        eng.wait_ge(sems[k], v)
                    ins = fn(eng)
                    if inc is not None:
                        ins.then_inc(sems[inc[0]], inc[1])
                if engname == "sync":
                    for k, v, _ in final_tickets:
                        eng.wait_ge(sems[k], v)
            return body

        block.tensor(run("tensor"))
        block.vector(run("vector"))
        block.scalar(run("scalar"))
        block.gpsimd(run("gpsimd"))
        block.sync(run("sync"))


class Ring:
    def __init__(self, st, nc, name, n, shape, dtype, psum=False):
        self.bufs = []
        for i in range(n):
            if psum:
                h = st.enter_context(nc.psum_tensor(f"{name}{i}", shape, dtype))
            else:
                h = st.enter_context(nc.sbuf_tensor(f"{name}{i}", shape, dtype))
            self.bufs.append((h, (name, i)))
        self.i = 0

    def next(self):
        b = self.bufs[self.i % len(self.bufs)]
        self.i += 1
        return b


def _chunks(v, n):
    return np.ascontiguousarray(v.reshape(n, 128).T)


def pack_l0(inp):
    cols = {}
    parts = []
    pos = 0

    def add(name, arr):
        nonlocal pos
        arr = np.asarray(arr, np.float32)
        if arr.shape[0] < 128:
            pad = np.zeros((128 - arr.shape[0], arr.shape[1]), np.float32)
            arr = np.concatenate([arr, pad], 0)
        cols[name] = pos
        pos += arr.shape[1]
        parts.append(arr)

    cw = np.asarray(inp["l0_conv_w"])
    for c in range(4):
        add(f"convw{c}", cw[:, c * 128:(c + 1) * 128].T)
    add("convb", _chunks(inp["l0_conv_b"], 4))
    add("clng", _chunks(inp["l0_conv_ln_g"], 4))
    add("clnb", _chunks(inp["l0_conv_ln_b"], 4))
    mu = np.asarray(inp["l0_mu_shift"])
    add("mu_rkv", _chunks(mu[:1536], 12))
    lo = mu[1536:]
    add("mu_wl", np.concatenate([lo[0:64], lo[96:160]])[:, None])
    add("mu_al", np.concatenate([lo[64:96], lo[160:192]])[:, None])
    add("a0_0", _chunks(inp["l0_a0"][0], 4))
    add("a0_1", _chunks(inp["l0_a0"][1], 4))
    add("k_k", _chunks(inp["l0_k_k"], 4))
    add("k_a", _chunks(inp["l0_k_a"], 4))
    add("r_k", _chunks(np.asarray(inp["l0_r_k"]).reshape(-1), 4))
    add("gn_g", _chunks(inp["l0_gn_g"], 4))
    add("gn_b", _chunks(inp["l0_gn_b"], 4))
    return np.concatenate(parts, 1), cols


class K:
    def __init__(self, T, NSEQ, dbg=None):
        self.T, self.NSEQ = T, NSEQ
        self.TB = min(512, T)
        self.NTB = T // self.TB
        self.NT = T // 128
        self.NCH = T // 64
        self.dbg = dbg
        self.nc = bass.Bass("TRN2", target_bir_lowering=False)
        self.p = Prog(self.nc)
        self.st = contextlib.ExitStack()
        self.finals = []

    def sb(self, name, shape, dt=F32):
        return self.st.enter_context(self.nc.sbuf_tensor(name, list(shape), dt))

    def ring(self, name, n, shape, dt=F32, psum=False):
        return Ring(self.st, self.nc, name, n, list(shape), dt, psum)

    def din(self, name, shape, dt=F32):
        return self.nc.dram_tensor(name, list(shape), dt, kind="ExternalInput").ap()

    def dout(self, name, shape, dt=F32):
        return self.nc.dram_tensor(name, list(shape), dt, kind="ExternalOutput").ap()

    def dscr(self, name, shape, dt=F32):
        return self.nc.dram_tensor(name, list(shape), dt, kind="Internal").ap()

    def mm(self, out, lhsT, rhs, start, stop, reads, writes, signal=None):
        self.p.op("tensor", lambda e: e.matmul(out, lhsT=lhsT, rhs=rhs, start=start, stop=stop),
                  reads, writes, signal=(stop if signal is None else signal))

    def tr(self, out, in_, ident, reads, writes, signal=True):
        self.p.op("tensor", lambda e: e.transpose(out=out, in_=in_, identity=ident),
                  reads, writes, signal=signal)

    def act(self, out, in_, func, reads, writes, bias=0.0, scale=1.0):
        self.p.op("scalar", lambda e: e.activation(out=out, in_=in_, func=func, bias=bias, scale=scale),
                  reads, writes)

    def tt(self, out, in0, in1, op, reads, writes, eng="vector"):
        self.p.op(eng, lambda e: e.tensor_tensor(out=out, in0=in0, in1=in1, op=op), reads, writes)

    def ts(self, out, in0, s1, s2, op0, op1, reads, writes, eng="vector"):
        if op1 is None:
            self.p.op(eng, lambda e: e.tensor_scalar(out=out, in0=in0, scalar1=s1, scalar2=None, op0=op0),
                      reads, writes)
        else:
            self.p.op(eng, lambda e: e.tensor_scalar(out=out, in0=in0, scalar1=s1, scalar2=s2, op0=op0, op1=op1),
                      reads, writes)

    def stt(self, out, in0, scalar, in1, op0, op1, reads, writes):
        self.p.op("vector", lambda e: e.scalar_tensor_tensor(out=out, in0=in0, scalar=scalar, in1=in1,
                                                            op0=op0, op1=op1), reads, writes)

    def cp(self, out, in_, reads, writes, eng="vector"):
        if eng == "scalar":
            self.p.op("scalar", lambda e: e.copy(out=out, in_=in_), reads, writes)
        else:
            self.p.op(eng, lambda e: e.tensor_copy(out=out, in_=in_), reads, writes)

    def memset(self, ap, val, writes, eng="gpsimd"):
        self.p.op(eng, lambda e: e.memset(ap, val), (), writes)

    def fence(self, reads, writes):
        if not hasattr(self, "_fz"):
            self._fz = self.sb("fence_scratch", [128, 1])
        self.p.op("vector", lambda e: e.memset(self._fz[:, :], 0.0), reads, list(writes) + ["_fence_scratch"])

    def recip(self, out, in_, reads, writes):
        self.p.op("vector", lambda e: e.reciprocal(out=out, in_=in_), reads, writes)

    def dma(self, out, in_, reads, writes, eng="sync"):
        return self.p.dma(lambda e: e.dma_start(out=out, in_=in_), reads, writes, eng=eng)

    def dump(self, name, ap, shape, reads, dt=F32):
        o = self.dout(name, shape, dt)
        t = self.dma(o, ap, reads, [("dbg", name)])
        self.finals.append(t)

    def finish(self):
        self.p.emit(self.st, self.finals)
        self.st.close()
        return self.nc


def host_consts():
    c = {}
    c["ident"] = np.eye(128, dtype=np.float32)
    c["ones_ln"] = np.full((128, 128), 1.0 / 512.0, np.float32)
    bo = np.zeros((128, 128), np.float32)
    bo[:64, :64] = 1.0
    bo[64:, 64:] = 1.0
    c["blk64"] = bo
    tp = np.arange(128)[:, None]
    t = np.arange(128)[None, :]
    same = (tp // 64) == (t // 64)
    m = {}
    m["f_incl"] = same & (tp <= t)
    m["f_excl"] = same & (tp < t)
    m["f_rest"] = same & (tp > t)
    m["b_incl"] = same & (tp >= t)
    m["b_excl"] = same & (tp > t)
    m["b_rest"] = same & (tp < t)
    c["cmask"] = np.concatenate([m[k].astype(np.float32) for k in
                                 ("f_excl", "f_incl", "f_rest", "b_excl", "b_incl", "b_rest")], 1)
    j = np.arange(64)[:, None]
    tt = np.arange(64)[None, :]
    im = np.concatenate([(j < tt), (j <= tt), (j > tt), (j >= tt)], 1).astype(np.float32)
    im2 = np.concatenate([im, im], 0)
    c["maskA0"] = np.tile(im2[:, 0:128], (1, 8))
    c["maskA1"] = np.tile(im2[:, 128:256], (1, 8))
    c["maskN0"] = np.tile(im2[:, 128:192], (1, 8))
    c["maskN1"] = np.tile(im2[:, 0:64], (1, 8))
    e = np.concatenate([np.eye(64, dtype=np.float32)] * 2, 0)
    c["eye8"] = np.tile(e, (1, 8))
    c["ones128"] = np.ones((128, 128), np.float32)
    c["gmask0"] = np.tile((same & (tp <= t)).astype(np.float32), (1, 4))
    c["gmask1"] = np.tile((same & (tp >= t)).astype(np.float32), (1, 4))
    return c


BF_CONSTS = ("maskA0", "maskA1", "maskN0", "maskN1", "eye8", "gmask0", "gmask1")


def build_xT(k, s, x_dram, w_in, C, srckey=None):
    T, TB, NTB, NT = k.T, k.TB, k.NTB, k.NT
    xT = k.xT
    ident = C["ident"]

    for tt_ in range(NT):
        xin, xk = k.r_xin.next()
        rd = [(srckey, s, tt_)] if srckey else []
        k.dma(xin[:, 0:1024], x_dram[s * T + tt_ * 128: s * T + (tt_ + 1) * 128, :], rd, [xk])
        for g in range(2):
            ps, pk = k.r_ps.next()
            for j in range(4):
                fc = g * 4 + j
                k.tr(ps[:, j * 128:(j + 1) * 128], xin[:, fc * 128:(fc + 1) * 128], ident[:, :],
                     [xk, "ident"], [pk], signal=(j == 3))
            dst = xT[:, g * 4:(g + 1) * 4, tt_ * 128:(tt_ + 1) * 128]
            src = ps[:, :].rearrange("p (a b) -> p a b", a=4)
            k.cp(dst, src, [pk], [("xT", tt_)], eng=("scalar" if g == 0 else "vector"))
    xT_keys = [("xT", i) for i in range(NT)]
    k.xT_keys = xT_keys

    def proj(c0, m, tb, wt, wk):
        ps, pk = k.r_ps.next()
        for kc in range(8):
            k.mm(ps[0:m, 0:TB], wt[:, kc, 0:m], xT[:, kc, tb * TB:(tb + 1) * TB],
                 kc == 0, kc == 7, [wk] + xT_keys, [pk])
        return ps, pk

    k.proj = proj

    def load_w(c0, m):
        wt, wk = (k.r_w128 if m <= 128 else k.r_w512).next()
        k.dma(wt[:, :, 0:m], w_in[:, c0:c0 + m].rearrange("(kc p) m -> p kc m", p=128), [], [wk], eng="gpsimd")
        return wt, wk

    k.load_w = load_w


def build_l0_front(k, s, x_dram, w_in, pv, PC, C):
    T, TB, NTB, NT = k.T, k.TB, k.NTB, k.NT
    proj, load_w = k.proj, k.load_w
    cvs = []
    for c in range(4):
        wv, wvk = load_w(c * 128, 128)
        wg, wgk = load_w(512 + c * 128, 128)
        ub, uk = k.big[4]
        u = ub.bitcast(BF16)
        k.memset(u[:, 0:15], 0.0, [uk], eng="vector")
        k.memset(u[:, 15 + T:T + 32], 0.0, [uk], eng="vector")
        dgb, dgk = k.big[5]
        dg = dgb.bitcast(BF16)
        wc = PC[f"convw{c}"]
        for j in range(31):
            k.ts(dg[:, j * 128:(j + 1) * 128], k.identb[:, :], pv[:, wc + j:wc + j + 1], None, ALU.mult, None,
                 ["identb", "pv"], [dgk], eng="vector")
        for tb in range(NTB):
            psv, pvk = proj(c * 128, 128, tb, wv, wvk)
            psg, pgk = proj(512 + c * 128, 128, tb, wg, wgk)
            sg, sgk = k.r_tmp.next()
            k.act(sg[:, 0:TB], psg[:, 0:TB], AF.Sigmoid, [pgk], [sgk])
            k.tt(u[:, 15 + tb * TB:15 + (tb + 1) * TB], psv[:, 0:TB], sg[:, 0:TB], ALU.mult, [pvk, sgk], [uk])
        a0, a0k = k.big[c]
        for tb in range(NTB):
            ps, pk = k.r_ps.next()
            for j in range(31):
                k.mm(ps[:, 0:TB], dg[:, j * 128:(j + 1) * 128], u[:, tb * TB + j:tb * TB + j + TB],
                     j == 0, j == 30, [dgk, uk], [pk])
            k.act(a0[:, tb * TB:(tb + 1) * TB], ps[:, 0:TB], AF.Identity, [pk, "pv"], [a0k],
                  bias=pv[:, PC["convb"] + c:PC["convb"] + c + 1])
        cvs.append((a0, a0k))
    for tb in range(NTB):
        sl = slice(tb * TB, (tb + 1) * TB)
        psm, pmk = k.r_ps.next()
        for c in range(4):
            k.mm(psm[:, 0:TB], C["ones_ln"][:, :], cvs[c][0][:, sl], c == 0, c == 3, ["ones_ln", cvs[c][1]], [pmk])
        psq, pqk = k.r_ps.next()
        for c in range(4):
            sq, sqk = k.r_tmp.next()
            k.act(sq[:, 0:TB], cvs[c][0][:, sl], AF.Square, [cvs[c][1]], [sqk])
            k.mm(psq[:, 0:TB], C["ones_ln"][:, :], sq[:, 0:TB], c == 0, c == 3, ["ones_ln", sqk], [pqk])
        msq, msqk = k.r_tmp.next()
        k.act(msq[:, 0:TB], psm[:, 0:TB], AF.Square, [pmk], [msqk])
        var, vark = k.r_tmp.next()
        k.tt(var[:, 0:TB], psq[:, 0:TB], msq[:, 0:TB], ALU.subtract, [pqk, msqk], [vark])
        sd, sdk = k.r_tmp.next()
        k.act(sd[:, 0:TB], var[:, 0:TB], AF.Ln, [vark, "eps"], [sdk], bias=k.eps_ln[:, 0:1])
        rstd, rsk = k.r_keep.next()
        k.act(rstd[:, 0:TB], sd[:, 0:TB], AF.Exp, [sdk], [rsk], scale=-0.5)
        if k.dbg == "conv" and tb == 0:
            mm_, mmk = k.r_tmp.next()
            k.cp(mm_[:, 0:TB], psm[:, 0:TB], [pmk], [mmk])
            k.dump(f"dbg_mean{s}", mm_[:, 0:TB], [128, TB], [mmk])
            k.dump(f"dbg_var{s}", var[:, 0:TB], [128, TB], [vark])
            k.dump(f"dbg_rstd{s}", rstd[:, 0:TB], [128, TB], [rsk])
        for c in range(4):
            t1, t1k = k.r_tmp.next()
            k.tt(t1[:, 0:TB], cvs[c][0][:, sl], psm[:, 0:TB], ALU.subtract, [cvs[c][1], pmk], [t1k])
            t2, t2k = k.r_tmp.next()
            k.tt(t2[:, 0:TB], t1[:, 0:TB], rstd[:, 0:TB], ALU.mult, [t1k, rsk], [t2k])
            t3, t3k = k.r_tmp.next()
            k.ts(t3[:, 0:TB], t2[:, 0:TB], pv[:, PC["clng"] + c:PC["clng"] + c + 1],
                 pv[:, PC["clnb"] + c:PC["clnb"] + c + 1], ALU.mult, ALU.add, [t2k, "pv"], [t3k])
            s1, s1k = k.r_tmp.next()
            k.act(s1[:, 0:TB], t3[:, 0:TB], AF.Silu, [t3k], [s1k])
            if k.dbg == "conv" and tb == 0 and c == 0:
                k.dump(f"dbg_t3{s}", t3[:, 0:TB], [128, TB], [t3k])
                k.dump(f"dbg_s1{s}", s1[:, 0:TB], [128, TB], [s1k])
            if tb == 0:
                k.wga[c] = load_w(1024 + c * 128, 128)
            wt, wk = k.wga[c]
            psa, pak = proj(1024 + c * 128, 128, tb, wt, wk)
            sga, sgak = k.r_tmp.next()
            k.act(sga[:, 0:TB], psa[:, 0:TB], AF.Silu, [pak], [sgak])
            k.emit_y(s, c, tb, s1, sga, [s1k, sgak])


def pack_l0_cols():
    dummy = {
        "l0_conv_w": np.zeros((31, 512), np.float32), "l0_conv_b": np.zeros(512, np.float32),
        "l0_conv_ln_g": np.zeros(512, np.float32), "l0_conv_ln_b": np.zeros(512, np.float32),
        "l0_mu_shift": np.zeros(1728, np.float32), "l0_a0": np.zeros((2, 512), np.float32),
        "l0_k_k": np.zeros(512, np.float32), "l0_k_a": np.zeros(512, np.float32),
        "l0_r_k": np.zeros((8, 64), np.float32), "l0_gn_g": np.zeros(512, np.float32),
        "l0_gn_b": np.zeros(512, np.float32),
    }
    arr, cols = pack_l0(dummy)
    cols = dict(cols)
    cols["_n"] = arr.shape[1]
    return cols


def shifted_proj(k, wt, wk, m, mu_ap, out, outk, zb, sb_):
    T, TB, NTB = k.T, k.TB, k.NTB
    z, zk = zb
    s_, sk = sb_
    k.memset(z[0:m, 0:1], 0.0, [zk])
    k.memset(z[0:m, T + 1:T + 2], 0.0, [zk])
    for tb in range(NTB):
        ps, pk = k.r_ps.next()
        for kc in range(8):
            k.mm(ps[0:m, 0:TB], wt[:, kc, 0:m], k.xT[:, kc, tb * TB:(tb + 1) * TB],
                 kc == 0, kc == 7, [wk] + k.xT_keys, [pk])
        k.cp(z[0:m, 1 + tb * TB:1 + (tb + 1) * TB], ps[0:m, 0:TB], [pk], [zk], eng="scalar")
    k.tt(s_[0:m, 0:T], z[0:m, 0:T], z[0:m, 2:T + 2], ALU.add, [zk], [sk])
    k.stt(s_[0:m, 0:T], s_[0:m, 0:T], 0.5, z[0:m, 1:T + 1], ALU.mult, ALU.subtract, [sk, zk], [sk])
    k.stt(out[0:m, 0:T], s_[0:m, 0:T], mu_ap, z[0:m, 1:T + 1], ALU.mult, ALU.add, [sk, zk, "pv"], [outk])


def rwkv_stream(k, c, d, R_, C, T_, written_y, written_k):
    T, TB, NTB, NT, NCH = k.T, k.TB, k.NTB, k.NT, k.NCH
    CPB = TB // 64
    pv, PC = T_["pv"], T_["PC"]

    def pcol(name, i=0):
        return pv[:, PC[name] + i:PC[name] + i + 1]

    (rT, rk_), (kT, kk_), (vT, vk_), (kkt, kkk) = T_["rT"], T_["kT"], T_["vT"], T_["kkt"]
    (kms, kmsk), (yac, yack) = T_["kms"], T_["yac"]
    (tw, twk), (als, alk) = T_["tw"], T_["als"]
    S32, Sbf, S32key, Sbfkey = R_["S32"], R_["Sbf"], R_["S32key"], R_["Sbfkey"]
    if True:
        cur = 0
        k.memset(S32[cur], 0.0, [S32key[cur]], eng="vector")
        k.memset(Sbf[cur], 0.0, [Sbfkey[cur]], eng="vector")
        blocks = range(NTB) if d == 0 else range(NTB - 1, -1, -1)
        for tb in blocks:
            sl = slice(tb * TB, (tb + 1) * TB)
            psa, pak = k.r_ps.next()
            k.mm(psa[:, 0:TB], k.aup[d * 32:(d + 1) * 32, c * 128:(c + 1) * 128], als[d * 32:(d + 1) * 32, sl],
                 True, True, ["aup", alk], [pak])
            a_, ak_ = k.r_tmp.next()
            k.act(a_[:, 0:TB], psa[:, 0:TB], AF.Tanh, [pak, "ha0"], [ak_], bias=k.ha0[:, d * 4 + c:d * 4 + c + 1],
                  scale=0.5)
            ka, kak = k.r_tmp.next()
            k.ts(ka[:, 0:TB], a_[:, 0:TB], k.hka[:, c:c + 1], k.omk[:, c:c + 1], ALU.mult, ALU.add,
                 [ak_, "hka", "omk"], [kak])
            km, kmk = k.r_tmp.next()
            k.tt(km[:, 0:TB], kT[:, sl], ka[:, 0:TB], ALU.mult, [kk_, kak], [kmk])
            kmkey = (kmsk, tb)
            if tb not in written_k:
                written_k.add(tb)
                k.cp(kms[:, sl], km[:, 0:TB], [kmk], [kmkey], eng="gpsimd")
            else:
                k.tt(kms[:, sl], kms[:, sl], km[:, 0:TB], ALU.add, [kmkey, kmk], [kmkey], eng="gpsimd")
            be, bek = k.r_tmp.next()
            k.stt(be[:, 0:TB], a_[:, 0:TB], 1.0, kkt[:, sl], ALU.add, ALU.mult, [ak_, kkk], [bek])
            psw, pwk = k.r_ps.next()
            for i in range(TB // 128):
                tcol = slice(tb * TB + i * 128, tb * TB + (i + 1) * 128)
                k.mm(psw[:, i * 128:(i + 1) * 128], tw[d * 64:(d + 1) * 64, tcol],
                     k.wup[d * 64:(d + 1) * 64, c * 128:(c + 1) * 128], True, False, [twk, "wup"], [pwk])
                k.mm(psw[:, i * 128:(i + 1) * 128], C["ones128"][d * 64:d * 64 + 1, :],
                     k.w0x[d * 64:d * 64 + 1, c * 128:(c + 1) * 128], False, True,
                     ["ones128", "w0x"], [pwk])
            sig, sigk = k.r_tmp.next()
            k.act(sig[:, 0:TB], psw[:, 0:TB], AF.Tanh, [pwk], [sigk], scale=0.5)
            k.ts(sig[:, 0:TB], sig[:, 0:TB], 1.0, None, ALU.add, None, [sigk], [sigk])
            pcs = []
            for x in range(3):
                pc_, pck = k.r_ps.next()
                for i in range(TB // 128):
                    k.mm(pc_[:, i * 128:(i + 1) * 128], sig[:, i * 128:(i + 1) * 128],
                         C["cmask"][:, (d * 3 + x) * 128:(d * 3 + x + 1) * 128], True, True,
                         [sigk, "cmask"], [pck])
                pcs.append((pc_, pck))
            Ge, Gek = k.r_tmp.next()
            k.act(Ge[:, 0:TB], pcs[0][0][:, 0:TB], AF.Exp, [pcs[0][1]], [Gek], scale=0.5 * CW)
            Gi, Gik = k.r_tmp.next()
            k.act(Gi[:, 0:TB], pcs[1][0][:, 0:TB], AF.Exp, [pcs[1][1]], [Gik], scale=0.5 * CW)
            Gn, Gnk = k.r_tmp.next()
            k.act(Gn[:, 0:TB], pcs[1][0][:, 0:TB], AF.Exp, [pcs[1][1]], [Gnk], scale=-0.5 * CW)
            Gr, Grk = k.r_tmp.next()
            k.act(Gr[:, 0:TB], pcs[2][0][:, 0:TB], AF.Exp, [pcs[2][1]], [Grk], scale=0.5 * CW)
            gC, gCk = R_['gC']
            lastcol = 63 if d == 0 else 0
            k.act(gC[:, 0:CPB], pcs[1][0][:, 0:TB].rearrange("p (a b) -> p a b", b=64)[:, :, lastcol],
                  AF.Exp, [pcs[1][1]], [gCk], scale=0.5 * CW)
            ar, ark = R_['ar']
            arv = ar[:, 0:CPB, :]
            k.stt(arv[:, :, 0:64], kkt[:, sl].rearrange("p (a b) -> p a b", b=64), -1.0,
                  Ge[:, 0:TB].rearrange("p (a b) -> p a b", b=64), ALU.mult, ALU.mult, [kkk, Gek], [ark])
            k.tt(arv[:, :, 64:128], rT[:, sl].rearrange("p (a b) -> p a b", b=64),
                 Gi[:, 0:TB].rearrange("p (a b) -> p a b", b=64), ALU.mult, [rk_, Gik], [ark])
            bt, btk = R_['slots'][0]
            k.stt(bt[:, 0:TB], be[:, 0:TB], 0.5, Gn[:, 0:TB], ALU.mult, ALU.mult, [bek, Gnk], [btk])
            kt, ktk = R_['slots'][1]
            k.tt(kt[:, 0:TB], km[:, 0:TB], Gn[:, 0:TB], ALU.mult, [kmk, Gnk], [ktk])
            bg, bgk = R_['trans'].next()
            k.stt(bg[:, 0:TB], be[:, 0:TB], 0.5, Gr[:, 0:TB], ALU.mult, ALU.mult, [bek, Grk], [bgk])
            kg, kgk = R_['trans'].next()
            k.tt(kg[:, 0:TB], km[:, 0:TB], Gr[:, 0:TB], ALU.mult, [kmk, Grk], [kgk], eng="gpsimd")
            toks = []
            for ti_, (src, srck) in enumerate(((bg, bgk), (kg, kgk))):
                pt_, ptk = k.r_ps.next()
                pt = pt_.bitcast(BF16)
                for ci in range(CPB):
                    for hh in range(2):
                        hb = hh * 64
                        k.tr(pt[hb:hb + 64, ci * 64:(ci + 1) * 64], src[hb:hb + 64, ci * 64:(ci + 1) * 64],
                             k.identb[hb:hb + 64, hb:hb + 64], [srck, "identb"], [ptk],
                             signal=(ci == CPB - 1 and hh == 1))
                tk_, tkk = R_['slots'][3 + ti_]
                k.cp(tk_[:, 0:TB], pt[:, 0:TB], [ptk], [tkk], eng="scalar")
                toks.append((tk_, tkk))
            (tokB, tBk), (tokK, tKk) = toks

            def tv(ci_):
                g_ = tb * CPB + ci_
                t_, tk2 = T_["tokV"][g_ // 16]
                return t_[:, (g_ % 16) * 64:(g_ % 16 + 1) * 64], tk2
            yield
            p1a, p1ak = k.r_ps.next()
            p1b, p1bk = k.r_ps.next()
            p2a, p2ak = k.r_ps.next()
            p2b, p2bk = k.r_ps.next()
            p3, p3k = k.r_ps.next()
            HC = max(CPB // 2, 1)
            for ci in range(CPB):
                cc = slice(ci * 64, (ci + 1) * 64)
                P1, P1k = (p1a, p1ak) if ci < HC else (p1b, p1bk)
                P2, P2k = (p2a, p2ak) if ci < HC else (p2b, p2bk)
                o = (ci % HC) * 128
                last = (ci == CPB - 1) or (ci == HC - 1)
                for hh in range(2):
                    hb = hh * 64
                    sg_ = last and hh == 1
                    k.mm(P1[hb:hb + 64, o:o + 128], bt[hb:hb + 64, cc], ar[hb:hb + 64, ci, :], True, True, [btk, ark], [P1k], signal=sg_)
                    k.mm(P2[hb:hb + 64, o:o + 128], kt[hb:hb + 64, cc], ar[hb:hb + 64, ci, :], True, True, [ktk, ark], [P2k], signal=sg_)
                    k.mm(p3[hb:hb + 64, cc], ar[hb:hb + 64, ci, 0:64], bt[hb:hb + 64, cc], True, True, [btk, ark], [p3k], signal=(ci == CPB - 1 and hh == 1))
            A1, A1k = R_['A'][0]
            A2, A2k = R_['A'][1]
            HW_ = HC * 128
            mA = k.maskA[d]
            k.tt(A1[:, 0:HW_], p1a[:, 0:HW_], mA[:, 0:HW_], ALU.mult, [p1ak, "maskA"], [A1k])
            k.tt(A2[:, 0:HW_], p2a[:, 0:HW_], mA[:, 0:HW_], ALU.mult, [p2ak, "maskA"], [A2k])
            if CPB > 1:
                k.tt(A1[:, HW_:2 * HW_], p1b[:, 0:HW_], mA[:, 0:HW_], ALU.mult, [p1bk, "maskA"], [A1k])
                k.tt(A2[:, HW_:2 * HW_], p2b[:, 0:HW_], mA[:, 0:HW_], ALU.mult, [p2bk, "maskA"], [A2k])
            A1v = A1[:, 0:CPB * 128].rearrange("p (a b) -> p a b", b=128)
            A2v = A2[:, 0:CPB * 128].rearrange("p (a b) -> p a b", b=128)
            W_ = CPB * 64
            qs = R_['q']
            (Q, Qk), (QT, QTk), (R, Rk) = qs[0], qs[1], qs[2]
            yield
            k.tt(QT[:, 0:W_], p3[:, 0:W_], k.maskN[d][:, 0:W_], ALU.mult, [p3k, "maskN"], [QTk])
            k.tt(R[:, 0:W_].rearrange("p (a b) -> p a b", b=64), A1v[:, :, 0:64],
                 k.eye8[:, 0:W_].rearrange("p (a b) -> p a b", b=64), ALU.add, [A1k, "eye8"], [Rk])
            for lvl in range(5):
                pq, pqk = k.r_ps.next()
                pqt, pqtk = k.r_ps.next()
                need_q = lvl < 4
                for ci in range(CPB):
                    cc = slice(ci * 64, (ci + 1) * 64)
                    for hh in range(2):
                        hb = hh * 64
                        lastm = (ci == CPB - 1 and hh == 1)
                        Qop = A1v[hb:hb + 64, ci, 0:64] if lvl == 0 else Q[hb:hb + 64, cc]
                        Qopk = A1k if lvl == 0 else Qk
                        if need_q:
                            k.mm(pq[hb:hb + 64, cc], QT[hb:hb + 64, cc], Qop, True, True, [Qopk, QTk], [pqk], signal=lastm)
                        k.mm(pqt[hb:hb + 64, cc], Qop, QT[hb:hb + 64, cc], True, True, [Qopk, QTk], [pqtk], signal=lastm)
                o_ = 3 if lvl % 2 == 0 else 0
                (Q2, Q2k), (QT2, QT2k), (R2, R2k) = qs[o_], qs[o_ + 1], qs[o_ + 2]
                if need_q:
                    k.cp(Q2[:, 0:W_], pq[:, 0:W_], [pqk], [Q2k], eng="scalar")
                k.cp(QT2[:, 0:W_], pqt[:, 0:W_], [pqtk], [QT2k], eng="vector")
                yield
                pr, prk = k.r_ps.next()
                for ci in range(CPB):
                    cc = slice(ci * 64, (ci + 1) * 64)
                    for hh in range(2):
                        hb = hh * 64
                        k.mm(pr[hb:hb + 64, cc], QT2[hb:hb + 64, cc], R[hb:hb + 64, cc], True, True, [QT2k, Rk], [prk], signal=(ci == CPB - 1 and hh == 1))
                k.tt(R2[:, 0:W_], pr[:, 0:W_], R[:, 0:W_], ALU.add, [prk, Rk], [R2k])
                Q, Qk, QT, QTk, R, Rk = Q2, Q2k, QT2, QT2k, R2, R2k
                yield
            order = range(CPB) if d == 0 else range(CPB - 1, -1, -1)
            for ci in order:
                cc = slice(ci * 64, (ci + 1) * 64)
                gcol = tb * TB + ci * 64
                pw, pwk_ = k.r_ps.next()
                for hh in range(2):
                    hb = hh * 64
                    k.mm(pw[hb:hb + 64, 0:64], A2v[hb:hb + 64, ci, 0:64], tv(ci)[0][hb:hb + 64, :], True, False, [A2k, tv(ci)[1]], [pwk_], signal=False)
                    k.mm(pw[hb:hb + 64, 0:64], ar[hb:hb + 64, ci, 0:64], Sbf[cur][hb:hb + 64, :], False, True, [ark, Sbfkey[cur]], [pwk_], signal=(hh == 1))
                Wsb, Wk = R_['ch'][0]
                k.cp(Wsb[:, 0:64], pw[:, 0:64], [pwk_], [Wk], eng="scalar")
                yield
                pu, puk = k.r_ps.next()
                for hh in range(2):
                    hb = hh * 64
                    k.mm(pu[hb:hb + 64, 0:64], R[hb:hb + 64, cc], Wsb[hb:hb + 64, 0:64], True, True, [Rk, Wk], [puk], signal=(hh == 1))
                Usb, Uk = R_['ch'][1]
                k.cp(Usb[:, 0:64], pu[:, 0:64], [puk], [Uk], eng="vector")
                yield
                py, pyk = k.r_ps.next()
                pS, pSk = k.r_ps.next()
                for hh in range(2):
                    hb = hh * 64
                    k.mm(py[hb:hb + 64, 0:64], Sbf[cur][hb:hb + 64, :], ar[hb:hb + 64, ci, 64:128], True, False, [Sbfkey[cur], ark], [pyk], signal=False)
                    k.mm(py[hb:hb + 64, 0:64], Usb[hb:hb + 64, 0:64], A1v[hb:hb + 64, ci, 64:128], False, False, [Uk, A1k], [pyk], signal=False)
                    k.mm(py[hb:hb + 64, 0:64], tv(ci)[0][hb:hb + 64, :], A2v[hb:hb + 64, ci, 64:128], False, True, [tv(ci)[1], A2k], [pyk], signal=(hh == 1))
                for hh in range(2):
                    hb = hh * 64
                    k.mm(pS[hb:hb + 64, 0:64], tokB[hb:hb + 64, cc], Usb[hb:hb + 64, 0:64], True, False, [tBk, Uk], [pSk], signal=False)
                    k.mm(pS[hb:hb + 64, 0:64], tokK[hb:hb + 64, cc], tv(ci)[0][hb:hb + 64, :], False, True, [tKk, tv(ci)[1]], [pSk], signal=(hh == 1))
                nxt = 1 - cur
                k.stt(S32[nxt], S32[cur], gC[:, ci:ci + 1], pS[:, 0:64], ALU.mult, ALU.add,
                      [S32key[cur], gCk, pSk], [S32key[nxt]])
                k.cp(Sbf[nxt], S32[nxt], [S32key[nxt]], [Sbfkey[nxt]], eng="scalar")
                cur = nxt
                ykey = (yack, gcol // 64)
                if gcol not in written_y:
                    written_y.add(gcol)
                    k.cp(yac[:, gcol:gcol + 64], py[:, 0:64], [pyk], [ykey], eng="vector")
                else:
                    k.tt(yac[:, gcol:gcol + 64], yac[:, gcol:gcol + 64], py[:, 0:64], ALU.add,
                         [ykey, pyk], [ykey])
                yield


def build_l0_rwkv(k, s, w_in, pv, PC, C):
    T, TB, NTB, NT, NCH = k.T, k.TB, k.NTB, k.NT, k.NCH
    CPB = TB // 64

    def pcol(name, i=0):
        return pv[:, PC[name] + i:PC[name] + i + 1]

    wt, wk = k.r_w128.next()
    k.dma(wt[:, :, 0:64], w_in[:, 3072:3136].rearrange("(kc p) m -> p kc m", p=128), [], [wk], eng="gpsimd")
    k.dma(wt[:, :, 64:128], w_in[:, 3168:3232].rearrange("(kc p) m -> p kc m", p=128), [], [wk], eng="gpsimd")
    tw, twk = k.big[6]
    shifted_proj(k, wt, wk, 128, pcol("mu_wl"), tw, twk, k.big[4], k.big[5])
    k.act(tw[:, 0:T], tw[:, 0:T], AF.Tanh, [twk], [twk])
    wt, wk = k.r_w128.next()
    k.dma(wt[:, :, 0:32], w_in[:, 3136:3168].rearrange("(kc p) m -> p kc m", p=128), [], [wk], eng="gpsimd")
    k.dma(wt[:, :, 32:64], w_in[:, 3232:3264].rearrange("(kc p) m -> p kc m", p=128), [], [wk], eng="gpsimd")
    als, alk = k.big[7]
    shifted_proj(k, wt, wk, 64, pv[0:64, PC["mu_al"]:PC["mu_al"] + 1], als, alk, k.big[4], k.big[5])

    for c in range(4):
        rT, rk_ = k.big[0]
        kT, kk_ = k.big[1]
        vT, vk_ = k.big[2]
        kkt, kkk = k.big[3]
        for (dst, dk_, col0, mui) in ((rT, rk_, 1536, c), (kT, kk_, 2048, 4 + c), (vT, vk_, 2560, 8 + c)):
            wt, wk = k.load_w(col0 + c * 128, 128)
            shifted_proj(k, wt, wk, 128, pcol("mu_rkv", mui), dst, dk_, k.big[4], k.big[5])
        k.ts(kkt[:, 0:T], kT[:, 0:T], pcol("k_k", c), None, ALU.mult, None, [kk_, "pv"], [kkk])
        for tb in range(NTB):
            sl = slice(tb * TB, (tb + 1) * TB)
            sq, sqk = k.r_tmp.next()
            k.act(sq[:, 0:TB], kkt[:, sl], AF.Square, [kkk], [sqk])
            ps, pk = k.r_ps.next()
            k.mm(ps[:, 0:TB], C["blk64"][:, :], sq[:, 0:TB], True, True, ["blk64", sqk], [pk])
            sd, sdk = k.r_tmp.next()
            k.act(sd[:, 0:TB], ps[:, 0:TB], AF.Ln, [pk, "eps"], [sdk], bias=k.eps_kk[:, 0:1])
            rn, rnk = k.r_tmp.next()
            k.act(rn[:, 0:TB], sd[:, 0:TB], AF.Exp, [sdk], [rnk], scale=-0.5)
            k.tt(kkt[:, sl], kkt[:, sl], rn[:, 0:TB], ALU.mult, [kkk, rnk], [kkk])
        kms, kmsk = k.big[4]
        yac, yack = k.big[5]
        bfq = k.r_sc.bufs + k.r_sc2.bufs + k.r_q.bufs + k.r_rf.bufs
        trans = MiniRing(bfq[22:25])
        T_ = {"pv": pv, "PC": PC, "rT": (rT, rk_), "kT": (kT, kk_), "vT": (vT, vk_), "kkt": (kkt, kkk),
              "kms": (kms, kmsk), "yac": (yac, yack), "tw": (tw, twk), "als": (als, alk)}
        tokVall = [(k.r_keep.bufs[i][0].bitcast(BF16), k.r_keep.bufs[i][1]) for i in range(2)]
        T_["tokV"] = tokVall
        for tb in range(NTB):
            vb, vbk = trans.next()
            k.cp(vb[:, 0:TB], vT[:, tb * TB:(tb + 1) * TB], [vk_], [vbk], eng="vector")
            pt_, ptk = k.r_ps.next()
            pt = pt_.bitcast(BF16)
            for ci in range(CPB):
                for hh in range(2):
                    hb = hh * 64
                    k.tr(pt[hb:hb + 64, ci * 64:(ci + 1) * 64], vb[hb:hb + 64, ci * 64:(ci + 1) * 64],
                         k.identb[hb:hb + 64, hb:hb + 64], [vbk, "identb"], [ptk],
                         signal=(ci == CPB - 1 and hh == 1))
            g0 = tb * CPB
            tv_, tvk = tokVall[g0 // 16]
            k.cp(tv_[:, (g0 % 16) * 64:(g0 % 16) * 64 + TB], pt[:, 0:TB], [ptk], [tvk], eng="scalar")
        finek = [(kmsk, j) for j in range(NTB)] + [(yack, j) for j in range(NCH)]
        k.fence([kmsk, yack], finek + [kmsk, yack])
        written_y, written_k = set(), set()
        gens = []
        for d in range(2):
            R_ = {"slots": bfq[11 * d:11 * d + 5], "q": bfq[11 * d + 5:11 * d + 11], "trans": trans,
                  "A": k.r_am.bufs[2 * d:2 * d + 2], "ar": k.r_ar.bufs[d], "gC": k.r_gc.bufs[d],
                  "ch": k.r_ch.bufs[2 * d:2 * d + 2],
                  "S32": [k.S32[:, 2 * d, :], k.S32[:, 2 * d + 1, :]],
                  "Sbf": [k.Sbf[:, 2 * d, :], k.Sbf[:, 2 * d + 1, :]],
                  "S32key": [("S32", 2 * d), ("S32", 2 * d + 1)], "Sbfkey": [("Sbf", 2 * d), ("Sbf", 2 * d + 1)]}
            gens.append(rwkv_stream(k, c, d, R_, C, T_, written_y, written_k))
        interleave(gens, stagger=RWKV_STAGGER)
        k.fence(finek + [kmsk, yack], [kmsk, yack])
        if k.dbg == "rwkv":
            k.dump(f"dbg_y{s}_{c}", yac[:, 0:T], [128, T], [yack])
            k.dump(f"dbg_kk{s}_{c}", kkt[:, 0:T], [128, T], [kkk])
            k.dump(f"dbg_r{s}_{c}", rT[:, 0:T], [128, T], [rk_])
        wt, wk = k.load_w(3264 + c * 128, 128)
        for tb in range(NTB):
            sl = slice(tb * TB, (tb + 1) * TB)
            t0, t0k = k.r_tmp.next()
            k.stt(t0[:, 0:TB], kms[:, sl], k.rkh[:, c:c + 1], rT[:, sl], ALU.mult, ALU.mult,
                  [kmsk, "rkh", rk_], [t0k])
            psb, pbk = k.r_ps.next()
            k.mm(psb[:, 0:TB], C["blk64"][:, :], t0[:, 0:TB], True, True, ["blk64", t0k], [pbk])
            bon, bonk = k.r_keep.next()
            k.tt(bon[:, 0:TB], psb[:, 0:TB], vT[:, sl], ALU.mult, [pbk, vk_], [bonk])
            psm, pmk = k.r_ps.next()
            k.mm(psm[:, 0:TB], k.blk64s[:, :], yac[:, sl], True, True, ["blk64s", yack], [pmk])
            sq, sqk = k.r_tmp.next()
            k.act(sq[:, 0:TB], yac[:, sl], AF.Square, [yack], [sqk])
            psq, pqk_ = k.r_ps.next()
            k.mm(psq[:, 0:TB], k.blk64s[:, :], sq[:, 0:TB], True, True, ["blk64s", sqk], [pqk_])
            msq, msqk = k.r_tmp.next()
            k.act(msq[:, 0:TB], psm[:, 0:TB], AF.Square, [pmk], [msqk])
            var, vark = k.r_tmp.next()
            k.tt(var[:, 0:TB], psq[:, 0:TB], msq[:, 0:TB], ALU.subtract, [pqk_, msqk], [vark])
            sd, sdk = k.r_tmp.next()
            k.act(sd[:, 0:TB], var[:, 0:TB], AF.Ln, [vark, "eps"], [sdk], bias=k.eps_gn[:, 0:1])
            rstd, rsk = k.r_tmp.next()
            k.act(rstd[:, 0:TB], sd[:, 0:TB], AF.Exp, [sdk], [rsk], scale=-0.5)
            t1, t1k = k.r_tmp.next()
            k.tt(t1[:, 0:TB], yac[:, sl], psm[:, 0:TB], ALU.subtract, [yack, pmk], [t1k])
            t2, t2k = k.r_tmp.next()
            k.tt(t2[:, 0:TB], t1[:, 0:TB], rstd[:, 0:TB], ALU.mult, [t1k, rsk], [t2k])
            t3, t3k = k.r_tmp.next()
            k.ts(t3[:, 0:TB], t2[:, 0:TB], pcol("gn_g", c), pcol("gn_b", c), ALU.mult, ALU.add, [t2k, "pv"], [t3k])
            t4, t4k = k.r_tmp.next()
            k.tt(t4[:, 0:TB], t3[:, 0:TB], bon[:, 0:TB], ALU.add, [t3k, bonk], [t4k])
            psg, pgk = k.proj(3264 + c * 128, 128, tb, wt, wk)
            sg, sgk = k.r_tmp.next()
            k.act(sg[:, 0:TB], psg[:, 0:TB], AF.Silu, [pgk], [sgk])
            k.emit_y(s, 4 + c, tb, t4, sg, [t4k, sgk])


def emit_y(k, s, c, tb, a, b, keys):
    TB = k.TB
    yb, ybk = k.r_sc2.next()
    k.tt(yb[:, 0:TB], a[:, 0:TB], b[:, 0:TB], ALU.mult, keys, [ybk])
    row0 = (s * 8 + c) * 128
    k.dma(k.yscr[row0:row0 + 128, tb * TB:(tb + 1) * TB], yb[:, 0:TB], [ybk], [("yscr", s, c, tb)])


K.emit_y = emit_y


def out_tile_stream(k, s, tt_, slot, x_src, dst, dkey, srckey, wo, gb, gbk, ncy):
    T = k.T
    xT = k.xT
    tok0 = s * T + tt_ * 128
    tcol = slice(tt_ * 128, (tt_ + 1) * 128)
    xb, xk0 = k.big[4 + slot // 2]
    xin = xb[:, (slot % 2) * 1024:(slot % 2 + 1) * 1024]
    xk = (xk0, slot % 2)
    rbb, rk0 = k.big[6 + slot // 2]
    rb = rbb[:, (slot % 2) * 1024:(slot % 2 + 1) * 1024]
    rbk = (rk0, slot % 2)
    st_, stk = (k.r_st.bufs + k.r_gc.bufs)[slot]
    k.dma(xin, x_src[tok0:tok0 + 128, :], [(srckey, s, tt_)] if srckey else [], [xk])
    for h in range(2):
        ps, pk = k.r_ps.next()
        for c in range(ncy):
            k.mm(ps[:, 0:512], xT[:, c, tcol], wo[h][0][:, c * 512:(c + 1) * 512], c == 0, c == ncy - 1,
                 [("xT", tt_), wo[h][1]], [pk])
        k.stt(rb[:, h * 512:(h + 1) * 512], xin[:, h * 512:(h + 1) * 512], ALPHA, ps[:, 0:512],
              ALU.mult, ALU.add, [xk, pk], [rbk])
    yield
    junk, jk = k.r_tmp.next()
    k.p.op("scalar", lambda e, junk=junk, rb=rb, st_=st_: e.activation(
        out=junk[:, 0:512], in_=rb[:, 0:512], func=AF.Identity, accum_out=st_[:, 0:1]), [rbk], [jk, stk])
    k.p.op("scalar", lambda e, junk=junk, rb=rb, st_=st_: e.activation(
        out=junk[:, 0:512], in_=rb[:, 512:1024], func=AF.Identity, accum_out=st_[:, 1:2]), [rbk], [jk, stk])
    k.p.op("scalar", lambda e, junk=junk, rb=rb, st_=st_: e.activation(
        out=junk[:, 0:512], in_=rb[:, 0:512], func=AF.Square, accum_out=st_[:, 2:3]), [rbk], [jk, stk])
    k.p.op("scalar", lambda e, junk=junk, rb=rb, st_=st_: e.activation(
        out=junk[:, 0:512], in_=rb[:, 512:1024], func=AF.Square, accum_out=st_[:, 3:4]), [rbk], [jk, stk])
    yield
    sk = [stk]
    k.tt(st_[:, 4:5], st_[:, 0:1], st_[:, 1:2], ALU.add, sk, sk)
    k.tt(st_[:, 5:6], st_[:, 2:3], st_[:, 3:4], ALU.add, sk, sk)
    k.ts(st_[:, 4:6], st_[:, 4:6], 1.0 / 1024.0, None, ALU.mult, None, sk, sk)
    k.tt(st_[:, 6:7], st_[:, 4:5], st_[:, 4:5], ALU.mult, sk, sk)
    k.tt(st_[:, 5:6], st_[:, 5:6], st_[:, 6:7], ALU.subtract, sk, sk)
    k.act(st_[:, 6:7], st_[:, 5:6], AF.Sqrt, sk + ["eps"], sk, bias=k.eps_ln[:, 0:1])
    k.recip(st_[:, 6:7], st_[:, 6:7], sk, sk)
    k.stt(st_[:, 7:8], st_[:, 4:5], -1.0, st_[:, 6:7], ALU.mult, ALU.mult, sk, sk)
    yield
    k.p.op("scalar", lambda e, rb=rb, st_=st_: e.activation(
        out=rb, in_=rb, func=AF.Identity, bias=st_[:, 7:8], scale=st_[:, 6:7]), [rbk] + sk, [rbk])
    k.tt(rb, rb, gb[:, 0:1024], ALU.mult, [rbk, gbk], [rbk])
    k.tt(rb, rb, gb[:, 1024:2048], ALU.add, [rbk, gbk], [rbk])
    t = k.dma(dst[tok0:tok0 + 128, :], rb, [rbk], [(dkey, s, tt_)])
    k.last_out.append(t)
    yield


def build_out_ln(k, s, x_src, li, gb_dram, dst, dkey, srckey=None, ncy=8):
    T, TB, NTB, NT = k.T, k.TB, k.NTB, k.NT
    wo = []
    for h in range(2):
        bt_, bk = k.big[1 + h]
        wv = bt_.bitcast(BF16)
        k.dma(wv[:, 0:4096].rearrange("p (c m) -> p c m", c=8),
              k.wobf[li][:, h * 512:(h + 1) * 512].rearrange("(c p) m -> p c m", p=128),
              [("wobf", li, q_) for q_ in range(4)], [bk])
        wo.append((wv, bk))
    gb, gbk = k.big[3]
    k.dma(gb[:, 0:2048], gb_dram.partition_broadcast(128), [], [gbk])
    fine = [(("big", i), j) for i in range(4, 8) for j in range(2)]
    coarse = [("big", i) for i in range(4, 8)]
    k.fence(coarse, fine + coarse)
    xkeys = [("xT", i) for i in range(NT)]
    for tb in range(NTB):
        for c in range(ncy):
            row0 = (s * 8 + c) * 128
            k.dma(k.xT[:, c, tb * TB:(tb + 1) * TB], k.yscr[row0:row0 + 128, tb * TB:(tb + 1) * TB],
                  [("yscr", s, c, tb)], [("xT", tb * (TB // 128) + i) for i in range(TB // 128)])
    gens = [out_tile_stream(k, s, tt_, tt_ % 4, x_src, dst, dkey, srckey, wo, gb, gbk, ncy) for tt_ in range(NT)]
    active = []
    pend = list(gens)
    while pend or active:
        if pend and len(active) < 4:
            active.append(pend.pop(0))
        for g in list(active):
            try:
                next(g)
            except StopIteration:
                active.remove(g)
    k.fence(fine + coarse, coarse)


def build(T, NSEQ, dbg=None, layers=(0, 1)):
    k = K(T, NSEQ, dbg)
    BW = 2080
    x_dram = k.din("x", [NSEQ * T, D])
    w_in0 = k.din("l0_w_in", [D, EVEN_COLS])
    w_out0 = k.din("l0_w_out", [D, D])
    gb0 = k.din("l0_gb", [1, 2048])
    PC = pack_l0_cols()
    NPV = PC["_n"]
    pv0_d = k.din("pv0", [128, NPV])
    wup_d = k.din("l0_wup", [128, 512])
    aup_d = k.din("l0_aup", [64, 512])
    w0x_d = k.din("l0_w0x", [128, 512])
    w_in1 = k.din("l1_w_in", [D, ODD_COLS])
    w_out1 = k.din("l1_w_out", [D, D])
    gb1 = k.din("l1_gb", [1, 2048])
    gup_d = k.din("l1_gup", [64, 512])
    gbx_d = k.din("l1_gbx", [128, 512])
    ng_d = k.din("l1_ng", [1, 1024])
    consts = host_consts()
    cd = {n: k.din("c_" + n, list(a.shape)) for n, a in consts.items()}
    out_d = k.dout("out", [NSEQ * T, D])
    k.yscr = k.dscr("yscr", [NSEQ * 8 * 128, T], BF16)
    x1scr = k.dscr("x1scr", [NSEQ * T, D])
    k.last_out = []
    k.wobf = {}
    for li, wsrc in ((0, w_out0), (1, w_out1)):
        if li in layers:
            scr = k.dscr(f"wobf{li}", [D, D], BF16)
            for q_ in range(4):
                k.dma(scr[q_ * 256:(q_ + 1) * 256, :], wsrc[q_ * 256:(q_ + 1) * 256, :], [], [("wobf", li, q_)],
                      eng="gpsimd")
            k.wobf[li] = scr
    k.xT = k.sb("xT", [128, 8, T], BF16)
    k.r_ps = k.ring("ps", 8, [128, 512], F32, psum=True)
    k.r_w128 = k.ring("w128_", 6, [128, 8, 128], BF16)
    k.r_tmp = k.ring("tmp", 12, [128, 512])
    k.r_keep = k.ring("keep", 4, [128, 512])
    k.big = [(k.sb(f"big{i}", [128, BW]), ("big", i)) for i in range(8)]
    k.r_xin = Ring.__new__(Ring)
    k.r_xin.bufs = [k.big[4], k.big[5]]
    k.r_xin.i = 0
    k.r_ar = k.ring("ar", 2, [128, 8, 128], BF16)
    k.r_sc = k.ring("sc", 10, [128, 512], BF16)
    k.r_sc2 = k.ring("scb", 5, [128, 512], BF16)
    k.r_am = k.ring("am", 4, [128, 1024], BF16)
    k.r_q = k.ring("q", 8, [128, 512], BF16)
    k.r_rf = k.ring("rf", 2, [128, 512], BF16)
    k.r_ch = k.ring("ch", 4, [128, 64], BF16)
    k.r_gc = k.ring("gc", 3, [128, 8])
    k.r_st = k.ring("st", 3, [128, 8])
    k.S32 = k.sb("S32", [128, 4, 64])
    k.Sbf = k.sb("Sbf", [128, 4, 64], BF16)
    k.wga = {}
    pv = k.sb("pv", [128, NPV])
    k.dma(pv[:, :], pv0_d[:, :], [], ["pv"])
    C = {}
    for n, a in consts.items():
        if n in BF_CONSTS:
            C[n] = k.sb("C_" + n, list(a.shape), BF16)
            k.dma(C[n][:, :], cd[n][:, :], [], [n], eng="gpsimd")
        else:
            C[n] = k.sb("C_" + n, list(a.shape))
            k.dma(C[n][:, :], cd[n][:, :], [], [n])
    k.maskA = [C["maskA0"], C["maskA1"]]
    k.maskN = [C["maskN0"], C["maskN1"]]
    k.eye8 = C["eye8"]
    k.identb = k.sb("identb", [128, 128], BF16)
    k.dma(k.identb[:, :], cd["ident"][:, :], [], ["identb"], eng="gpsimd")
    k.wup = k.sb("wup", [128, 512])
    k.dma(k.wup[:, :], wup_d[:, :], [], ["wup"])
    k.aup = k.sb("aup", [64, 512])
    k.dma(k.aup[:, :], aup_d[:, :], [], ["aup"])
    k.w0x = k.sb("w0x", [128, 512])
    k.dma(k.w0x[:, :], w0x_d[:, :], [], ["w0x"])
    k.gup = k.sb("gup", [64, 512])
    k.dma(k.gup[:, :], gup_d[:, :], [], ["gup"])
    k.gbx = k.sb("gbx", [128, 512])
    k.dma(k.gbx[:, :], gbx_d[:, :], [], ["gbx"])
    k.ng_d = ng_d
    k.Sg32 = k.sb("Sg32", [128, 256])
    k.Sgbf = k.sb("Sgbf", [128, 256], BF16)
    k.ssq = k.sb("ssq", [128, 48])
    k.onec = k.sb("onec", [128, 2])
    k.memset(k.onec[:, 0:1], 1.0, ["onec"])
    k.memset(k.onec[:, 1:2], 1e-6, ["onec"])
    eps = k.sb("epsv", [128, 4])
    k.memset(eps[:, 0:1], 1e-5, ["eps"])
    k.memset(eps[:, 1:2], 1e-12, ["eps"])
    k.memset(eps[:, 2:3], 64e-5, ["eps"])
    k.eps_ln, k.eps_kk, k.eps_gn = eps[:, 0:1], eps[:, 1:2], eps[:, 2:3]
    k.blk64s = k.sb("blk64s", [128, 128])
    k.ts(k.blk64s[:, :], C["blk64"][:, :], 1.0 / 64.0, None, ALU.mult, None, ["blk64"], ["blk64s"])
    k.omk = k.sb("omk", [128, 4])
    k.ts(k.omk[:, :], pv[:, PC["k_a"]:PC["k_a"] + 4], -0.5, 1.0, ALU.mult, ALU.add, ["pv"], ["omk"])
    k.hka = k.sb("hka", [128, 4])
    k.ts(k.hka[:, :], pv[:, PC["k_a"]:PC["k_a"] + 4], 0.5, None, ALU.mult, None, ["pv"], ["hka"])
    k.ha0 = k.sb("ha0", [128, 8])
    k.ts(k.ha0[:, :], pv[:, PC["a0_0"]:PC["a0_0"] + 8], 0.5, None, ALU.mult, None, ["pv"], ["ha0"])
    k.rkh = k.sb("rkh", [128, 4])
    k.ts(k.rkh[:, :], pv[:, PC["r_k"]:PC["r_k"] + 4], 0.5, None, ALU.mult, None, ["pv"], ["rkh"])

    for s in range(NSEQ):
        if 0 in layers:
            build_xT(k, s, x_dram, w_in0, C)
            build_l0_front(k, s, x_dram, w_in0, pv, PC, C)
            build_l0_rwkv(k, s, w_in0, pv, PC, C)
            if 1 in layers:
                build_out_ln(k, s, x_dram, 0, gb0, x1scr, "x1")
            else:
                build_out_ln(k, s, x_dram, 0, gb0, out_d, "out")
                k.finals.extend(k.last_out)
            k.last_out = []
        if 1 in layers:
            src, sk = (x1scr, "x1") if 0 in layers else (x_dram, None)
            build_xT(k, s, src, w_in1, C, sk)
            build_l1_gla(k, s, w_in1, C)
            build_out_ln(k, s, src, 1, gb1, out_d, "out", sk)
            k.finals.extend(k.last_out)
            k.last_out = []
    return k.finish()


def host_inputs(inp, x_core):
    pv0, _ = pack_l0(inp)
    m = {"x": np.ascontiguousarray(x_core.reshape(-1, D)), "l0_w_in": np.asarray(inp["l0_w_in"]),
         "l0_w_out": np.asarray(inp["l0_w_out"]), "pv0": pv0,
         "l0_gb": np.concatenate([inp["l0_ln_g"], inp["l0_ln_b"]])[None, :].astype(np.float32),
         "l0_wup": np.ascontiguousarray(np.asarray(inp["l0_w_up"]).reshape(128, 512)),
         "l0_aup": np.ascontiguousarray(np.asarray(inp["l0_a_up"]).reshape(64, 512))}
    w0x = np.zeros((128, 512), np.float32)
    w0x[0] = inp["l0_w0"][0]
    w0x[64] = inp["l0_w0"][1]
    m["l0_w0x"] = w0x
    m["l1_w_in"] = np.asarray(inp["l1_w_in"])
    m["l1_w_out"] = np.asarray(inp["l1_w_out"])
    m["l1_gb"] = np.concatenate([inp["l1_ln_g"], inp["l1_ln_b"]])[None, :].astype(np.float32)
    gup = np.zeros((64, 512), np.float32)
    gup[0:16] = inp["l1_g_up"][0]
    gup[32:48] = inp["l1_g_up"][1]
    m["l1_gup"] = gup
    gbx = np.zeros((128, 512), np.float32)
    gbx[0] = inp["l1_g_bias"][0]
    gbx[32] = inp["l1_g_bias"][1]
    m["l1_gbx"] = gbx
    m["l1_ng"] = np.asarray(inp["l1_norm_g"])[None, :].astype(np.float32)
    for n, a in host_consts().items():
        m["c_" + n] = a
    return m


CG = -1.0 / 16.0
SEQ_STREAMS = False
RWKV_STAGGER = 5
GLA_STAGGER = 3


def interleave(gens, stagger=0):
    gens = list(gens)
    if SEQ_STREAMS:
        for g in gens:
            for _ in g:
                pass
        return
    active = []
    rnd = 0
    pending = list(enumerate(gens))
    while pending or active:
        while pending and pending[0][0] * stagger <= rnd:
            active.append(pending.pop(0)[1])
        for g in list(active):
            try:
                next(g)
            except StopIteration:
                active.remove(g)
        rnd += 1


class MiniRing:
    def __init__(self, bufs):
        self.bufs = list(bufs)
        self.i = 0

    def next(self):
        b = self.bufs[self.i % len(self.bufs)]
        self.i += 1
        return b


def gla_stream(k, s, h, d, R, C, written):
    T, TB, NTB, NT = k.T, k.TB, k.NTB, k.NT
    CPB = TB // 64
    TPB = TB // 128
    xT, xT_keys = k.xT, k.xT_keys
    lr, lrk = R["lr"]
    (wq, wqk), (wkk, wkkk), (wv0, wv0k), (wv1, wv1k) = R["w"]
    (qt, qtk), (kt, ktk), (tokK, tKk) = R["slots"]
    tokV, tVk = R["tokV"]
    S32t, S32k = R["S32"]
    Sbft, Sbfk = R["Sbf"]
    gC, gCk = R["gC"]
    oview = R["oview"]
    trans = R["trans"]
    S32 = [S32t[:, 0:256], S32t[:, 256:512]]
    Sbf = [Sbft[:, 0:256], Sbft[:, 256:512]]
    S32key = [(S32k, 0), (S32k, 1)]
    Sbfkey = [(Sbfk, 0), (Sbfk, 1)]
    cur = 0
    k.memset(S32[0], 0.0, [S32key[0]], eng="vector")
    k.memset(Sbf[0], 0.0, [Sbfkey[0]], eng="vector")
    blocks = range(NTB) if d == 0 else range(NTB - 1, -1, -1)
    for tb in blocks:
        psl, plk = k.r_ps.next()
        for i in range(TPB):
            tcol = slice(tb * TB + i * 128, tb * TB + (i + 1) * 128)
            k.mm(psl[:, i * 128:(i + 1) * 128], lr[d * 32:d * 32 + 16, tcol],
                 k.gup[d * 32:d * 32 + 16, h * 128:(h + 1) * 128], True, False, [lrk, "gup"], [plk])
            k.mm(psl[:, i * 128:(i + 1) * 128], C["ones128"][d * 32:d * 32 + 1, :],
                 k.gbx[d * 32:d * 32 + 1, h * 128:(h + 1) * 128], False, True, ["ones128", "gbx"], [plk])
        e1, e1k = k.r_tmp.next()
        k.act(e1[:, 0:TB], psl[:, 0:TB], AF.Exp, [plk], [e1k], scale=-1.0)
        sp, spk = k.r_tmp.next()
        k.act(sp[:, 0:TB], e1[:, 0:TB], AF.Ln, [e1k, "onec"], [spk], bias=k.onec[:, 0:1])
        pcs = []
        for x in (1, 2):
            pc_, pck = k.r_ps.next()
            for i in range(TPB):
                k.mm(pc_[:, i * 128:(i + 1) * 128], sp[:, i * 128:(i + 1) * 128],
                     C["cmask"][:, (d * 3 + x) * 128:(d * 3 + x + 1) * 128], True, True,
                     [spk, "cmask"], [pck])
            pcs.append((pc_, pck))
        Gi, Gik = k.r_tmp.next()
        k.act(Gi[:, 0:TB], pcs[0][0][:, 0:TB], AF.Exp, [pcs[0][1]], [Gik], scale=CG)
        Gn, Gnk = k.r_tmp.next()
        k.act(Gn[:, 0:TB], pcs[0][0][:, 0:TB], AF.Exp, [pcs[0][1]], [Gnk], scale=-CG)
        Gr, Grk = k.r_tmp.next()
        k.act(Gr[:, 0:TB], pcs[1][0][:, 0:TB], AF.Exp, [pcs[1][1]], [Grk], scale=CG)
        lastcol = 63 if d == 0 else 0
        k.act(gC[:, 0:CPB], pcs[0][0][:, 0:TB].rearrange("p (a b) -> p a b", b=64)[:, :, lastcol],
              AF.Exp, [pcs[0][1]], [gCk], scale=CG)
        psq, pqk = k.proj(h * 128, 128, tb, wq, wqk)
        k.stt(qt[:, 0:TB], psq[:, 0:TB], float(128 ** -0.5), Gi[:, 0:TB], ALU.mult, ALU.mult,
              [pqk, Gik], [qtk])
        psk, pkk = k.proj(512 + h * 128, 128, tb, wkk, wkkk)
        k.tt(kt[:, 0:TB], psk[:, 0:TB], Gn[:, 0:TB], ALU.mult, [pkk, Gnk], [ktk])
        kg, kgk = trans.next()
        k.tt(kg[:, 0:TB], psk[:, 0:TB], Gr[:, 0:TB], ALU.mult, [pkk, Grk], [kgk])
        pt_, ptk = k.r_ps.next()
        pt = pt_.bitcast(BF16)
        for i in range(TPB):
            k.tr(pt[:, i * 128:(i + 1) * 128], kg[:, i * 128:(i + 1) * 128], k.identb[:, :],
                 [kgk, "identb"], [ptk], signal=(i == TPB - 1))
        k.cp(tokK[:, 0:TB], pt[:, 0:TB], [ptk], [tKk], eng="scalar")
        yield
        pss, pssk = k.r_ps.next()
        for i in range(TPB):
            cc = slice(i * 128, (i + 1) * 128)
            k.mm(pss[:, cc], kt[:, cc], qt[:, cc], True, True, [ktk, qtk], [pssk])
        ST, STk = trans.next()
        k.tt(ST[:, 0:TB], pss[:, 0:TB], C[f"gmask{d}"][:, 0:TB], ALU.mult, [pssk, f"gmask{d}"], [STk])
        for i in range(TPB):
            tt_ = tb * TPB + i
            po, pok = k.r_ps.next()
            k.mm(po[:, 0:256], ST[:, i * 128:(i + 1) * 128], tokV[:, tt_ * 256:(tt_ + 1) * 256], True, True,
                 [STk, tVk], [pok])
            ov, ovk = oview(tt_)
            if (h, tt_) not in written:
                written.add((h, tt_))
                k.cp(ov, po[:, 0:256], [pok], [ovk], eng="scalar")
            else:
                k.tt(ov, ov, po[:, 0:256], ALU.add, [ovk, pok], [ovk])
        yield
        order = range(CPB) if d == 0 else range(CPB - 1, -1, -1)
        for ci in order:
            i, hp = ci // 2, (ci % 2) * 64
            tt_ = tb * TPB + i
            cc = slice(ci * 64, (ci + 1) * 64)
            nxt = 1 - cur
            po, pok = k.r_ps.next()
            k.mm(po[hp:hp + 64, 0:256], qt[:, cc], Sbf[cur], True, True, [qtk, Sbfkey[cur]], [pok])
            pS, pSk = k.r_ps.next()
            k.mm(pS[:, 0:256], tokK[hp:hp + 64, i * 128:(i + 1) * 128],
                 tokV[hp:hp + 64, tt_ * 256:(tt_ + 1) * 256], True, True, [tKk, tVk], [pSk])
            k.stt(S32[nxt], S32[cur], gC[:, ci:ci + 1], pS[:, 0:256], ALU.mult, ALU.add,
                  [S32key[cur], gCk, pSk], [S32key[nxt]])
            k.cp(Sbf[nxt], S32[nxt], [S32key[nxt]], [Sbfkey[nxt]], eng="scalar")
            ov, ovk = oview(tt_)
            k.tt(ov[hp:hp + 64, :], ov[hp:hp + 64, :], po[hp:hp + 64, 0:256], ALU.add, [ovk, pok], [ovk])
            cur = nxt
            yield


def build_l1_gla(k, s, w_in, C):
    T, TB, NTB, NT = k.T, k.TB, k.NTB, k.NT
    TPB = TB // 128
    xT, xT_keys = k.xT, k.xT_keys
    wt, wk = k.r_w128.next()
    k.memset(wt[:, :, 0:64], 0.0, [wk])
    k.dma(wt[:, :, 0:16], w_in[:, 3072:3088].rearrange("(kc p) m -> p kc m", p=128), [], [wk], eng="gpsimd")
    k.dma(wt[:, :, 32:48], w_in[:, 3088:3104].rearrange("(kc p) m -> p kc m", p=128), [], [wk], eng="gpsimd")
    lr, lrk = k.big[4]
    for tb in range(NTB):
        ps, pk = k.r_ps.next()
        for kc in range(8):
            k.mm(ps[0:64, 0:TB], wt[:, kc, 0:64], xT[:, kc, tb * TB:(tb + 1) * TB], kc == 0, kc == 7,
                 [wk] + xT_keys, [pk])
        k.cp(lr[0:64, tb * TB:(tb + 1) * TB], ps[0:64, 0:TB], [pk], [lrk], eng="scalar")
    ngb, ngk = k.big[5]
    k.dma(ngb[:, 0:1024], k.ng_d.partition_broadcast(128), [], [ngk])
    fine = []
    for i in range(4):
        fine += [(("big", i), j) for j in range(8)]
        fine += [(("keep", i), j) for j in range(2)] + [(("q", i), j) for j in range(2)]
    coarse = [("big", i) for i in range(4)] + [("keep", i) for i in range(4)] + [("q", i) for i in range(4)]
    k.fence(coarse, fine + coarse)
    NT = k.NT
    bf512 = k.r_sc.bufs + k.r_sc2.bufs
    trans = MiniRing(bf512[12:15])
    small = k.r_gc.bufs + k.r_st.bufs
    wtiles = k.r_w128.bufs + k.r_ar.bufs
    for pair in range(2):
        heads = (2 * pair, 2 * pair + 1)
        written = set()
        gens = []
        wi = 0

        def loadw(tile, c0):
            wt_, wk_ = tile
            k.dma(wt_[:, :, 0:128], w_in[:, c0:c0 + 128].rearrange("(kc p) m -> p kc m", p=128), [], [wk_],
                  eng="gpsimd")
            return tile

        ovs = {}
        for hi, h in enumerate(heads):
            ws = [loadw(wtiles[hi * 4 + 0], h * 128), loadw(wtiles[hi * 4 + 1], 512 + h * 128),
                  loadw(wtiles[hi * 4 + 2], 1024 + h * 256), loadw(wtiles[hi * 4 + 3], 1024 + h * 256 + 128)]
            oacc = [k.big[2 * hi], k.big[2 * hi + 1]]

            def oview(tt_, oacc=oacc):
                b, bk = oacc[tt_ // 8]
                return b[:, (tt_ % 8) * 256:(tt_ % 8 + 1) * 256], (bk, tt_ % 8)

            ovs[h] = oview
            tvb, tvk_ = k.big[6 + hi]
            tokVh = tvb.bitcast(BF16)
            for tt_ in range(NT):
                tcol = slice(tt_ * 128, (tt_ + 1) * 128)
                pv_, pvk = k.r_ps.next()
                for half in range(2):
                    wv, wvk = ws[2 + half]
                    for kc in range(8):
                        k.mm(pv_[:, half * 128:(half + 1) * 128], xT[:, kc, tcol], wv[:, kc, 0:128],
                             kc == 0, kc == 7, [wvk] + xT_keys, [pvk])
                k.cp(tokVh[:, tt_ * 256:(tt_ + 1) * 256], pv_[:, 0:256], [pvk], [tvk_],
                     eng=("scalar" if tt_ % 2 else "vector"))
            for d in range(2):
                si = hi * 2 + d
                R = {"lr": (lr, lrk), "w": ws, "slots": bf512[3 * si:3 * si + 3], "tokV": (tokVh, tvk_),
                     "S32": k.r_keep.bufs[si], "Sbf": k.r_q.bufs[si], "gC": small[si], "oview": oview,
                     "trans": trans}
                gens.append(gla_stream(k, s, h, d, R, C, written))
        interleave(gens, stagger=GLA_STAGGER)
        for hi, h in enumerate(heads):
            oview = ovs[h]
            wg0, wg0k = loadw(wtiles[hi * 4 + 0], 2048 + h * 256)
            wg1, wg1k = loadw(wtiles[hi * 4 + 1], 2048 + h * 256 + 128)
            ssq = k.ssq
            for tt_ in range(NT):
                ov, ovk = oview(tt_)
                junk, jk = k.r_tmp.next()
                k.p.op("scalar", lambda e, junk=junk, ov=ov, ssq=ssq, tt_=tt_: e.activation(
                    out=junk[:, 0:256], in_=ov, func=AF.Square, accum_out=ssq[:, tt_:tt_ + 1]), [ovk], [jk, "ssq"])
            k.act(ssq[:, 16:16 + NT], ssq[:, 0:NT], AF.Sqrt, ["ssq", "onec"], ["ssq"], bias=k.onec[:, 1:2],
                  scale=1.0 / 256.0)
            k.recip(ssq[:, 32:32 + NT], ssq[:, 16:16 + NT], ["ssq"], ["ssq"])
            for tt_ in range(NT):
                ov, ovk = oview(tt_)
                tcol = slice(tt_ * 128, (tt_ + 1) * 128)
                on, onk = k.r_tmp.next()
                k.stt(on[:, 0:256], ov, ssq[:, 32 + tt_:33 + tt_], ngb[:, h * 256:(h + 1) * 256], ALU.mult, ALU.mult,
                      [ovk, "ssq", ngk], [onk])
                pg, pgk = k.r_ps.next()
                for half, (wg, wgk) in enumerate(((wg0, wg0k), (wg1, wg1k))):
                    for kc in range(8):
                        k.mm(pg[:, half * 128:(half + 1) * 128], xT[:, kc, tcol], wg[:, kc, 0:128],
                             kc == 0, kc == 7, [wgk] + xT_keys, [pgk])
                sg, sgk = k.r_tmp.next()
                k.act(sg[:, 0:256], pg[:, 0:256], AF.Silu, [pgk], [sgk])
                yb, ybk = trans.next()
                k.tt(yb[:, 0:256], on[:, 0:256], sg[:, 0:256], ALU.mult, [onk, sgk], [ybk])
                pt_, ptk = k.r_ps.next()
                pt = pt_.bitcast(BF16)
                for half in range(2):
                    k.tr(pt[:, half * 128:(half + 1) * 128], yb[:, half * 128:(half + 1) * 128], k.identb[:, :],
                         [ybk, "identb"], [ptk], signal=(half == 1))
                yf, yfk = trans.next()
                k.cp(yf[:, 0:256], pt[:, 0:256], [ptk], [yfk], eng="vector")
                tb = tt_ // TPB
                for half in range(2):
                    row0 = (s * 8 + 2 * h + half) * 128
                    k.dma(k.yscr[row0:row0 + 128, tt_ * 128:(tt_ + 1) * 128], yf[:, half * 128:(half + 1) * 128],
                          [yfk], [("yscr", s, 2 * h + half, tb), ("yscrw", s, 2 * h + half, tt_)])
    k.fence(fine + coarse, coarse)


N_CORES = 8
SEQ_LEN = 2048
BATCH = 16


def kernel(**inputs):
    inp = {n: np.asarray(v) for n, v in inputs.items()}
    x = inp["x"]
    nseq = BATCH // N_CORES
    nc = build(SEQ_LEN, nseq, layers=(0, 1))
    in_maps = [host_inputs(inp, x[c * nseq:(c + 1) * nseq]) for c in range(N_CORES)]
    res = run_bass_kernel_spmd(nc, in_maps, core_ids=list(range(N_CORES)))
    outs = [np.asarray(r["out"]).reshape(nseq, SEQ_LEN, D) for r in res.results]
    return np.concatenate(outs, 0).astype(np.float32)
```

```python
import contextlib
import numpy as np
import concourse.bass as bass
import concourse.mybir as mybir
from concourse.bass_utils import run_bass_kernel_spmd

F32 = mybir.dt.float32
BF16 = mybir.dt.bfloat16
AF = mybir.ActivationFunctionType
ALU = mybir.AluOpType
AX = mybir.AxisListType

ENGS = ("tensor", "vector", "scalar", "gpsimd", "sync")

D = 1024
EVEN_COLS = 3776
ODD_COLS = 3104
ALPHA = float((2 * 2) ** 0.25)
CW = -float(np.exp(-0.5))


class Prog:
    NDMA_SEMS = 24

    def __init__(self, nc):
        self.nc = nc
        self.ops = {e: [] for e in ENGS}
        self.cnt = {e: 0 for e in ENGS}
        self.pending = {e: False for e in ENGS}
        self.waited = {e: {} for e in ENGS}
        self.last_w = {}
        self.readers = {}
        self.dma_i = 0
        self.dma_j = 0
        self.dma_cnt = [0] * self.NDMA_SEMS
        self.dma_last = [None] * self.NDMA_SEMS

    def _deps(self, eng, reads, writes):
        need = {}

        def add(t):
            if t is None:
                return
            k, v, te = t
            if te == eng and eng == "tensor":
                return
            if need.get(k, 0) < v:
                need[k] = v

        for r in reads:
            add(self.last_w.get(r))
        for w in writes:
            add(self.last_w.get(w))
            for t in self.readers.get(w, ()):
                if t[2] == eng and t[0][0] == "e":
                    continue
                add(t)
        out = []
        for k, v in need.items():
            if self.waited[eng].get(k, 0) >= v:
                continue
            self.waited[eng][k] = v
            out.append((k, v))
        return out

    def _commit(self, ticket, reads, writes):
        for r in reads:
            self.readers.setdefault(r, []).append(ticket)
        for w in writes:
            self.last_w[w] = ticket
            self.readers[w] = []

    def op(self, eng, fn, reads=(), writes=(), signal=True):
        waits = self._deps(eng, reads, writes)
        if signal:
            self.cnt[eng] += 1
            ticket = (("e", eng), self.cnt[eng], eng)
            self.pending[eng] = False
        else:
            ticket = (("e", eng), self.cnt[eng] + 1, eng)
            self.pending[eng] = True
        self.ops[eng].append((fn, waits, (("e", eng), 1) if signal else None))
        self._commit(ticket, reads, writes)
        return ticket

    def dma(self, fn, reads=(), writes=(), eng="sync"):
        half = self.NDMA_SEMS // 2
        if eng == "sync":
            i = self.dma_i % half
            self.dma_i += 1
        else:
            i = half + self.dma_j % half
            self.dma_j += 1
        waits = self._deps(eng, reads, writes)
        prev = self.dma_last[i]
        if prev is not None and self.waited[eng].get(prev[0], 0) < prev[1]:
            self.waited[eng][prev[0]] = prev[1]
            waits.append((prev[0], prev[1]))
        self.dma_cnt[i] += 16
        ticket = (("d", i), self.dma_cnt[i], eng)
        self.dma_last[i] = ticket
        self.ops[eng].append((fn, waits, (("d", i), 16)))
        self._commit(ticket, reads, writes)
        return ticket

    def emit(self, st, final_tickets):
        nc = self.nc
        for e in ENGS:
            assert not self.pending[e], f"unsignalled trailing op on {e}"
        sems = {}
        for e in ENGS:
            sems[("e", e)] = st.enter_context(nc.semaphore(f"s_{e}"))
        for i in range(self.NDMA_SEMS):
            sems[("d", i)] = st.enter_context(nc.semaphore(f"s_d{i}"))
        block = st.enter_context(nc.Block())

        def run(engname):
            def body(eng):
                for fn, waits, inc in self.ops[engname]:
                    for k, v in waits:
                        eng.wait_ge(sems[k], v)
                    ins = fn(eng)
                    if inc is not None:
                        ins.then_inc(sems[inc[0]], inc[1])
                if engname == "sync":
                    for k, v, _ in final_tickets:
                        eng.wait_ge(sems[k], v)
            return body

        block.tensor(run("tensor"))
        block.vector(run("vector"))
        block.scalar(run("scalar"))
        block.gpsimd(run("gpsimd"))
        block.sync(run("sync"))


class Ring:
    def __init__(self, st, nc, name, n, shape, dtype, psum=False):
        self.bufs = []
        for i in range(n):
            if psum:
                h = st.enter_context(nc.psum_tensor(f"{name}{i}", shape, dtype))
            else:
                h = st.enter_context(nc.sbuf_tensor(f"{name}{i}", shape, dtype))
            self.bufs.append((h, (name, i)))
        self.i = 0

    def next(self):
        b = self.bufs[self.i % len(self.bufs)]
        self.i += 1
        return b


def _chunks(v, n):
    return np.ascontiguousarray(v.reshape(n, 128).T)


def pack_l0(inp):
    cols = {}
    parts = []
    pos = 0

    def add(name, arr):
        nonlocal pos
        arr = np.asarray(arr, np.float32)
        if arr.shape[0] < 128:
            pad = np.zeros((128 - arr.shape[0], arr.shape[1]), np.float32)
            arr = np.concatenate([arr, pad], 0)
        cols[name] = pos
        pos += arr.shape[1]
        parts.append(arr)

    cw = np.asarray(inp["l0_conv_w"])
    for c in range(4):
        add(f"convw{c}", cw[:, c * 128:(c + 1) * 128].T)
    add("convb", _chunks(inp["l0_conv_b"], 4))
    add("clng", _chunks(inp["l0_conv_ln_g"], 4))
    add("clnb", _chunks(inp["l0_conv_ln_b"], 4))
    mu = np.asarray(inp["l0_mu_shift"])
    add("mu_rkv", _chunks(mu[:1536], 12))
    lo = mu[1536:]
    add("mu_wl", np.concatenate([lo[0:64], lo[96:160]])[:, None])
    add("mu_al", np.concatenate([lo[64:96], lo[160:192]])[:, None])
    add("a0_0", _chunks(inp["l0_a0"][0], 4))
    add("a0_1", _chunks(inp["l0_a0"][1], 4))
    add("k_k", _chunks(inp["l0_k_k"], 4))
    add("k_a", _chunks(inp["l0_k_a"], 4))
    add("r_k", _chunks(np.asarray(inp["l0_r_k"]).reshape(-1), 4))
    add("gn_g", _chunks(inp["l0_gn_g"], 4))
    add("gn_b", _chunks(inp["l0_gn_b"], 4))
    return np.concatenate(parts, 1), cols


class K:
    def __init__(self, T, NSEQ, dbg=None):
        self.T, self.NSEQ = T, NSEQ
        self.TB = min(512, T)
        self.NTB = T // self.TB
        self.NT = T // 128
        self.NCH = T // 64
        self.dbg = dbg
        self.nc = bass.Bass("TRN2", target_bir_lowering=False)
        self.p = Prog(self.nc)
        self.st = contextlib.ExitStack()
        self.finals = []

    def sb(self, name, shape, dt=F32):
        return self.st.enter_context(self.nc.sbuf_tensor(name, list(shape), dt))

    def ring(self, name, n, shape, dt=F32, psum=False):
        return Ring(self.st, self.nc, name, n, list(shape), dt, psum)

    def din(self, name, shape, dt=F32):
        return self.nc.dram_tensor(name, list(shape), dt, kind="ExternalInput").ap()

    def dout(self, name, shape, dt=F32):
        return self.nc.dram_tensor(name, list(shape), dt, kind="ExternalOutput").ap()

    def dscr(self, name, shape, dt=F32):
        return self.nc.dram_tensor(name, list(shape), dt, kind="Internal").ap()

    def mm(self, out, lhsT, rhs, start, stop, reads, writes, signal=None):
        self.p.op("tensor", lambda e: e.matmul(out, lhsT=lhsT, rhs=rhs, start=start, stop=stop),
                  reads, writes, signal=(stop if signal is None else signal))

    def tr(self, out, in_, ident, reads, writes, signal=True):
        self.p.op("tensor", lambda e: e.transpose(out=out, in_=in_, identity=ident),
                  reads, writes, signal=signal)

    def act(self, out, in_, func, reads, writes, bias=0.0, scale=1.0):
        self.p.op("scalar", lambda e: e.activation(out=out, in_=in_, func=func, bias=bias, scale=scale),
                  reads, writes)

    def tt(self, out, in0, in1, op, reads, writes, eng="vector"):
        self.p.op(eng, lambda e: e.tensor_tensor(out=out, in0=in0, in1=in1, op=op), reads, writes)

    def ts(self, out, in0, s1, s2, op0, op1, reads, writes, eng="vector"):
        if op1 is None:
            self.p.op(eng, lambda e: e.tensor_scalar(out=out, in0=in0, scalar1=s1, scalar2=None, op0=op0),
                      reads, writes)
        else:
            self.p.op(eng, lambda e: e.tensor_scalar(out=out, in0=in0, scalar1=s1, scalar2=s2, op0=op0, op1=op1),
                      reads, writes)

    def stt(self, out, in0, scalar, in1, op0, op1, reads, writes):
        self.p.op("vector", lambda e: e.scalar_tensor_tensor(out=out, in0=in0, scalar=scalar, in1=in1,
                                                            op0=op0, op1=op1), reads, writes)

    def cp(self, out, in_, reads, writes, eng="vector"):
        if eng == "scalar":
            self.p.op("scalar", lambda e: e.copy(out=out, in_=in_), reads, writes)
        else:
            self.p.op(eng, lambda e: e.tensor_copy(out=out, in_=in_), reads, writes)

    def memset(self, ap, val, writes, eng="gpsimd"):
        self.p.op(eng, lambda e: e.memset(ap, val), (), writes)

    def fence(self, reads, writes):
        if not hasattr(self, "_fz"):
            self._fz = self.sb("fence_scratch", [128, 1])
        self.p.op("vector", lambda e: e.memset(self._fz[:, :], 0.0), reads, list(writes) + ["_fence_scratch"])

    def recip(self, out, in_, reads, writes):
        self.p.op("vector", lambda e: e.reciprocal(out=out, in_=in_), reads, writes)

    def dma(self, out, in_, reads, writes, eng="sync"):
        return self.p.dma(lambda e: e.dma_start(out=out, in_=in_), reads, writes, eng=eng)

    def dump(self, name, ap, shape, reads, dt=F32):
        o = self.dout(name, shape, dt)
        t = self.dma(o, ap, reads, [("dbg", name)])
        self.finals.append(t)

    def finish(self):
        self.p.emit(self.st, self.finals)
        self.st.close()
        return self.nc


def host_consts():
    c = {}
    c["ident"] = np.eye(128, dtype=np.float32)
    c["ones_ln"] = np.full((128, 128), 1.0 / 512.0, np.float32)
    bo = np.zeros((128, 128), np.float32)
    bo[:64, :64] = 1.0
    bo[64:, 64:] = 1.0
    c["blk64"] = bo
    tp = np.arange(128)[:, None]
    t = np.arange(128)[None, :]
    same = (tp // 64) == (t // 64)
    m = {}
    m["f_incl"] = same & (tp <= t)
    m["f_excl"] = same & (tp < t)
    m["f_rest"] = same & (tp > t)
    m["b_incl"] = same & (tp >= t)
    m["b_excl"] = same & (tp > t)
    m["b_rest"] = same & (tp < t)
    c["cmask"] = np.concatenate([m[k].astype(np.float32) for k in
                                 ("f_excl", "f_incl", "f_rest", "b_excl", "b_incl", "b_rest")], 1)
    j = np.arange(64)[:, None]
    tt = np.arange(64)[None, :]
    im = np.concatenate([(j < tt), (j <= tt), (j > tt), (j >= tt)], 1).astype(np.float32)
    im2 = np.concatenate([im, im], 0)
    c["maskA0"] = np.tile(im2[:, 0:128], (1, 8))
    c["maskA1"] = np.tile(im2[:, 128:256], (1, 8))
    c["maskN0"] = np.tile(im2[:, 128:192], (1, 8))
    c["maskN1"] = np.tile(im2[:, 0:64], (1, 8))
    e = np.concatenate([np.eye(64, dtype=np.float32)] * 2, 0)
    c["eye8"] = np.tile(e, (1, 8))
    c["ones128"] = np.ones((128, 128), np.float32)
    c["gmask0"] = np.tile((same & (tp <= t)).astype(np.float32), (1, 4))
    c["gmask1"] = np.tile((same & (tp >= t)).astype(np.float32), (1, 4))
    return c


BF_CONSTS = ("maskA0", "maskA1", "maskN0", "maskN1", "eye8", "gmask0", "gmask1")


def build_xT(k, s, x_dram, w_in, C, srckey=None):
    T, TB, NTB, NT = k.T, k.TB, k.NTB, k.NT
    xT = k.xT
    ident = C["ident"]

    for tt_ in range(NT):
        xin, xk = k.r_xin.next()
        rd = [(srckey, s, tt_)] if srckey else []
        k.dma(xin[:, 0:1024], x_dram[s * T + tt_ * 128: s * T + (tt_ + 1) * 128, :], rd, [xk])
        for g in range(2):
            ps, pk = k.r_ps.next()
            for j in range(4):
                fc = g * 4 + j
                k.tr(ps[:, j * 128:(j + 1) * 128], xin[:, fc * 128:(fc + 1) * 128], ident[:, :],
                     [xk, "ident"], [pk], signal=(j == 3))
            dst = xT[:, g * 4:(g + 1) * 4, tt_ * 128:(tt_ + 1) * 128]
            src = ps[:, :].rearrange("p (a b) -> p a b", a=4)
            k.cp(dst, src, [pk], [("xT", tt_)], eng=("scalar" if g == 0 else "vector"))
    xT_keys = [("xT", i) for i in range(NT)]
    k.xT_keys = xT_keys

    def proj(c0, m, tb, wt, wk):
        ps, pk = k.r_ps.next()
        for kc in range(8):
            k.mm(ps[0:m, 0:TB], wt[:, kc, 0:m], xT[:, kc, tb * TB:(tb + 1) * TB],
                 kc == 0, kc == 7, [wk] + xT_keys, [pk])
        return ps, pk

    k.proj = proj

    def load_w(c0, m):
        wt, wk = (k.r_w128 if m <= 128 else k.r_w512).next()
        k.dma(wt[:, :, 0:m], w_in[:, c0:c0 + m].rearrange("(kc p) m -> p kc m", p=128), [], [wk], eng="gpsimd")
        return wt, wk

    k.load_w = load_w


def build_l0_front(k, s, x_dram, w_in, pv, PC, C):
    T, TB, NTB, NT = k.T, k.TB, k.NTB, k.NT
    proj, load_w = k.proj, k.load_w
    cvs = []
    for c in range(4):
        wv, wvk = load_w(c * 128, 128)
        wg, wgk = load_w(512 + c * 128, 128)
        ub, uk = k.big[4]
        u = ub.bitcast(BF16)
        k.memset(u[:, 0:15], 0.0, [uk], eng="vector")
        k.memset(u[:, 15 + T:T + 32], 0.0, [uk], eng="vector")
        dgb, dgk = k.big[5]
        dg = dgb.bitcast(BF16)
        wc = PC[f"convw{c}"]
        for j in range(31):
            k.ts(dg[:, j * 128:(j + 1) * 128], k.identb[:, :], pv[:, wc + j:wc + j + 1], None, ALU.mult, None,
                 ["identb", "pv"], [dgk], eng="vector")
        for tb in range(NTB):
            psv, pvk = proj(c * 128, 128, tb, wv, wvk)
            psg, pgk = proj(512 + c * 128, 128, tb, wg, wgk)
            sg, sgk = k.r_tmp.next()
            k.act(sg[:, 0:TB], psg[:, 0:TB], AF.Sigmoid, [pgk], [sgk])
            k.tt(u[:, 15 + tb * TB:15 + (tb + 1) * TB], psv[:, 0:TB], sg[:, 0:TB], ALU.mult, [pvk, sgk], [uk])
        a0, a0k = k.big[c]
        for tb in range(NTB):
            ps, pk = k.r_ps.next()
            for j in range(31):
                k.mm(ps[:, 0:TB], dg[:, j * 128:(j + 1) * 128], u[:, tb * TB + j:tb * TB + j + TB],
                     j == 0, j == 30, [dgk, uk], [pk])
            k.act(a0[:, tb * TB:(tb + 1) * TB], ps[:, 0:TB], AF.Identity, [pk, "pv"], [a0k],
                  bias=pv[:, PC["convb"] + c:PC["convb"] + c + 1])
        cvs.append((a0, a0k))
    for tb in range(NTB):
        sl = slice(tb * TB, (tb + 1) * TB)
        psm, pmk = k.r_ps.next()
        for c in range(4):
            k.mm(psm[:, 0:TB], C["ones_ln"][:, :], cvs[c][0][:, sl], c == 0, c == 3, ["ones_ln", cvs[c][1]], [pmk])
        psq, pqk = k.r_ps.next()
        for c in range(4):
            sq, sqk = k.r_tmp.next()
            k.act(sq[:, 0:TB], cvs[c][0][:, sl], AF.Square, [cvs[c][1]], [sqk])
            k.mm(psq[:, 0:TB], C["ones_ln"][:, :], sq[:, 0:TB], c == 0, c == 3, ["ones_ln", sqk], [pqk])
        msq, msqk = k.r_tmp.next()
        k.act(msq[:, 0:TB], psm[:, 0:TB], AF.Square, [pmk], [msqk])
        var, vark = k.r_tmp.next()
        k.tt(var[:, 0:TB], psq[:, 0:TB], msq[:, 0:TB], ALU.subtract, [pqk, msqk], [vark])
        sd, sdk = k.r_tmp.next()
        k.act(sd[:, 0:TB], var[:, 0:TB], AF.Ln, [vark, "eps"], [sdk], bias=k.eps_ln[:, 0:1])
        rstd, rsk = k.r_keep.next()
        k.act(rstd[:, 0:TB], sd[:, 0:TB], AF.Exp, [sdk], [rsk], scale=-0.5)
        if k.dbg == "conv" and tb == 0:
            mm_, mmk = k.r_tmp.next()
            k.cp(mm_[:, 0:TB], psm[:, 0:TB], [pmk], [mmk])
            k.dump(f"dbg_mean{s}", mm_[:, 0:TB], [128, TB], [mmk])
            k.dump(f"dbg_var{s}", var[:, 0:TB], [128, TB], [vark])
            k.dump(f"dbg_rstd{s}", rstd[:, 0:TB], [128, TB], [rsk])
        for c in range(4):
            t1, t1k = k.r_tmp.next()
            k.tt(t1[:, 0:TB], cvs[c][0][:, sl], psm[:, 0:TB], ALU.subtract, [cvs[c][1], pmk], [t1k])
            t2, t2k = k.r_tmp.next()
            k.tt(t2[:, 0:TB], t1[:, 0:TB], rstd[:, 0:TB], ALU.mult, [t1k, rsk], [t2k])
            t3, t3k = k.r_tmp.next()
            k.ts(t3[:, 0:TB], t2[:, 0:TB], pv[:, PC["clng"] + c:PC["clng"] + c + 1],
                 pv[:, PC["clnb"] + c:PC["clnb"] + c + 1], ALU.mult, ALU.add, [t2k, "pv"], [t3k])
            s1, s1k = k.r_tmp.next()
            k.act(s1[:, 0:TB], t3[:, 0:TB], AF.Silu, [t3k], [s1k])
            if k.dbg == "conv" and tb == 0 and c == 0:
                k.dump(f"dbg_t3{s}", t3[:, 0:TB], [128, TB], [t3k])
                k.dump(f"dbg_s1{s}", s1[:, 0:TB], [128, TB], [s1k])
            if tb == 0:
                k.wga[c] = load_w(1024 + c * 128, 128)
            wt, wk = k.wga[c]
            psa, pak = proj(1024 + c * 128, 128, tb, wt, wk)
            sga, sgak = k.r_tmp.next()
            k.act(sga[:, 0:TB], psa[:, 0:TB], AF.Silu, [pak], [sgak])
            k.emit_y(s, c, tb, s1, sga, [s1k, sgak])


def pack_l0_cols():
    dummy = {
        "l0_conv_w": np.zeros((31, 512), np.float32), "l0_conv_b": np.zeros(512, np.float32),
        "l0_conv_ln_g": np.zeros(512, np.float32), "l0_conv_ln_b": np.zeros(512, np.float32),
        "l0_mu_shift": np.zeros(1728, np.float32), "l0_a0": np.zeros((2, 512), np.float32),
        "l0_k_k": np.zeros(512, np.float32), "l0_k_a": np.zeros(512, np.float32),
        "l0_r_k": np.zeros((8, 64), np.float32), "l0_gn_g": np.zeros(512, np.float32),
        "l0_gn_b": np.zeros(512, np.float32),
    }
    arr, cols = pack_l0(dummy)
    cols = dict(cols)
    cols["_n"] = arr.shape[1]
    return cols


def shifted_proj(k, wt, wk, m, mu_ap, out, outk, zb, sb_):
    T, TB, NTB = k.T, k.TB, k.NTB
    z, zk = zb
    s_, sk = sb_
    k.memset(z[0:m, 0:1], 0.0, [zk])
    k.memset(z[0:m, T + 1:T + 2], 0.0, [zk])
    for tb in range(NTB):
        ps, pk = k.r_ps.next()
        for kc in range(8):
            k.mm(ps[0:m, 0:TB], wt[:, kc, 0:m], k.xT[:, kc, tb * TB:(tb + 1) * TB],
                 kc == 0, kc == 7, [wk] + k.xT_keys, [pk])
        k.cp(z[0:m, 1 + tb * TB:1 + (tb + 1) * TB], ps[0:m, 0:TB], [pk], [zk], eng="scalar")
    k.tt(s_[0:m, 0:T], z[0:m, 0:T], z[0:m, 2:T + 2], ALU.add, [zk], [sk])
    k.stt(s_[0:m, 0:T], s_[0:m, 0:T], 0.5, z[0:m, 1:T + 1], ALU.mult, ALU.subtract, [sk, zk], [sk])
    k.stt(out[0:m, 0:T], s_[0:m, 0:T], mu_ap, z[0:m, 1:T + 1], ALU.mult, ALU.add, [sk, zk, "pv"], [outk])


def rwkv_stream(k, c, d, R_, C, T_, written_y, written_k):
    T, TB, NTB, NT, NCH = k.T, k.TB, k.NTB, k.NT, k.NCH
    CPB = TB // 64
    pv, PC = T_["pv"], T_["PC"]

    def pcol(name, i=0):
        return pv[:, PC[name] + i:PC[name] + i + 1]

    (rT, rk_), (kT, kk_), (vT, vk_), (kkt, kkk) = T_["rT"], T_["kT"], T_["vT"], T_["kkt"]
    (kms, kmsk), (yac, yack) = T_["kms"], T_["yac"]
    (tw, twk), (als, alk) = T_["tw"], T_["als"]
    S32, Sbf, S32key, Sbfkey = R_["S32"], R_["Sbf"], R_["S32key"], R_["Sbfkey"]
    if True:
        cur = 0
        k.memset(S32[cur], 0.0, [S32key[cur]], eng="vector")
        k.memset(Sbf[cur], 0.0, [Sbfkey[cur]], eng="vector")
        blocks = range(NTB) if d == 0 else range(NTB - 1, -1, -1)
        for tb in blocks:
            sl = slice(tb * TB, (tb + 1) * TB)
            psa, pak = k.r_ps.next()
            k.mm(psa[:, 0:TB], k.aup[d * 32:(d + 1) * 32, c * 128:(c + 1) * 128], als[d * 32:(d + 1) * 32, sl],
                 True, True, ["aup", alk], [pak])
            a_, ak_ = k.r_tmp.next()
            k.act(a_[:, 0:TB], psa[:, 0:TB], AF.Tanh, [pak, "ha0"], [ak_], bias=k.ha0[:, d * 4 + c:d * 4 + c + 1],
                  scale=0.5)
            ka, kak = k.r_tmp.next()
            k.ts(ka[:, 0:TB], a_[:, 0:TB], k.hka[:, c:c + 1], k.omk[:, c:c + 1], ALU.mult, ALU.add,
                 [ak_, "hka", "omk"], [kak])
            km, kmk = k.r_tmp.next()
            k.tt(km[:, 0:TB], kT[:, sl], ka[:, 0:TB], ALU.mult, [kk_, kak], [kmk])
            kmkey = (kmsk, tb)
            if tb not in written_k:
                written_k.add(tb)
                k.cp(kms[:, sl], km[:, 0:TB], [kmk], [kmkey], eng="gpsimd")
            else:
                k.tt(kms[:, sl], kms[:, sl], km[:, 0:TB], ALU.add, [kmkey, kmk], [kmkey], eng="gpsimd")
            be, bek = k.r_tmp.next()
            k.stt(be[:, 0:TB], a_[:, 0:TB], 1.0, kkt[:, sl], ALU.add, ALU.mult, [ak_, kkk], [bek])
            psw, pwk = k.r_ps.next()
            for i in range(TB // 128):
                tcol = slice(tb * TB + i * 128, tb * TB + (i + 1) * 128)
                k.mm(psw[:, i * 128:(i + 1) * 128], tw[d * 64:(d + 1) * 64, tcol],
                     k.wup[d * 64:(d + 1) * 64, c * 128:(c + 1) * 128], True, False, [twk, "wup"], [pwk])
                k.mm(psw[:, i * 128:(i + 1) * 128], C["ones128"][d * 64:d * 64 + 1, :],
                     k.w0x[d * 64:d * 64 + 1, c * 128:(c + 1) * 128], False, True,
                     ["ones128", "w0x"], [pwk])
            sig, sigk = k.r_tmp.next()
            k.act(sig[:, 0:TB], psw[:, 0:TB], AF.Tanh, [pwk], [sigk], scale=0.5)
            k.ts(sig[:, 0:TB], sig[:, 0:TB], 1.0, None, ALU.add, None, [sigk], [sigk])
            pcs = []
            for x in range(3):
                pc_, pck = k.r_ps.next()
                for i in range(TB // 128):
                    k.mm(pc_[:, i * 128:(i + 1) * 128], sig[:, i * 128:(i + 1) * 128],
                         C["cmask"][:, (d * 3 + x) * 128:(d * 3 + x + 1) * 128], True, True,
                         [sigk, "cmask"], [pck])
                pcs.append((pc_, pck))
            Ge, Gek = k.r_tmp.next()
            k.act(Ge[:, 0:TB], pcs[0][0][:, 0:TB], AF.Exp, [pcs[0][1]], [Gek], scale=0.5 * CW)
            Gi, Gik = k.r_tmp.next()
            k.act(Gi[:, 0:TB], pcs[1][0][:, 0:TB], AF.Exp, [pcs[1][1]], [Gik], scale=0.5 * CW)
            Gn, Gnk = k.r_tmp.next()
            k.act(Gn[:, 0:TB], pcs[1][0][:, 0:TB], AF.Exp, [pcs[1][1]], [Gnk], scale=-0.5 * CW)
            Gr, Grk = k.r_tmp.next()
            k.act(Gr[:, 0:TB], pcs[2][0][:, 0:TB], AF.Exp, [pcs[2][1]], [Grk], scale=0.5 * CW)
            gC, gCk = R_['gC']
            lastcol = 63 if d == 0 else 0
            k.act(gC[:, 0:CPB], pcs[1][0][:, 0:TB].rearrange("p (a b) -> p a b", b=64)[:, :, lastcol],
                  AF.Exp, [pcs[1][1]], [gCk], scale=0.5 * CW)
            ar, ark = R_['ar']
            arv = ar[:, 0:CPB, :]
            k.stt(arv[:, :, 0:64], kkt[:, sl].rearrange("p (a b) -> p a b", b=64), -1.0,
                  Ge[:, 0:TB].rearrange("p (a b) -> p a b", b=64), ALU.mult, ALU.mult, [kkk, Gek], [ark])
            k.tt(arv[:, :, 64:128], rT[:, sl].rearrange("p (a b) -> p a b", b=64),
                 Gi[:, 0:TB].rearrange("p (a b) -> p a b", b=64), ALU.mult, [rk_, Gik], [ark])
            bt, btk = R_['slots'][0]
            k.stt(bt[:, 0:TB], be[:, 0:TB], 0.5, Gn[:, 0:TB], ALU.mult, ALU.mult, [bek, Gnk], [btk])
            kt, ktk = R_['slots'][1]
            k.tt(kt[:, 0:TB], km[:, 0:TB], Gn[:, 0:TB], ALU.mult, [kmk, Gnk], [ktk])
            bg, bgk = R_['trans'].next()
            k.stt(bg[:, 0:TB], be[:, 0:TB], 0.5, Gr[:, 0:TB], ALU.mult, ALU.mult, [bek, Grk], [bgk])
            kg, kgk = R_['trans'].next()
            k.tt(kg[:, 0:TB], km[:, 0:TB], Gr[:, 0:TB], ALU.mult, [kmk, Grk], [kgk], eng="gpsimd")
            toks = []
            for ti_, (src, srck) in enumerate(((bg, bgk), (kg, kgk))):
                pt_, ptk = k.r_ps.next()
                pt = pt_.bitcast(BF16)
                for ci in range(CPB):
                    for hh in range(2):
                        hb = hh * 64
                        k.tr(pt[hb:hb + 64, ci * 64:(ci + 1) * 64], src[hb:hb + 64, ci * 64:(ci + 1) * 64],
                             k.identb[hb:hb + 64, hb:hb + 64], [srck, "identb"], [ptk],
                             signal=(ci == CPB - 1 and hh == 1))
                tk_, tkk = R_['slots'][3 + ti_]
                k.cp(tk_[:, 0:TB], pt[:, 0:TB], [ptk], [tkk], eng="scalar")
                toks.append((tk_, tkk))
            (tokB, tBk), (tokK, tKk) = toks

            def tv(ci_):
                g_ = tb * CPB + ci_
                t_, tk2 = T_["tokV"][g_ // 16]
                return t_[:, (g_ % 16) * 64:(g_ % 16 + 1) * 64], tk2
            yield
            p1a, p1ak = k.r_ps.next()
            p1b, p1bk = k.r_ps.next()
            p2a, p2ak = k.r_ps.next()
            p2b, p2bk = k.r_ps.next()
            p3, p3k = k.r_ps.next()
            HC = max(CPB // 2, 1)
            for ci in range(CPB):
                cc = slice(ci * 64, (ci + 1) * 64)
                P1, P1k = (p1a, p1ak) if ci < HC else (p1b, p1bk)
                P2, P2k = (p2a, p2ak) if ci < HC else (p2b, p2bk)
                o = (ci % HC) * 128
                last = (ci == CPB - 1) or (ci == HC - 1)
                for hh in range(2):
                    hb = hh * 64
                    sg_ = last and hh == 1
                    k.mm(P1[hb:hb + 64, o:o + 128], bt[hb:hb + 64, cc], ar[hb:hb + 64, ci, :], True, True, [btk, ark], [P1k], signal=sg_)
                    k.mm(P2[hb:hb + 64, o:o + 128], kt[hb:hb + 64, cc], ar[hb:hb + 64, ci, :], True, True, [ktk, ark], [P2k], signal=sg_)
                    k.mm(p3[hb:hb + 64, cc], ar[hb:hb + 64, ci, 0:64], bt[hb:hb + 64, cc], True, True, [btk, ark], [p3k], signal=(ci == CPB - 1 and hh == 1))
            A1, A1k = R_['A'][0]
            A2, A2k = R_['A'][1]
            HW_ = HC * 128
            mA = k.maskA[d]
            k.tt(A1[:, 0:HW_], p1a[:, 0:HW_], mA[:, 0:HW_], ALU.mult, [p1ak, "maskA"], [A1k])
            k.tt(A2[:, 0:HW_], p2a[:, 0:HW_], mA[:, 0:HW_], ALU.mult, [p2ak, "maskA"], [A2k])
            if CPB > 1:
                k.tt(A1[:, HW_:2 * HW_], p1b[:, 0:HW_], mA[:, 0:HW_], ALU.mult, [p1bk, "maskA"], [A1k])
                k.tt(A2[:, HW_:2 * HW_], p2b[:, 0:HW_], mA[:, 0:HW_], ALU.mult, [p2bk, "maskA"], [A2k])
            A1v = A1[:, 0:CPB * 128].rearrange("p (a b) -> p a b", b=128)
            A2v = A2[:, 0:CPB * 128].rearrange("p (a b) -> p a b", b=128)
            W_ = CPB * 64
            qs = R_['q']
            (Q, Qk), (QT, QTk), (R, Rk) = qs[0], qs[1], qs[2]
            yield
            k.tt(QT[:, 0:W_], p3[:, 0:W_], k.maskN[d][:, 0:W_], ALU.mult, [p3k, "maskN"], [QTk])
            k.tt(R[:, 0:W_].rearrange("p (a b) -> p a b", b=64), A1v[:, :, 0:64],
                 k.eye8[:, 0:W_].rearrange("p (a b) -> p a b", b=64), ALU.add, [A1k, "eye8"], [Rk])
            for lvl in range(5):
                pq, pqk = k.r_ps.next()
                pqt, pqtk = k.r_ps.next()
                need_q = lvl < 4
                for ci in range(CPB):
                    cc = slice(ci * 64, (ci + 1) * 64)
                    for hh in range(2):
                        hb = hh * 64
                        lastm = (ci == CPB - 1 and hh == 1)
                        Qop = A1v[hb:hb + 64, ci, 0:64] if lvl == 0 else Q[hb:hb + 64, cc]
                        Qopk = A1k if lvl == 0 else Qk
                        if need_q:
                            k.mm(pq[hb:hb + 64, cc], QT[hb:hb + 64, cc], Qop, True, True, [Qopk, QTk], [pqk], signal=lastm)
                        k.mm(pqt[hb:hb + 64, cc], Qop, QT[hb:hb + 64, cc], True, True, [Qopk, QTk], [pqtk], signal=lastm)
                o_ = 3 if lvl % 2 == 0 else 0
                (Q2, Q2k), (QT2, QT2k), (R2, R2k) = qs[o_], qs[o_ + 1], qs[o_ + 2]
                if need_q:
                    k.cp(Q2[:, 0:W_], pq[:, 0:W_], [pqk], [Q2k], eng="scalar")
                k.cp(QT2[:, 0:W_], pqt[:, 0:W_], [pqtk], [QT2k], eng="vector")
                yield
                pr, prk = k.r_ps.next()
                for ci in range(CPB):
                    cc = slice(ci * 64, (ci + 1) * 64)
                    for hh in range(2):
                        hb = hh * 64
                        k.mm(pr[hb:hb + 64, cc], QT2[hb:hb + 64, cc], R[hb:hb + 64, cc], True, True, [QT2k, Rk], [prk], signal=(ci == CPB - 1 and hh == 1))
                k.tt(R2[:, 0:W_], pr[:, 0:W_], R[:, 0:W_], ALU.add, [prk, Rk], [R2k])
                Q, Qk, QT, QTk, R, Rk = Q2, Q2k, QT2, QT2k, R2, R2k
                yield
            order = range(CPB) if d == 0 else range(CPB - 1, -1, -1)
            for ci in order:
                cc = slice(ci * 64, (ci + 1) * 64)
                gcol = tb * TB + ci * 64
                pw, pwk_ = k.r_ps.next()
                for hh in range(2):
                    hb = hh * 64
                    k.mm(pw[hb:hb + 64, 0:64], A2v[hb:hb + 64, ci, 0:64], tv(ci)[0][hb:hb + 64, :], True, False, [A2k, tv(ci)[1]], [pwk_], signal=False)
                    k.mm(pw[hb:hb + 64, 0:64], ar[hb:hb + 64, ci, 0:64], Sbf[cur][hb:hb + 64, :], False, True, [ark, Sbfkey[cur]], [pwk_], signal=(hh == 1))
                Wsb, Wk = R_['ch'][0]
                k.cp(Wsb[:, 0:64], pw[:, 0:64], [pwk_], [Wk], eng="scalar")
                yield
                pu, puk = k.r_ps.next()
                for hh in range(2):
                    hb = hh * 64
                    k.mm(pu[hb:hb + 64, 0:64], R[hb:hb + 64, cc], Wsb[hb:hb + 64, 0:64], True, True, [Rk, Wk], [puk], signal=(hh == 1))
                Usb, Uk = R_['ch'][1]
                k.cp(Usb[:, 0:64], pu[:, 0:64], [puk], [Uk], eng="vector")
                yield
                py, pyk = k.r_ps.next()
                pS, pSk = k.r_ps.next()
                for hh in range(2):
                    hb = hh * 64
                    k.mm(py[hb:hb + 64, 0:64], Sbf[cur][hb:hb + 64, :], ar[hb:hb + 64, ci, 64:128], True, False, [Sbfkey[cur], ark], [pyk], signal=False)
                    k.mm(py[hb:hb + 64, 0:64], Usb[hb:hb + 64, 0:64], A1v[hb:hb + 64, ci, 64:128], False, False, [Uk, A1k], [pyk], signal=False)
                    k.mm(py[hb:hb + 64, 0:64], tv(ci)[0][hb:hb + 64, :], A2v[hb:hb + 64, ci, 64:128], False, True, [tv(ci)[1], A2k], [pyk], signal=(hh == 1))
                for hh in range(2):
                    hb = hh * 64
                    k.mm(pS[hb:hb + 64, 0:64], tokB[hb:hb + 64, cc], Usb[hb:hb + 64, 0:64], True, False, [tBk, Uk], [pSk], signal=False)
                    k.mm(pS[hb:hb + 64, 0:64], tokK[hb:hb + 64, cc], tv(ci)[0][hb:hb + 64, :], False, True, [tKk, tv(ci)[1]], [pSk], signal=(hh == 1))
                nxt = 1 - cur
                k.stt(S32[nxt], S32[cur], gC[:, ci:ci + 1], pS[:, 0:64], ALU.mult, ALU.add,
                      [S32key[cur], gCk, pSk], [S32key[nxt]])
                k.cp(Sbf[nxt], S32[nxt], [S32key[nxt]], [Sbfkey[nxt]], eng="scalar")
                cur = nxt
                ykey = (yack, gcol // 64)
                if gcol not in written_y:
                    written_y.add(gcol)
                    k.cp(yac[:, gcol:gcol + 64], py[:, 0:64], [pyk], [ykey], eng="vector")
                else:
                    k.tt(yac[:, gcol:gcol + 64], yac[:, gcol:gcol + 64], py[:, 0:64], ALU.add,
                         [ykey, pyk], [ykey])
                yield


def build_l0_rwkv(k, s, w_in, pv, PC, C):
    T, TB, NTB, NT, NCH = k.T, k.TB, k.NTB, k.NT, k.NCH
    CPB = TB // 64

    def pcol(name, i=0):
        return pv[:, PC[name] + i:PC[name] + i + 1]

    wt, wk = k.r_w128.next()
    k.dma(wt[:, :, 0:64], w_in[:, 3072:3136].rearrange("(kc p) m -> p kc m", p=128), [], [wk], eng="gpsimd")
    k.dma(wt[:, :, 64:128], w_in[:, 3168:3232].rearrange("(kc p) m -> p kc m", p=128), [], [wk], eng="gpsimd")
    tw, twk = k.big[6]
    shifted_proj(k, wt, wk, 128, pcol("mu_wl"), tw, twk, k.big[4], k.big[5])
    k.act(tw[:, 0:T], tw[:, 0:T], AF.Tanh, [twk], [twk])
    wt, wk = k.r_w128.next()
    k.dma(wt[:, :, 0:32], w_in[:, 3136:3168].rearrange("(kc p) m -> p kc m", p=128), [], [wk], eng="gpsimd")
    k.dma(wt[:, :, 32:64], w_in[:, 3232:3264].rearrange("(kc p) m -> p kc m", p=128), [], [wk], eng="gpsimd")
    als, alk = k.big[7]
    shifted_proj(k, wt, wk, 64, pv[0:64, PC["mu_al"]:PC["mu_al"] + 1], als, alk, k.big[4], k.big[5])

    for c in range(4):
        rT, rk_ = k.big[0]
        kT, kk_ = k.big[1]
        vT, vk_ = k.big[2]
        kkt, kkk = k.big[3]
        for (dst, dk_, col0, mui) in ((rT, rk_, 1536, c), (kT, kk_, 2048, 4 + c), (vT, vk_, 2560, 8 + c)):
            wt, wk = k.load_w(col0 + c * 128, 128)
            shifted_proj(k, wt, wk, 128, pcol("mu_rkv", mui), dst, dk_, k.big[4], k.big[5])
        k.ts(kkt[:, 0:T], kT[:, 0:T], pcol("k_k", c), None, ALU.mult, None, [kk_, "pv"], [kkk])
        for tb in range(NTB):
            sl = slice(tb * TB, (tb + 1) * TB)
            sq, sqk = k.r_tmp.next()
            k.act(sq[:, 0:TB], kkt[:, sl], AF.Square, [kkk], [sqk])
            ps, pk = k.r_ps.next()
            k.mm(ps[:, 0:TB], C["blk64"][:, :], sq[:, 0:TB], True, True, ["blk64", sqk], [pk])
            sd, sdk = k.r_tmp.next()
            k.act(sd[:, 0:TB], ps[:, 0:TB], AF.Ln, [pk, "eps"], [sdk], bias=k.eps_kk[:, 0:1])
            rn, rnk = k.r_tmp.next()
            k.act(rn[:, 0:TB], sd[:, 0:TB], AF.Exp, [sdk], [rnk], scale=-0.5)
            k.tt(kkt[:, sl], kkt[:, sl], rn[:, 0:TB], ALU.mult, [kkk, rnk], [kkk])
        kms, kmsk = k.big[4]
        yac, yack = k.big[5]
        bfq = k.r_sc.bufs + k.r_sc2.bufs + k.r_q.bufs + k.r_rf.bufs
        trans = MiniRing(bfq[22:25])
        T_ = {"pv": pv, "PC": PC, "rT": (rT, rk_), "kT": (kT, kk_), "vT": (vT, vk_), "kkt": (kkt, kkk),
              "kms": (kms, kmsk), "yac": (yac, yack), "tw": (tw, twk), "als": (als, alk)}
        tokVall = [(k.r_keep.bufs[i][0].bitcast(BF16), k.r_keep.bufs[i][1]) for i in range(2)]
        T_["tokV"] = tokVall
        for tb in range(NTB):
            vb, vbk = trans.next()
            k.cp(vb[:, 0:TB], vT[:, tb * TB:(tb + 1) * TB], [vk_], [vbk], eng="vector")
            pt_, ptk = k.r_ps.next()
            pt = pt_.bitcast(BF16)
            for ci in range(CPB):
                for hh in range(2):
                    hb = hh * 64
                    k.tr(pt[hb:hb + 64, ci * 64:(ci + 1) * 64], vb[hb:hb + 64, ci * 64:(ci + 1) * 64],
                         k.identb[hb:hb + 64, hb:hb + 64], [vbk, "identb"], [ptk],
                         signal=(ci == CPB - 1 and hh == 1))
            g0 = tb * CPB
            tv_, tvk = tokVall[g0 // 16]
            k.cp(tv_[:, (g0 % 16) * 64:(g0 % 16) * 64 + TB], pt[:, 0:TB], [ptk], [tvk], eng="scalar")
        finek = [(kmsk, j) for j in range(NTB)] + [(yack, j) for j in range(NCH)]
        k.fence([kmsk, yack], finek + [kmsk, yack])
        written_y, written_k = set(), set()
        gens = []
        for d in range(2):
            R_ = {"slots": bfq[11 * d:11 * d + 5], "q": bfq[11 * d + 5:11 * d + 11], "trans": trans,
                  "A": k.r_am.bufs[2 * d:2 * d + 2], "ar": k.r_ar.bufs[d], "gC": k.r_gc.bufs[d],
                  "ch": k.r_ch.bufs[2 * d:2 * d + 2],
                  "S32": [k.S32[:, 2 * d, :], k.S32[:, 2 * d + 1, :]],
                  "Sbf": [k.Sbf[:, 2 * d, :], k.Sbf[:, 2 * d + 1, :]],
                  "S32key": [("S32", 2 * d), ("S32", 2 * d + 1)], "Sbfkey": [("Sbf", 2 * d), ("Sbf", 2 * d + 1)]}
            gens.append(rwkv_stream(k, c, d, R_, C, T_, written_y, written_k))
        interleave(gens, stagger=RWKV_STAGGER)
        k.fence(finek + [kmsk, yack], [kmsk, yack])
        if k.dbg == "rwkv":
            k.dump(f"dbg_y{s}_{c}", yac[:, 0:T], [128, T], [yack])
            k.dump(f"dbg_kk{s}_{c}", kkt[:, 0:T], [128, T], [kkk])
            k.dump(f"dbg_r{s}_{c}", rT[:, 0:T], [128, T], [rk_])
        wt, wk = k.load_w(3264 + c * 128, 128)
        for tb in range(NTB):
            sl = slice(tb * TB, (tb + 1) * TB)
            t0, t0k = k.r_tmp.next()
            k.stt(t0[:, 0:TB], kms[:, sl], k.rkh[:, c:c + 1], rT[:, sl], ALU.mult, ALU.mult,
                  [kmsk, "rkh", rk_], [t0k])
            psb, pbk = k.r_ps.next()
            k.mm(psb[:, 0:TB], C["blk64"][:, :], t0[:, 0:TB], True, True, ["blk64", t0k], [pbk])
            bon, bonk = k.r_keep.next()
            k.tt(bon[:, 0:TB], psb[:, 0:TB], vT[:, sl], ALU.mult, [pbk, vk_], [bonk])
            psm, pmk = k.r_ps.next()
            k.mm(psm[:, 0:TB], k.blk64s[:, :], yac[:, sl], True, True, ["blk64s", yack], [pmk])
            sq, sqk = k.r_tmp.next()
            k.act(sq[:, 0:TB], yac[:, sl], AF.Square, [yack], [sqk])
            psq, pqk_ = k.r_ps.next()
            k.mm(psq[:, 0:TB], k.blk64s[:, :], sq[:, 0:TB], True, True, ["blk64s", sqk], [pqk_])
            msq, msqk = k.r_tmp.next()
            k.act(msq[:, 0:TB], psm[:, 0:TB], AF.Square, [pmk], [msqk])
            var, vark = k.r_tmp.next()
            k.tt(var[:, 0:TB], psq[:, 0:TB], msq[:, 0:TB], ALU.subtract, [pqk_, msqk], [vark])
            sd, sdk = k.r_tmp.next()
            k.act(sd[:, 0:TB], var[:, 0:TB], AF.Ln, [vark, "eps"], [sdk], bias=k.eps_gn[:, 0:1])
            rstd, rsk = k.r_tmp.next()
            k.act(rstd[:, 0:TB], sd[:, 0:TB], AF.Exp, [sdk], [rsk], scale=-0.5)
            t1, t1k = k.r_tmp.next()
            k.tt(t1[:, 0:TB], yac[:, sl], psm[:, 0:TB], ALU.subtract, [yack, pmk], [t1k])
            t2, t2k = k.r_tmp.next()
            k.tt(t2[:, 0:TB], t1[:, 0:TB], rstd[:, 0:TB], ALU.mult, [t1k, rsk], [t2k])
            t3, t3k = k.r_tmp.next()
            k.ts(t3[:, 0:TB], t2[:, 0:TB], pcol("gn_g", c), pcol("gn_b", c), ALU.mult, ALU.add, [t2k, "pv"], [t3k])
            t4, t4k = k.r_tmp.next()
            k.tt(t4[:, 0:TB], t3[:, 0:TB], bon[:, 0:TB], ALU.add, [t3k, bonk], [t4k])
            psg, pgk = k.proj(3264 + c * 128, 128, tb, wt, wk)
            sg, sgk = k.r_tmp.next()
            k.act(sg[:, 0:TB], psg[:, 0:TB], AF.Silu, [pgk], [sgk])
            k.emit_y(s, 4 + c, tb, t4, sg, [t4k, sgk])


def emit_y(k, s, c, tb, a, b, keys):
    TB = k.TB
    yb, ybk = k.r_sc2.next()
    k.tt(yb[:, 0:TB], a[:, 0:TB], b[:, 0:TB], ALU.mult, keys, [ybk])
    row0 = (s * 8 + c) * 128
    k.dma(k.yscr[row0:row0 + 128, tb * TB:(tb + 1) * TB], yb[:, 0:TB], [ybk], [("yscr", s, c, tb)])


K.emit_y = emit_y


def out_tile_stream(k, s, tt_, slot, x_src, dst, dkey, srckey, wo, gb, gbk, ncy):
    T = k.T
    xT = k.xT
    tok0 = s * T + tt_ * 128
    tcol = slice(tt_ * 128, (tt_ + 1) * 128)
    xb, xk0 = k.big[4 + slot // 2]
    xin = xb[:, (slot % 2) * 1024:(slot % 2 + 1) * 1024]
    xk = (xk0, slot % 2)
    rbb, rk0 = k.big[6 + slot // 2]
    rb = rbb[:, (slot % 2) * 1024:(slot % 2 + 1) * 1024]
    rbk = (rk0, slot % 2)
    st_, stk = (k.r_st.bufs + k.r_gc.bufs)[slot]
    k.dma(xin, x_src[tok0:tok0 + 128, :], [(srckey, s, tt_)] if srckey else [], [xk])
    for h in range(2):
        ps, pk = k.r_ps.next()
        for c in range(ncy):
            k.mm(ps[:, 0:512], xT[:, c, tcol], wo[h][0][:, c * 512:(c + 1) * 512], c == 0, c == ncy - 1,
                 [("xT", tt_, c), wo[h][1]], [pk])
        k.stt(rb[:, h * 512:(h + 1) * 512], xin[:, h * 512:(h + 1) * 512], ALPHA, ps[:, 0:512],
              ALU.mult, ALU.add, [xk, pk], [rbk])
    yield
    junk, jk = k.r_tmp.next()
    k.p.op("scalar", lambda e, junk=junk, rb=rb, st_=st_: e.activation(
        out=junk[:, 0:512], in_=rb[:, 0:512], func=AF.Identity, accum_out=st_[:, 0:1]), [rbk], [jk, stk])
    k.p.op("scalar", lambda e, junk=junk, rb=rb, st_=st_: e.activation(
        out=junk[:, 0:512], in_=rb[:, 512:1024], func=AF.Identity, accum_out=st_[:, 1:2]), [rbk], [jk, stk])
    k.p.op("scalar", lambda e, junk=junk, rb=rb, st_=st_: e.activation(
        out=junk[:, 0:512], in_=rb[:, 0:512], func=AF.Square, accum_out=st_[:, 2:3]), [rbk], [jk, stk])
    k.p.op("scalar", lambda e, junk=junk, rb=rb, st_=st_: e.activation(
        out=junk[:, 0:512], in_=rb[:, 512:1024], func=AF.Square, accum_out=st_[:, 3:4]), [rbk], [jk, stk])
    yield
    sk = [stk]
    k.tt(st_[:, 4:5], st_[:, 0:1], st_[:, 1:2], ALU.add, sk, sk)
    k.tt(st_[:, 5:6], st_[:, 2:3], st_[:, 3:4], ALU.add, sk, sk)
    k.ts(st_[:, 4:6], st_[:, 4:6], 1.0 / 1024.0, None, ALU.mult, None, sk, sk)
    k.tt(st_[:, 6:7], st_[:, 4:5], st_[:, 4:5], ALU.mult, sk, sk)
    k.tt(st_[:, 5:6], st_[:, 5:6], st_[:, 6:7], ALU.subtract, sk, sk)
    k.act(st_[:, 6:7], st_[:, 5:6], AF.Sqrt, sk + ["eps"], sk, bias=k.eps_ln[:, 0:1])
    k.recip(st_[:, 6:7], st_[:, 6:7], sk, sk)
    k.stt(st_[:, 7:8], st_[:, 4:5], -1.0, st_[:, 6:7], ALU.mult, ALU.mult, sk, sk)
    yield
    k.p.op("scalar", lambda e, rb=rb, st_=st_: e.activation(
        out=rb, in_=rb, func=AF.Identity, bias=st_[:, 7:8], scale=st_[:, 6:7]), [rbk] + sk, [rbk])
    k.tt(rb, rb, gb[:, 0:1024], ALU.mult, [rbk, gbk], [rbk])
    k.tt(rb, rb, gb[:, 1024:2048], ALU.add, [rbk, gbk], [rbk])
    t = k.dma(dst[tok0:tok0 + 128, :], rb, [rbk], [(dkey, s, tt_)])
    k.last_out.append(t)
    yield


def build_out_ln(k, s, x_src, li, gb_dram, dst, dkey, srckey=None, ncy=8):
    T, TB, NTB, NT = k.T, k.TB, k.NTB, k.NT
    wo = []
    for h in range(2):
        bt_, bk = k.big[1 + h]
        wv = bt_.bitcast(BF16)
        k.dma(wv[:, 0:4096].rearrange("p (c m) -> p c m", c=8),
              k.wobf[li][:, h * 512:(h + 1) * 512].rearrange("(c p) m -> p c m", p=128),
              [("wobf", li, q_) for q_ in range(4)], [bk])
        wo.append((wv, bk))
    gb, gbk = k.big[3]
    k.dma(gb[:, 0:2048], gb_dram.partition_broadcast(128), [], [gbk])
    fine = [(("big", i), j) for i in range(4, 8) for j in range(2)]
    coarse = [("big", i) for i in range(4, 8)]
    k.fence(coarse, fine + coarse)
    xkeys = [("xT", i) for i in range(NT)]
    xfine = [("xT", i, c) for i in range(NT) for c in range(ncy)]
    k.fence(xkeys, xfine + xkeys)
    for tb in range(NTB):
        for c in range(ncy):
            row0 = (s * 8 + c) * 128
            k.dma(k.xT[:, c, tb * TB:(tb + 1) * TB], k.yscr[row0:row0 + 128, tb * TB:(tb + 1) * TB],
                  [("yscr", s, c, tb)], [("xT", tb * (TB // 128) + i, c) for i in range(TB // 128)])
    gens = [out_tile_stream(k, s, tt_, tt_ % 4, x_src, dst, dkey, srckey, wo, gb, gbk, ncy) for tt_ in range(NT)]
    active = []
    pend = list(gens)
    while pend or active:
        if pend and len(active) < 4:
            active.append(pend.pop(0))
        for g in list(active):
            try:
                next(g)
            except StopIteration:
                active.remove(g)
    k.fence(fine + coarse + xfine + xkeys, coarse + xkeys)


def build(T, NSEQ, dbg=None, layers=(0, 1)):
    k = K(T, NSEQ, dbg)
    BW = 2080
    x_dram = k.din("x", [NSEQ * T, D])
    w_in0 = k.din("l0_w_in", [D, EVEN_COLS])
    w_out0 = k.din("l0_w_out", [D, D])
    gb0 = k.din("l0_gb", [1, 2048])
    PC = pack_l0_cols()
    NPV = PC["_n"]
    pv0_d = k.din("pv0", [128, NPV])
    wup_d = k.din("l0_wup", [128, 512])
    aup_d = k.din("l0_aup", [64, 512])
    w0x_d = k.din("l0_w0x", [128, 512])
    w_in1 = k.din("l1_w_in", [D, ODD_COLS])
    w_out1 = k.din("l1_w_out", [D, D])
    gb1 = k.din("l1_gb", [1, 2048])
    gup_d = k.din("l1_gup", [64, 512])
    gbx_d = k.din("l1_gbx", [128, 512])
    ng_d = k.din("l1_ng", [1, 1024])
    consts = host_consts()
    cd = {n: k.din("c_" + n, list(a.shape)) for n, a in consts.items()}
    out_d = k.dout("out", [NSEQ * T, D])
    k.yscr = k.dscr("yscr", [NSEQ * 8 * 128, T], BF16)
    x1scr = k.dscr("x1scr", [NSEQ * T, D])
    k.last_out = []
    k.wobf = {}
    for li, wsrc in ((0, w_out0), (1, w_out1)):
        if li in layers:
            scr = k.dscr(f"wobf{li}", [D, D], BF16)
            for q_ in range(4):
                k.dma(scr[q_ * 256:(q_ + 1) * 256, :], wsrc[q_ * 256:(q_ + 1) * 256, :], [], [("wobf", li, q_)],
                      eng="gpsimd")
            k.wobf[li] = scr
    k.xT = k.sb("xT", [128, 8, T], BF16)
    k.r_ps = k.ring("ps", 8, [128, 512], F32, psum=True)
    k.r_w128 = k.ring("w128_", 6, [128, 8, 128], BF16)
    k.r_tmp = k.ring("tmp", 12, [128, 512])
    k.r_keep = k.ring("keep", 4, [128, 512])
    k.big = [(k.sb(f"big{i}", [128, BW]), ("big", i)) for i in range(8)]
    k.r_xin = Ring.__new__(Ring)
    k.r_xin.bufs = [k.big[4], k.big[5]]
    k.r_xin.i = 0
    k.r_ar = k.ring("ar", 2, [128, 8, 128], BF16)
    k.r_sc = k.ring("sc", 10, [128, 512], BF16)
    k.r_sc2 = k.ring("scb", 5, [128, 512], BF16)
    k.r_am = k.ring("am", 4, [128, 1024], BF16)
    k.r_q = k.ring("q", 8, [128, 512], BF16)
    k.r_rf = k.ring("rf", 2, [128, 512], BF16)
    k.r_ch = k.ring("ch", 4, [128, 64], BF16)
    k.r_gc = k.ring("gc", 3, [128, 8])
    k.r_st = k.ring("st", 3, [128, 8])
    k.S32 = k.sb("S32", [128, 4, 64])
    k.Sbf = k.sb("Sbf", [128, 4, 64], BF16)
    k.wga = {}
    pv = k.sb("pv", [128, NPV])
    k.dma(pv[:, :], pv0_d[:, :], [], ["pv"])
    C = {}
    for n, a in consts.items():
        if n in BF_CONSTS:
            C[n] = k.sb("C_" + n, list(a.shape), BF16)
            k.dma(C[n][:, :], cd[n][:, :], [], [n], eng="gpsimd")
        else:
            C[n] = k.sb("C_" + n, list(a.shape))
            k.dma(C[n][:, :], cd[n][:, :], [], [n])
    k.maskA = [C["maskA0"], C["maskA1"]]
    k.maskN = [C["maskN0"], C["maskN1"]]
    k.eye8 = C["eye8"]
    k.identb = k.sb("identb", [128, 128], BF16)
    k.dma(k.identb[:, :], cd["ident"][:, :], [], ["identb"], eng="gpsimd")
    k.wup = k.sb("wup", [128, 512])
    k.dma(k.wup[:, :], wup_d[:, :], [], ["wup"])
    k.aup = k.sb("aup", [64, 512])
    k.dma(k.aup[:, :], aup_d[:, :], [], ["aup"])
    k.w0x = k.sb("w0x", [128, 512])
    k.dma(k.w0x[:, :], w0x_d[:, :], [], ["w0x"])
    k.gup = k.sb("gup", [64, 512])
    k.dma(k.gup[:, :], gup_d[:, :], [], ["gup"])
    k.gbx = k.sb("gbx", [128, 512])
    k.dma(k.gbx[:, :], gbx_d[:, :], [], ["gbx"])
    k.ng_d = ng_d
    k.Sg32 = k.sb("Sg32", [128, 256])
    k.Sgbf = k.sb("Sgbf", [128, 256], BF16)
    k.ssq = k.sb("ssq", [128, 48])
    k.onec = k.sb("onec", [128, 2])
    k.memset(k.onec[:, 0:1], 1.0, ["onec"])
    k.memset(k.onec[:, 1:2], 1e-6, ["onec"])
    eps = k.sb("epsv", [128, 4])
    k.memset(eps[:, 0:1], 1e-5, ["eps"])
    k.memset(eps[:, 1:2], 1e-12, ["eps"])
    k.memset(eps[:, 2:3], 64e-5, ["eps"])
    k.eps_ln, k.eps_kk, k.eps_gn = eps[:, 0:1], eps[:, 1:2], eps[:, 2:3]
    k.blk64s = k.sb("blk64s", [128, 128])
    k.ts(k.blk64s[:, :], C["blk64"][:, :], 1.0 / 64.0, None, ALU.mult, None, ["blk64"], ["blk64s"])
    k.omk = k.sb("omk", [128, 4])
    k.ts(k.omk[:, :], pv[:, PC["k_a"]:PC["k_a"] + 4], -0.5, 1.0, ALU.mult, ALU.add, ["pv"], ["omk"])
    k.hka = k.sb("hka", [128, 4])
    k.ts(k.hka[:, :], pv[:, PC["k_a"]:PC["k_a"] + 4], 0.5, None, ALU.mult, None, ["pv"], ["hka"])
    k.ha0 = k.sb("ha0", [128, 8])
    k.ts(k.ha0[:, :], pv[:, PC["a0_0"]:PC["a0_0"] + 8], 0.5, None, ALU.mult, None, ["pv"], ["ha0"])
    k.rkh = k.sb("rkh", [128, 4])
    k.ts(k.rkh[:, :], pv[:, PC["r_k"]:PC["r_k"] + 4], 0.5, None, ALU.mult, None, ["pv"], ["rkh"])

    for s in range(NSEQ):
        if 0 in layers:
            build_xT(k, s, x_dram, w_in0, C)
            build_l0_front(k, s, x_dram, w_in0, pv, PC, C)
            build_l0_rwkv(k, s, w_in0, pv, PC, C)
            if 1 in layers:
                build_out_ln(k, s, x_dram, 0, gb0, x1scr, "x1")
            else:
                build_out_ln(k, s, x_dram, 0, gb0, out_d, "out")
                k.finals.extend(k.last_out)
            k.last_out = []
        if 1 in layers:
            src, sk = (x1scr, "x1") if 0 in layers else (x_dram, None)
            build_xT(k, s, src, w_in1, C, sk)
            build_l1_gla(k, s, w_in1, C)
            build_out_ln(k, s, src, 1, gb1, out_d, "out", sk)
            k.finals.extend(k.last_out)
            k.last_out = []
    return k.finish()


def host_inputs(inp, x_core):
    pv0, _ = pack_l0(inp)
    m = {"x": np.ascontiguousarray(x_core.reshape(-1, D)), "l0_w_in": np.asarray(inp["l0_w_in"]),
         "l0_w_out": np.asarray(inp["l0_w_out"]), "pv0": pv0,
         "l0_gb": np.concatenate([inp["l0_ln_g"], inp["l0_ln_b"]])[None, :].astype(np.float32),
         "l0_wup": np.ascontiguousarray(np.asarray(inp["l0_w_up"]).reshape(128, 512)),
         "l0_aup": np.ascontiguousarray(np.asarray(inp["l0_a_up"]).reshape(64, 512))}
    w0x = np.zeros((128, 512), np.float32)
    w0x[0] = inp["l0_w0"][0]
    w0x[64] = inp["l0_w0"][1]
    m["l0_w0x"] = w0x
    m["l1_w_in"] = np.asarray(inp["l1_w_in"])
    m["l1_w_out"] = np.asarray(inp["l1_w_out"])
    m["l1_gb"] = np.concatenate([inp["l1_ln_g"], inp["l1_ln_b"]])[None, :].astype(np.float32)
    gup = np.zeros((64, 512), np.float32)
    gup[0:16] = inp["l1_g_up"][0]
    gup[32:48] = inp["l1_g_up"][1]
    m["l1_gup"] = gup
    gbx = np.zeros((128, 512), np.float32)
    gbx[0] = inp["l1_g_bias"][0]
    gbx[32] = inp["l1_g_bias"][1]
    m["l1_gbx"] = gbx
    m["l1_ng"] = np.asarray(inp["l1_norm_g"])[None, :].astype(np.float32)
    for n, a in host_consts().items():
        m["c_" + n] = a
    return m


CG = -1.0 / 16.0
SEQ_STREAMS = False
RWKV_STAGGER = 5
GLA_STAGGER = 3


def interleave(gens, stagger=0):
    gens = list(gens)
    if SEQ_STREAMS:
        for g in gens:
            for _ in g:
                pass
        return
    active = []
    rnd = 0
    pending = list(enumerate(gens))
    while pending or active:
        while pending and pending[0][0] * stagger <= rnd:
            active.append(pending.pop(0)[1])
        for g in list(active):
            try:
                next(g)
            except StopIteration:
                active.remove(g)
        rnd += 1


class MiniRing:
    def __init__(self, bufs):
        self.bufs = list(bufs)
        self.i = 0

    def next(self):
        b = self.bufs[self.i % len(self.bufs)]
        self.i += 1
        return b


def gla_stream(k, s, h, d, R, C, written):
    T, TB, NTB, NT = k.T, k.TB, k.NTB, k.NT
    CPB = TB // 64
    TPB = TB // 128
    xT, xT_keys = k.xT, k.xT_keys
    lr, lrk = R["lr"]
    (wq, wqk), (wkk, wkkk), (wv0, wv0k), (wv1, wv1k) = R["w"]
    (qt, qtk), (kt, ktk), (tokK, tKk) = R["slots"]
    tokV, tVk = R["tokV"]
    S32t, S32k = R["S32"]
    Sbft, Sbfk = R["Sbf"]
    gC, gCk = R["gC"]
    oview = R["oview"]
    trans = R["trans"]
    S32 = [S32t[:, 0:256], S32t[:, 256:512]]
    Sbf = [Sbft[:, 0:256], Sbft[:, 256:512]]
    S32key = [(S32k, 0), (S32k, 1)]
    Sbfkey = [(Sbfk, 0), (Sbfk, 1)]
    cur = 0
    k.memset(S32[0], 0.0, [S32key[0]], eng="vector")
    k.memset(Sbf[0], 0.0, [Sbfkey[0]], eng="vector")
    blocks = range(NTB) if d == 0 else range(NTB - 1, -1, -1)
    for tb in blocks:
        psl, plk = k.r_ps.next()
        for i in range(TPB):
            tcol = slice(tb * TB + i * 128, tb * TB + (i + 1) * 128)
            k.mm(psl[:, i * 128:(i + 1) * 128], lr[d * 32:d * 32 + 16, tcol],
                 k.gup[d * 32:d * 32 + 16, h * 128:(h + 1) * 128], True, False, [lrk, "gup"], [plk])
            k.mm(psl[:, i * 128:(i + 1) * 128], C["ones128"][d * 32:d * 32 + 1, :],
                 k.gbx[d * 32:d * 32 + 1, h * 128:(h + 1) * 128], False, True, ["ones128", "gbx"], [plk])
        e1, e1k = k.r_tmp.next()
        k.act(e1[:, 0:TB], psl[:, 0:TB], AF.Exp, [plk], [e1k], scale=-1.0)
        sp, spk = k.r_tmp.next()
        k.act(sp[:, 0:TB], e1[:, 0:TB], AF.Ln, [e1k, "onec"], [spk], bias=k.onec[:, 0:1])
        pcs = []
        for x in (1, 2):
            pc_, pck = k.r_ps.next()
            for i in range(TPB):
                k.mm(pc_[:, i * 128:(i + 1) * 128], sp[:, i * 128:(i + 1) * 128],
                     C["cmask"][:, (d * 3 + x) * 128:(d * 3 + x + 1) * 128], True, True,
                     [spk, "cmask"], [pck])
            pcs.append((pc_, pck))
        Gi, Gik = k.r_tmp.next()
        k.act(Gi[:, 0:TB], pcs[0][0][:, 0:TB], AF.Exp, [pcs[0][1]], [Gik], scale=CG)
        Gn, Gnk = k.r_tmp.next()
        k.act(Gn[:, 0:TB], pcs[0][0][:, 0:TB], AF.Exp, [pcs[0][1]], [Gnk], scale=-CG)
        Gr, Grk = k.r_tmp.next()
        k.act(Gr[:, 0:TB], pcs[1][0][:, 0:TB], AF.Exp, [pcs[1][1]], [Grk], scale=CG)
        lastcol = 63 if d == 0 else 0
        k.act(gC[:, 0:CPB], pcs[0][0][:, 0:TB].rearrange("p (a b) -> p a b", b=64)[:, :, lastcol],
              AF.Exp, [pcs[0][1]], [gCk], scale=CG)
        psq, pqk = k.proj(h * 128, 128, tb, wq, wqk)
        k.stt(qt[:, 0:TB], psq[:, 0:TB], float(128 ** -0.5), Gi[:, 0:TB], ALU.mult, ALU.mult,
              [pqk, Gik], [qtk])
        psk, pkk = k.proj(512 + h * 128, 128, tb, wkk, wkkk)
        k.tt(kt[:, 0:TB], psk[:, 0:TB], Gn[:, 0:TB], ALU.mult, [pkk, Gnk], [ktk])
        kg, kgk = trans.next()
        k.tt(kg[:, 0:TB], psk[:, 0:TB], Gr[:, 0:TB], ALU.mult, [pkk, Grk], [kgk])
        pt_, ptk = k.r_ps.next()
        pt = pt_.bitcast(BF16)
        for i in range(TPB):
            k.tr(pt[:, i * 128:(i + 1) * 128], kg[:, i * 128:(i + 1) * 128], k.identb[:, :],
                 [kgk, "identb"], [ptk], signal=(i == TPB - 1))
        k.cp(tokK[:, 0:TB], pt[:, 0:TB], [ptk], [tKk], eng="scalar")
        yield
        pss, pssk = k.r_ps.next()
        for i in range(TPB):
            cc = slice(i * 128, (i + 1) * 128)
            k.mm(pss[:, cc], kt[:, cc], qt[:, cc], True, True, [ktk, qtk], [pssk])
        ST, STk = trans.next()
        k.tt(ST[:, 0:TB], pss[:, 0:TB], C[f"gmask{d}"][:, 0:TB], ALU.mult, [pssk, f"gmask{d}"], [STk])
        for i in range(TPB):
            tt_ = tb * TPB + i
            po, pok = k.r_ps.next()
            k.mm(po[:, 0:256], ST[:, i * 128:(i + 1) * 128], tokV[:, tt_ * 256:(tt_ + 1) * 256], True, True,
                 [STk, tVk], [pok])
            ov, ovk = oview(tt_)
            if (h, tt_) not in written:
                written.add((h, tt_))
                k.cp(ov, po[:, 0:256], [pok], [ovk], eng="scalar")
            else:
                k.tt(ov, ov, po[:, 0:256], ALU.add, [ovk, pok], [ovk])
        yield
        order = range(CPB) if d == 0 else range(CPB - 1, -1, -1)
        for ci in order:
            i, hp = ci // 2, (ci % 2) * 64
            tt_ = tb * TPB + i
            cc = slice(ci * 64, (ci + 1) * 64)
            nxt = 1 - cur
            po, pok = k.r_ps.next()
            k.mm(po[hp:hp + 64, 0:256], qt[:, cc], Sbf[cur], True, True, [qtk, Sbfkey[cur]], [pok])
            pS, pSk = k.r_ps.next()
            k.mm(pS[:, 0:256], tokK[hp:hp + 64, i * 128:(i + 1) * 128],
                 tokV[hp:hp + 64, tt_ * 256:(tt_ + 1) * 256], True, True, [tKk, tVk], [pSk])
            k.stt(S32[nxt], S32[cur], gC[:, ci:ci + 1], pS[:, 0:256], ALU.mult, ALU.add,
                  [S32key[cur], gCk, pSk], [S32key[nxt]])
            k.cp(Sbf[nxt], S32[nxt], [S32key[nxt]], [Sbfkey[nxt]], eng="scalar")
            ov, ovk = oview(tt_)
            k.tt(ov[hp:hp + 64, :], ov[hp:hp + 64, :], po[hp:hp + 64, 0:256], ALU.add, [ovk, pok], [ovk])
            cur = nxt
            yield


def build_l1_gla(k, s, w_in, C):
    T, TB, NTB, NT = k.T, k.TB, k.NTB, k.NT
    TPB = TB // 128
    xT, xT_keys = k.xT, k.xT_keys
    wt, wk = k.r_w128.next()
    k.memset(wt[:, :, 0:64], 0.0, [wk])
    k.dma(wt[:, :, 0:16], w_in[:, 3072:3088].rearrange("(kc p) m -> p kc m", p=128), [], [wk], eng="gpsimd")
    k.dma(wt[:, :, 32:48], w_in[:, 3088:3104].rearrange("(kc p) m -> p kc m", p=128), [], [wk], eng="gpsimd")
    lr, lrk = k.big[4]
    for tb in range(NTB):
        ps, pk = k.r_ps.next()
        for kc in range(8):
            k.mm(ps[0:64, 0:TB], wt[:, kc, 0:64], xT[:, kc, tb * TB:(tb + 1) * TB], kc == 0, kc == 7,
                 [wk] + xT_keys, [pk])
        k.cp(lr[0:64, tb * TB:(tb + 1) * TB], ps[0:64, 0:TB], [pk], [lrk], eng="scalar")
    ngb, ngk = k.big[5]
    k.dma(ngb[:, 0:1024], k.ng_d.partition_broadcast(128), [], [ngk])
    fine = []
    for i in range(4):
        fine += [(("big", i), j) for j in range(8)]
        fine += [(("keep", i), j) for j in range(2)] + [(("q", i), j) for j in range(2)]
    coarse = [("big", i) for i in range(4)] + [("keep", i) for i in range(4)] + [("q", i) for i in range(4)]
    k.fence(coarse, fine + coarse)
    NT = k.NT
    bf512 = k.r_sc.bufs + k.r_sc2.bufs
    trans = MiniRing(bf512[12:15])
    small = k.r_gc.bufs + k.r_st.bufs
    wtiles = k.r_w128.bufs + k.r_ar.bufs
    for pair in range(2):
        heads = (2 * pair, 2 * pair + 1)
        written = set()
        gens = []
        wi = 0

        def loadw(tile, c0):
            wt_, wk_ = tile
            k.dma(wt_[:, :, 0:128], w_in[:, c0:c0 + 128].rearrange("(kc p) m -> p kc m", p=128), [], [wk_],
                  eng="gpsimd")
            return tile

        ovs = {}
        for hi, h in enumerate(heads):
            ws = [loadw(wtiles[hi * 4 + 0], h * 128), loadw(wtiles[hi * 4 + 1], 512 + h * 128),
                  loadw(wtiles[hi * 4 + 2], 1024 + h * 256), loadw(wtiles[hi * 4 + 3], 1024 + h * 256 + 128)]
            oacc = [k.big[2 * hi], k.big[2 * hi + 1]]

            def oview(tt_, oacc=oacc):
                b, bk = oacc[tt_ // 8]
                return b[:, (tt_ % 8) * 256:(tt_ % 8 + 1) * 256], (bk, tt_ % 8)

            ovs[h] = oview
            tvb, tvk_ = k.big[6 + hi]
            tokVh = tvb.bitcast(BF16)
            for tt_ in range(NT):
                tcol = slice(tt_ * 128, (tt_ + 1) * 128)
                pv_, pvk = k.r_ps.next()
                for half in range(2):
                    wv, wvk = ws[2 + half]
                    for kc in range(8):
                        k.mm(pv_[:, half * 128:(half + 1) * 128], xT[:, kc, tcol], wv[:, kc, 0:128],
                             kc == 0, kc == 7, [wvk] + xT_keys, [pvk])
                k.cp(tokVh[:, tt_ * 256:(tt_ + 1) * 256], pv_[:, 0:256], [pvk], [tvk_],
                     eng=("scalar" if tt_ % 2 else "vector"))
            for d in range(2):
                si = hi * 2 + d
                R = {"lr": (lr, lrk), "w": ws, "slots": bf512[3 * si:3 * si + 3], "tokV": (tokVh, tvk_),
                     "S32": k.r_keep.bufs[si], "Sbf": k.r_q.bufs[si], "gC": small[si], "oview": oview,
                     "trans": trans}
                gens.append(gla_stream(k, s, h, d, R, C, written))
        interleave(gens, stagger=GLA_STAGGER)
        for hi, h in enumerate(heads):
            oview = ovs[h]
            wg0, wg0k = loadw(wtiles[hi * 4 + 0], 2048 + h * 256)
            wg1, wg1k = loadw(wtiles[hi * 4 + 1], 2048 + h * 256 + 128)
            ssq = k.ssq
            for tt_ in range(NT):
                ov, ovk = oview(tt_)
                junk, jk = k.r_tmp.next()
                k.p.op("scalar", lambda e, junk=junk, ov=ov, ssq=ssq, tt_=tt_: e.activation(
                    out=junk[:, 0:256], in_=ov, func=AF.Square, accum_out=ssq[:, tt_:tt_ + 1]), [ovk], [jk, "ssq"])
            k.act(ssq[:, 16:16 + NT], ssq[:, 0:NT], AF.Sqrt, ["ssq", "onec"], ["ssq"], bias=k.onec[:, 1:2],
                  scale=1.0 / 256.0)
            k.recip(ssq[:, 32:32 + NT], ssq[:, 16:16 + NT], ["ssq"], ["ssq"])
            for tt_ in range(NT):
                ov, ovk = oview(tt_)
                tcol = slice(tt_ * 128, (tt_ + 1) * 128)
                on, onk = k.r_tmp.next()
                k.stt(on[:, 0:256], ov, ssq[:, 32 + tt_:33 + tt_], ngb[:, h * 256:(h + 1) * 256], ALU.mult, ALU.mult,
                      [ovk, "ssq", ngk], [onk])
                pg, pgk = k.r_ps.next()
                for half, (wg, wgk) in enumerate(((wg0, wg0k), (wg1, wg1k))):
                    for kc in range(8):
                        k.mm(pg[:, half * 128:(half + 1) * 128], xT[:, kc, tcol], wg[:, kc, 0:128],
                             kc == 0, kc == 7, [wgk] + xT_keys, [pgk])
                sg, sgk = k.r_tmp.next()
                k.act(sg[:, 0:256], pg[:, 0:256], AF.Silu, [pgk], [sgk])
                yb, ybk = trans.next()
                k.tt(yb[:, 0:256], on[:, 0:256], sg[:, 0:256], ALU.mult, [onk, sgk], [ybk])
                pt_, ptk = k.r_ps.next()
                pt = pt_.bitcast(BF16)
                for half in range(2):
                    k.tr(pt[:, half * 128:(half + 1) * 128], yb[:, half * 128:(half + 1) * 128], k.identb[:, :],
                         [ybk, "identb"], [ptk], signal=(half == 1))
                yf, yfk = trans.next()
                k.cp(yf[:, 0:256], pt[:, 0:256], [ptk], [yfk], eng="vector")
                tb = tt_ // TPB
                for half in range(2):
                    row0 = (s * 8 + 2 * h + half) * 128
                    k.dma(k.yscr[row0:row0 + 128, tt_ * 128:(tt_ + 1) * 128], yf[:, half * 128:(half + 1) * 128],
                          [yfk], [("yscr", s, 2 * h + half, tb), ("yscrw", s, 2 * h + half, tt_)])
    k.fence(fine + coarse, coarse)


N_CORES = 8
SEQ_LEN = 2048
BATCH = 16


def kernel(**inputs):
    inp = {n: np.asarray(v) for n, v in inputs.items()}
    x = inp["x"]
    nseq = BATCH // N_CORES
    nc = build(SEQ_LEN, nseq, layers=(0, 1))
    in_maps = [host_inputs(inp, x[c * nseq:(c + 1) * nseq]) for c in range(N_CORES)]
    res = run_bass_kernel_spmd(nc, in_maps, core_ids=list(range(N_CORES)))
    outs = [np.asarray(r["out"]).reshape(nseq, SEQ_LEN, D) for r in res.results]
    return np.concatenate(outs, 0).astype(np.float32)
```

```python
import contextlib
import numpy as np
import concourse.bass as bass
import concourse.mybir as mybir
from concourse.bass_utils import run_bass_kernel_spmd

F32 = mybir.dt.float32
BF16 = mybir.dt.bfloat16
AF = mybir.ActivationFunctionType
ALU = mybir.AluOpType
AX = mybir.AxisListType

ENGS = ("tensor", "vector", "scalar", "gpsimd", "sync")

D = 1024
EVEN_COLS = 3776
ODD_COLS = 3104
ALPHA = float((2 * 2) ** 0.25)
CW = -float(np.exp(-0.5))


class Prog:
    NDMA_SEMS = 24

    def __init__(self, nc):
        self.nc = nc
        self.ops = {e: [] for e in ENGS}
        self.cnt = {e: 0 for e in ENGS}
        self.pending = {e: False for e in ENGS}
        self.waited = {e: {} for e in ENGS}
        self.last_w = {}
        self.readers = {}
        self.dma_i = 0
        self.dma_j = 0
        self.dma_cnt = [0] * self.NDMA_SEMS
        self.dma_last = [None] * self.NDMA_SEMS

    def _deps(self, eng, reads, writes):
        need = {}

        def add(t):
            if t is None:
                return
            k, v, te = t
            if te == eng and eng == "tensor":
                return
            if need.get(k, 0) < v:
                need[k] = v

        for r in reads:
            add(self.last_w.get(r))
        for w in writes:
            add(self.last_w.get(w))
            for t in self.readers.get(w, ()):
                if t[2] == eng and t[0][0] == "e":
                    continue
                add(t)
        out = []
        for k, v in need.items():
            if self.waited[eng].get(k, 0) >= v:
                continue
            self.waited[eng][k] = v
            out.append((k, v))
        return out

    def _commit(self, ticket, reads, writes):
        for r in reads:
            self.readers.setdefault(r, []).append(ticket)
        for w in writes:
            self.last_w[w] = ticket
            self.readers[w] = []

    def op(self, eng, fn, reads=(), writes=(), signal=True):
        waits = self._deps(eng, reads, writes)
        if signal:
            self.cnt[eng] += 1
            ticket = (("e", eng), self.cnt[eng], eng)
            self.pending[eng] = False
        else:
            ticket = (("e", eng), self.cnt[eng] + 1, eng)
            self.pending[eng] = True
        self.ops[eng].append((fn, waits, (("e", eng), 1) if signal else None))
        self._commit(ticket, reads, writes)
        return ticket

    def dma(self, fn, reads=(), writes=(), eng="sync"):
        half = self.NDMA_SEMS // 2
        if eng == "sync":
            i = self.dma_i % half
            self.dma_i += 1
        else:
            i = half + self.dma_j % half
            self.dma_j += 1
        waits = self._deps(eng, reads, writes)
        prev = self.dma_last[i]
        if prev is not None and self.waited[eng].get(prev[0], 0) < prev[1]:
            self.waited[eng][prev[0]] = prev[1]
            waits.append((prev[0], prev[1]))
        self.dma_cnt[i] += 16
        ticket = (("d", i), self.dma_cnt[i], eng)
        self.dma_last[i] = ticket
        self.ops[eng].append((fn, waits, (("d", i), 16)))
        self._commit(ticket, reads, writes)
        return ticket

    def emit(self, st, final_tickets):
        nc = self.nc
        for e in ENGS:
            assert not self.pending[e], f"unsignalled trailing op on {e}"
        sems = {}
        for e in ENGS:
            sems[("e", e)] = st.enter_context(nc.semaphore(f"s_{e}"))
        for i in range(self.NDMA_SEMS):
            sems[("d", i)] = st.enter_context(nc.semaphore(f"s_d{i}"))
        block = st.enter_context(nc.Block())

        def run(engname):
            def body(eng):
                for fn, waits, inc in self.ops[engname]:
                    for k, v in waits:
                        eng.wait_ge(sems[k], v)
                    ins = fn(eng)
                    if inc is not None:
                        ins.then_inc(sems[inc[0]], inc[1])
                if engname == "sync":
                    for k, v, _ in final_tickets:
                        eng.wait_ge(sems[k], v)
            return body

        block.tensor(run("tensor"))
        block.vector(run("vector"))
        block.scalar(run("scalar"))
        block.gpsimd(run("gpsimd"))
        block.sync(run("sync"))


class Ring:
    def __init__(self, st, nc, name, n, shape, dtype, psum=False):
        self.bufs = []
        for i in range(n):
            if psum:
                h = st.enter_context(nc.psum_tensor(f"{name}{i}", shape, dtype))
            else:
                h = st.enter_context(nc.sbuf_tensor(f"{name}{i}", shape, dtype))
            self.bufs.append((h, (name, i)))
        self.i = 0

    def next(self):
        b = self.bufs[self.i % len(self.bufs)]
        self.i += 1
        return b


def _chunks(v, n):
    return np.ascontiguousarray(v.reshape(n, 128).T)


def pack_l0(inp):
    cols = {}
    parts = []
    pos = 0

    def add(name, arr):
        nonlocal pos
        arr = np.asarray(arr, np.float32)
        if arr.shape[0] < 128:
            pad = np.zeros((128 - arr.shape[0], arr.shape[1]), np.float32)
            arr = np.concatenate([arr, pad], 0)
        cols[name] = pos
        pos += arr.shape[1]
        parts.append(arr)

    cw = np.asarray(inp["l0_conv_w"])
    for c in range(4):
        add(f"convw{c}", cw[:, c * 128:(c + 1) * 128].T)
    add("convb", _chunks(inp["l0_conv_b"], 4))
    add("clng", _chunks(inp["l0_conv_ln_g"], 4))
    add("clnb", _chunks(inp["l0_conv_ln_b"], 4))
    mu = np.asarray(inp["l0_mu_shift"])
    add("mu_rkv", _chunks(mu[:1536], 12))
    lo = mu[1536:]
    add("mu_wl", np.concatenate([lo[0:64], lo[96:160]])[:, None])
    add("mu_al", np.concatenate([lo[64:96], lo[160:192]])[:, None])
    add("a0_0", _chunks(inp["l0_a0"][0], 4))
    add("a0_1", _chunks(inp["l0_a0"][1], 4))
    add("k_k", _chunks(inp["l0_k_k"], 4))
    add("k_a", _chunks(inp["l0_k_a"], 4))
    add("r_k", _chunks(np.asarray(inp["l0_r_k"]).reshape(-1), 4))
    add("gn_g", _chunks(inp["l0_gn_g"], 4))
    add("gn_b", _chunks(inp["l0_gn_b"], 4))
    return np.concatenate(parts, 1), cols


class K:
    def __init__(self, T, NSEQ, dbg=None):
        self.T, self.NSEQ = T, NSEQ
        self.TB = min(512, T)
        self.NTB = T // self.TB
        self.NT = T // 128
        self.NCH = T // 64
        self.dbg = dbg
        self.nc = bass.Bass("TRN2", target_bir_lowering=False)
        self.p = Prog(self.nc)
        self.st = contextlib.ExitStack()
        self.finals = []

    def sb(self, name, shape, dt=F32):
        return self.st.enter_context(self.nc.sbuf_tensor(name, list(shape), dt))

    def ring(self, name, n, shape, dt=F32, psum=False):
        return Ring(self.st, self.nc, name, n, list(shape), dt, psum)

    def din(self, name, shape, dt=F32):
        return self.nc.dram_tensor(name, list(shape), dt, kind="ExternalInput").ap()

    def dout(self, name, shape, dt=F32):
        return self.nc.dram_tensor(name, list(shape), dt, kind="ExternalOutput").ap()

    def dscr(self, name, shape, dt=F32):
        return self.nc.dram_tensor(name, list(shape), dt, kind="Internal").ap()

    def mm(self, out, lhsT, rhs, start, stop, reads, writes, signal=None):
        self.p.op("tensor", lambda e: e.matmul(out, lhsT=lhsT, rhs=rhs, start=start, stop=stop),
                  reads, writes, signal=(stop if signal is None else signal))

    def tr(self, out, in_, ident, reads, writes, signal=True):
        self.p.op("tensor", lambda e: e.transpose(out=out, in_=in_, identity=ident),
                  reads, writes, signal=signal)

    def act(self, out, in_, func, reads, writes, bias=0.0, scale=1.0):
        self.p.op("scalar", lambda e: e.activation(out=out, in_=in_, func=func, bias=bias, scale=scale),
                  reads, writes)

    def tt(self, out, in0, in1, op, reads, writes, eng="vector"):
        self.p.op(eng, lambda e: e.tensor_tensor(out=out, in0=in0, in1=in1, op=op), reads, writes)

    def ts(self, out, in0, s1, s2, op0, op1, reads, writes, eng="vector"):
        if op1 is None:
            self.p.op(eng, lambda e: e.tensor_scalar(out=out, in0=in0, scalar1=s1, scalar2=None, op0=op0),
                      reads, writes)
        else:
            self.p.op(eng, lambda e: e.tensor_scalar(out=out, in0=in0, scalar1=s1, scalar2=s2, op0=op0, op1=op1),
                      reads, writes)

    def stt(self, out, in0, scalar, in1, op0, op1, reads, writes):
        self.p.op("vector", lambda e: e.scalar_tensor_tensor(out=out, in0=in0, scalar=scalar, in1=in1,
                                                            op0=op0, op1=op1), reads, writes)

    def cp(self, out, in_, reads, writes, eng="vector"):
        if eng == "scalar":
            self.p.op("scalar", lambda e: e.copy(out=out, in_=in_), reads, writes)
        else:
            self.p.op(eng, lambda e: e.tensor_copy(out=out, in_=in_), reads, writes)

    def memset(self, ap, val, writes, eng="gpsimd"):
        self.p.op(eng, lambda e: e.memset(ap, val), (), writes)

    def fence(self, reads, writes):
        if not hasattr(self, "_fz"):
            self._fz = self.sb("fence_scratch", [128, 1])
        self.p.op("vector", lambda e: e.memset(self._fz[:, :], 0.0), reads, list(writes) + ["_fence_scratch"])

    def recip(self, out, in_, reads, writes):
        self.p.op("vector", lambda e: e.reciprocal(out=out, in_=in_), reads, writes)

    def dma(self, out, in_, reads, writes, eng="sync"):
        return self.p.dma(lambda e: e.dma_start(out=out, in_=in_), reads, writes, eng=eng)

    def dump(self, name, ap, shape, reads, dt=F32):
        o = self.dout(name, shape, dt)
        t = self.dma(o, ap, reads, [("dbg", name)])
        self.finals.append(t)

    def finish(self):
        self.p.emit(self.st, self.finals)
        self.st.close()
        return self.nc


def host_consts():
    c = {}
    c["ident"] = np.eye(128, dtype=np.float32)
    c["ones_ln"] = np.full((128, 128), 1.0 / 512.0, np.float32)
    bo = np.zeros((128, 128), np.float32)
    bo[:64, :64] = 1.0
    bo[64:, 64:] = 1.0
    c["blk64"] = bo
    tp = np.arange(128)[:, None]
    t = np.arange(128)[None, :]
    same = (tp // 64) == (t // 64)
    m = {}
    m["f_incl"] = same & (tp <= t)
    m["f_excl"] = same & (tp < t)
    m["f_rest"] = same & (tp > t)
    m["b_incl"] = same & (tp >= t)
    m["b_excl"] = same & (tp > t)
    m["b_rest"] = same & (tp < t)
    c["cmask"] = np.concatenate([m[k].astype(np.float32) for k in
                                 ("f_excl", "f_incl", "f_rest", "b_excl", "b_incl", "b_rest")], 1)
    j = np.arange(64)[:, None]
    tt = np.arange(64)[None, :]
    im = np.concatenate([(j < tt), (j <= tt), (j > tt), (j >= tt)], 1).astype(np.float32)
    im2 = np.concatenate([im, im], 0)
    c["maskA0"] = np.tile(im2[:, 0:128], (1, 8))
    c["maskA1"] = np.tile(im2[:, 128:256], (1, 8))
    c["maskN0"] = np.tile(im2[:, 128:192], (1, 8))
    c["maskN1"] = np.tile(im2[:, 0:64], (1, 8))
    e = np.concatenate([np.eye(64, dtype=np.float32)] * 2, 0)
    c["eye8"] = np.tile(e, (1, 8))
    c["ones128"] = np.ones((128, 128), np.float32)
    c["gmask0"] = np.tile((same & (tp <= t)).astype(np.float32), (1, 4))
    c["gmask1"] = np.tile((same & (tp >= t)).astype(np.float32), (1, 4))
    return c


BF_CONSTS = ("maskA0", "maskA1", "maskN0", "maskN1", "eye8", "gmask0", "gmask1")


def build_xT(k, s, x_dram, w_in, C, srckey=None):
    T, TB, NTB, NT = k.T, k.TB, k.NTB, k.NT
    xT = k.xT
    ident = C["ident"]

    for tt_ in range(NT):
        xin, xk = k.r_xin.next()
        rd = [(srckey, s, tt_)] if srckey else []
        k.dma(xin[:, 0:1024], x_dram[s * T + tt_ * 128: s * T + (tt_ + 1) * 128, :], rd, [xk])
        for g in range(2):
            ps, pk = k.r_ps.next()
            for j in range(4):
                fc = g * 4 + j
                k.tr(ps[:, j * 128:(j + 1) * 128], xin[:, fc * 128:(fc + 1) * 128], ident[:, :],
                     [xk, "ident"], [pk], signal=(j == 3))
            dst = xT[:, g * 4:(g + 1) * 4, tt_ * 128:(tt_ + 1) * 128]
            src = ps[:, :].rearrange("p (a b) -> p a b", a=4)
            k.cp(dst, src, [pk], [("xT", tt_)], eng=("scalar" if g == 0 else "vector"))
    xT_keys = [("xT", i) for i in range(NT)]
    k.xT_keys = xT_keys

    def proj(c0, m, tb, wt, wk):
        ps, pk = k.r_ps.next()
        for kc in range(8):
            k.mm(ps[0:m, 0:TB], wt[:, kc, 0:m], xT[:, kc, tb * TB:(tb + 1) * TB],
                 kc == 0, kc == 7, [wk] + xT_keys, [pk])
        return ps, pk

    k.proj = proj

    def load_w(c0, m):
        wt, wk = (k.r_w128 if m <= 128 else k.r_w512).next()
        k.dma(wt[:, :, 0:m], w_in[:, c0:c0 + m].rearrange("(kc p) m -> p kc m", p=128), [], [wk], eng="gpsimd")
        return wt, wk

    k.load_w = load_w


def build_l0_front(k, s, x_dram, w_in, pv, PC, C):
    T, TB, NTB, NT = k.T, k.TB, k.NTB, k.NT
    proj, load_w = k.proj, k.load_w
    cvs = []
    for c in range(4):
        wv, wvk = load_w(c * 128, 128)
        wg, wgk = load_w(512 + c * 128, 128)
        ub, uk = k.big[4]
        u = ub.bitcast(BF16)
        k.memset(u[:, 0:15], 0.0, [uk], eng="vector")
        k.memset(u[:, 15 + T:T + 32], 0.0, [uk], eng="vector")
        dgb, dgk = k.big[5]
        dg = dgb.bitcast(BF16)
        wc = PC[f"convw{c}"]
        for j in range(31):
            k.ts(dg[:, j * 128:(j + 1) * 128], k.identb[:, :], pv[:, wc + j:wc + j + 1], None, ALU.mult, None,
                 ["identb", "pv"], [dgk], eng="vector")
        for tb in range(NTB):
            psv, pvk = proj(c * 128, 128, tb, wv, wvk)
            psg, pgk = proj(512 + c * 128, 128, tb, wg, wgk)
            sg, sgk = k.r_tmp.next()
            k.act(sg[:, 0:TB], psg[:, 0:TB], AF.Sigmoid, [pgk], [sgk])
            k.tt(u[:, 15 + tb * TB:15 + (tb + 1) * TB], psv[:, 0:TB], sg[:, 0:TB], ALU.mult, [pvk, sgk], [uk])
        a0, a0k = k.big[c]
        for tb in range(NTB):
            ps, pk = k.r_ps.next()
            for j in range(31):
                k.mm(ps[:, 0:TB], dg[:, j * 128:(j + 1) * 128], u[:, tb * TB + j:tb * TB + j + TB],
                     j == 0, j == 30, [dgk, uk], [pk])
            k.act(a0[:, tb * TB:(tb + 1) * TB], ps[:, 0:TB], AF.Identity, [pk, "pv"], [a0k],
                  bias=pv[:, PC["convb"] + c:PC["convb"] + c + 1])
        cvs.append((a0, a0k))
    for tb in range(NTB):
        sl = slice(tb * TB, (tb + 1) * TB)
        psm, pmk = k.r_ps.next()
        for c in range(4):
            k.mm(psm[:, 0:TB], C["ones_ln"][:, :], cvs[c][0][:, sl], c == 0, c == 3, ["ones_ln", cvs[c][1]], [pmk])
        psq, pqk = k.r_ps.next()
        for c in range(4):
            sq, sqk = k.r_tmp.next()
            k.act(sq[:, 0:TB], cvs[c][0][:, sl], AF.Square, [cvs[c][1]], [sqk])
            k.mm(psq[:, 0:TB], C["ones_ln"][:, :], sq[:, 0:TB], c == 0, c == 3, ["ones_ln", sqk], [pqk])
        msq, msqk = k.r_tmp.next()
        k.act(msq[:, 0:TB], psm[:, 0:TB], AF.Square, [pmk], [msqk])
        var, vark = k.r_tmp.next()
        k.tt(var[:, 0:TB], psq[:, 0:TB], msq[:, 0:TB], ALU.subtract, [pqk, msqk], [vark])
        sd, sdk = k.r_tmp.next()
        k.act(sd[:, 0:TB], var[:, 0:TB], AF.Ln, [vark, "eps"], [sdk], bias=k.eps_ln[:, 0:1])
        rstd, rsk = k.r_keep.next()
        k.act(rstd[:, 0:TB], sd[:, 0:TB], AF.Exp, [sdk], [rsk], scale=-0.5)
        if k.dbg == "conv" and tb == 0:
            mm_, mmk = k.r_tmp.next()
            k.cp(mm_[:, 0:TB], psm[:, 0:TB], [pmk], [mmk])
            k.dump(f"dbg_mean{s}", mm_[:, 0:TB], [128, TB], [mmk])
            k.dump(f"dbg_var{s}", var[:, 0:TB], [128, TB], [vark])
            k.dump(f"dbg_rstd{s}", rstd[:, 0:TB], [128, TB], [rsk])
        for c in range(4):
            t1, t1k = k.r_tmp.next()
            k.tt(t1[:, 0:TB], cvs[c][0][:, sl], psm[:, 0:TB], ALU.subtract, [cvs[c][1], pmk], [t1k])
            t2, t2k = k.r_tmp.next()
            k.tt(t2[:, 0:TB], t1[:, 0:TB], rstd[:, 0:TB], ALU.mult, [t1k, rsk], [t2k])
            t3, t3k = k.r_tmp.next()
            k.ts(t3[:, 0:TB], t2[:, 0:TB], pv[:, PC["clng"] + c:PC["clng"] + c + 1],
                 pv[:, PC["clnb"] + c:PC["clnb"] + c + 1], ALU.mult, ALU.add, [t2k, "pv"], [t3k])
            s1, s1k = k.r_tmp.next()
            k.act(s1[:, 0:TB], t3[:, 0:TB], AF.Silu, [t3k], [s1k])
            if k.dbg == "conv" and tb == 0 and c == 0:
                k.dump(f"dbg_t3{s}", t3[:, 0:TB], [128, TB], [t3k])
                k.dump(f"dbg_s1{s}", s1[:, 0:TB], [128, TB], [s1k])
            if tb == 0:
                k.wga[c] = load_w(1024 + c * 128, 128)
            wt, wk = k.wga[c]
            psa, pak = proj(1024 + c * 128, 128, tb, wt, wk)
            sga, sgak = k.r_tmp.next()
            k.act(sga[:, 0:TB], psa[:, 0:TB], AF.Silu, [pak], [sgak])
            k.emit_y(s, c, tb, s1, sga, [s1k, sgak])


def pack_l0_cols():
    dummy = {
        "l0_conv_w": np.zeros((31, 512), np.float32), "l0_conv_b": np.zeros(512, np.float32),
        "l0_conv_ln_g": np.zeros(512, np.float32), "l0_conv_ln_b": np.zeros(512, np.float32),
        "l0_mu_shift": np.zeros(1728, np.float32), "l0_a0": np.zeros((2, 512), np.float32),
        "l0_k_k": np.zeros(512, np.float32), "l0_k_a": np.zeros(512, np.float32),
        "l0_r_k": np.zeros((8, 64), np.float32), "l0_gn_g": np.zeros(512, np.float32),
        "l0_gn_b": np.zeros(512, np.float32),
    }
    arr, cols = pack_l0(dummy)
    cols = dict(cols)
    cols["_n"] = arr.shape[1]
    return cols


def shifted_proj(k, wt, wk, m, mu_ap, out, outk, zb, sb_):
    T, TB, NTB = k.T, k.TB, k.NTB
    z, zk = zb
    s_, sk = sb_
    k.memset(z[0:m, 0:1], 0.0, [zk])
    k.memset(z[0:m, T + 1:T + 2], 0.0, [zk])
    for tb in range(NTB):
        ps, pk = k.r_ps.next()
        for kc in range(8):
            k.mm(ps[0:m, 0:TB], wt[:, kc, 0:m], k.xT[:, kc, tb * TB:(tb + 1) * TB],
                 kc == 0, kc == 7, [wk] + k.xT_keys, [pk])
        k.cp(z[0:m, 1 + tb * TB:1 + (tb + 1) * TB], ps[0:m, 0:TB], [pk], [zk], eng="scalar")
    k.tt(s_[0:m, 0:T], z[0:m, 0:T], z[0:m, 2:T + 2], ALU.add, [zk], [sk])
    k.stt(s_[0:m, 0:T], s_[0:m, 0:T], 0.5, z[0:m, 1:T + 1], ALU.mult, ALU.subtract, [sk, zk], [sk])
    k.stt(out[0:m, 0:T], s_[0:m, 0:T], mu_ap, z[0:m, 1:T + 1], ALU.mult, ALU.add, [sk, zk, "pv"], [outk])


def rwkv_stream(k, c, d, R_, C, T_, written_y, written_k):
    T, TB, NTB, NT, NCH = k.T, k.TB, k.NTB, k.NT, k.NCH
    CPB = TB // 64
    pv, PC = T_["pv"], T_["PC"]

    def pcol(name, i=0):
        return pv[:, PC[name] + i:PC[name] + i + 1]

    (rT, rk_), (kT, kk_), (vT, vk_), (kkt, kkk) = T_["rT"], T_["kT"], T_["vT"], T_["kkt"]
    (kms, kmsk), (yac, yack) = T_["kms"], T_["yac"]
    (tw, twk), (als, alk) = T_["tw"], T_["als"]
    S32, Sbf, S32key, Sbfkey = R_["S32"], R_["Sbf"], R_["S32key"], R_["Sbfkey"]
    if True:
        cur = 0
        k.memset(S32[cur], 0.0, [S32key[cur]], eng="vector")
        k.memset(Sbf[cur], 0.0, [Sbfkey[cur]], eng="vector")
        blocks = range(NTB) if d == 0 else range(NTB - 1, -1, -1)
        for tb in blocks:
            sl = slice(tb * TB, (tb + 1) * TB)
            psa, pak = k.r_ps.next()
            k.mm(psa[:, 0:TB], k.aup[d * 32:(d + 1) * 32, c * 128:(c + 1) * 128], als[d * 32:(d + 1) * 32, sl],
                 True, True, ["aup", alk], [pak])
            a_, ak_ = k.r_tmp.next()
            k.act(a_[:, 0:TB], psa[:, 0:TB], AF.Tanh, [pak, "ha0"], [ak_], bias=k.ha0[:, d * 4 + c:d * 4 + c + 1],
                  scale=0.5)
            ka, kak = k.r_tmp.next()
            k.ts(ka[:, 0:TB], a_[:, 0:TB], k.hka[:, c:c + 1], k.omk[:, c:c + 1], ALU.mult, ALU.add,
                 [ak_, "hka", "omk"], [kak])
            km, kmk = k.r_tmp.next()
            k.tt(km[:, 0:TB], kT[:, sl], ka[:, 0:TB], ALU.mult, [kk_, kak], [kmk])
            kmkey = (kmsk, tb)
            if tb not in written_k:
                written_k.add(tb)
                k.cp(kms[:, sl], km[:, 0:TB], [kmk], [kmkey], eng="vector")
            else:
                k.tt(kms[:, sl], kms[:, sl], km[:, 0:TB], ALU.add, [kmkey, kmk], [kmkey], eng="vector")
            be, bek = k.r_tmp.next()
            k.stt(be[:, 0:TB], a_[:, 0:TB], 1.0, kkt[:, sl], ALU.add, ALU.mult, [ak_, kkk], [bek])
            psw, pwk = k.r_ps.next()
            for i in range(TB // 128):
                tcol = slice(tb * TB + i * 128, tb * TB + (i + 1) * 128)
                k.mm(psw[:, i * 128:(i + 1) * 128], tw[d * 64:(d + 1) * 64, tcol],
                     k.wup[d * 64:(d + 1) * 64, c * 128:(c + 1) * 128], True, False, [twk, "wup"], [pwk])
                k.mm(psw[:, i * 128:(i + 1) * 128], C["ones128"][d * 64:d * 64 + 1, :],
                     k.w0x[d * 64:d * 64 + 1, c * 128:(c + 1) * 128], False, True,
                     ["ones128", "w0x"], [pwk])
            sig, sigk = k.r_tmp.next()
            k.act(sig[:, 0:TB], psw[:, 0:TB], AF.Tanh, [pwk], [sigk], scale=0.5)
            k.ts(sig[:, 0:TB], sig[:, 0:TB], 1.0, None, ALU.add, None, [sigk], [sigk])
            pcs = []
            for x in range(3):
                pc_, pck = k.r_ps.next()
                for i in range(TB // 128):
                    k.mm(pc_[:, i * 128:(i + 1) * 128], sig[:, i * 128:(i + 1) * 128],
                         C["cmask"][:, (d * 3 + x) * 128:(d * 3 + x + 1) * 128], True, True,
                         [sigk, "cmask"], [pck])
                pcs.append((pc_, pck))
            Ge, Gek = k.r_tmp.next()
            k.act(Ge[:, 0:TB], pcs[0][0][:, 0:TB], AF.Exp, [pcs[0][1]], [Gek], scale=0.5 * CW)
            Gi, Gik = k.r_tmp.next()
            k.act(Gi[:, 0:TB], pcs[1][0][:, 0:TB], AF.Exp, [pcs[1][1]], [Gik], scale=0.5 * CW)
            Gn, Gnk = k.r_tmp.next()
            k.act(Gn[:, 0:TB], pcs[1][0][:, 0:TB], AF.Exp, [pcs[1][1]], [Gnk], scale=-0.5 * CW)
            Gr, Grk = k.r_tmp.next()
            k.act(Gr[:, 0:TB], pcs[2][0][:, 0:TB], AF.Exp, [pcs[2][1]], [Grk], scale=0.5 * CW)
            gC, gCk = R_['gC']
            lastcol = 63 if d == 0 else 0
            k.act(gC[:, 0:CPB], pcs[1][0][:, 0:TB].rearrange("p (a b) -> p a b", b=64)[:, :, lastcol],
                  AF.Exp, [pcs[1][1]], [gCk], scale=0.5 * CW)
            ar, ark = R_['ar']
            arv = ar[:, 0:CPB, :]
            k.stt(arv[:, :, 0:64], kkt[:, sl].rearrange("p (a b) -> p a b", b=64), -1.0,
                  Ge[:, 0:TB].rearrange("p (a b) -> p a b", b=64), ALU.mult, ALU.mult, [kkk, Gek], [ark])
            k.tt(arv[:, :, 64:128], rT[:, sl].rearrange("p (a b) -> p a b", b=64),
                 Gi[:, 0:TB].rearrange("p (a b) -> p a b", b=64), ALU.mult, [rk_, Gik], [ark])
            bt, btk = R_['slots'][0]
            k.stt(bt[:, 0:TB], be[:, 0:TB], 0.5, Gn[:, 0:TB], ALU.mult, ALU.mult, [bek, Gnk], [btk])
            kt, ktk = R_['slots'][1]
            k.tt(kt[:, 0:TB], km[:, 0:TB], Gn[:, 0:TB], ALU.mult, [kmk, Gnk], [ktk])
            bg, bgk = R_['trans'].next()
            k.stt(bg[:, 0:TB], be[:, 0:TB], 0.5, Gr[:, 0:TB], ALU.mult, ALU.mult, [bek, Grk], [bgk])
            kg, kgk = R_['trans'].next()
            k.tt(kg[:, 0:TB], km[:, 0:TB], Gr[:, 0:TB], ALU.mult, [kmk, Grk], [kgk])
            toks = []
            for ti_, (src, srck) in enumerate(((bg, bgk), (kg, kgk))):
                pt_, ptk = k.r_ps.next()
                pt = pt_.bitcast(BF16)
                for ci in range(CPB):
                    for hh in range(2):
                        hb = hh * 64
                        k.tr(pt[hb:hb + 64, ci * 64:(ci + 1) * 64], src[hb:hb + 64, ci * 64:(ci + 1) * 64],
                             k.identb[hb:hb + 64, hb:hb + 64], [srck, "identb"], [ptk],
                             signal=(ci == CPB - 1 and hh == 1))
                tk_, tkk = R_['slots'][3 + ti_]
                k.cp(tk_[:, 0:TB], pt[:, 0:TB], [ptk], [tkk], eng="scalar")
                toks.append((tk_, tkk))
            (tokB, tBk), (tokK, tKk) = toks

            def tv(ci_):
                g_ = tb * CPB + ci_
                t_, tk2 = T_["tokV"][g_ // 16]
                return t_[:, (g_ % 16) * 64:(g_ % 16 + 1) * 64], tk2
            yield
            p1a, p1ak = k.r_ps.next()
            p1b, p1bk = k.r_ps.next()
            p2a, p2ak = k.r_ps.next()
            p2b, p2bk = k.r_ps.next()
            p3, p3k = k.r_ps.next()
            HC = max(CPB // 2, 1)
            for ci in range(CPB):
                cc = slice(ci * 64, (ci + 1) * 64)
                P1, P1k = (p1a, p1ak) if ci < HC else (p1b, p1bk)
                P2, P2k = (p2a, p2ak) if ci < HC else (p2b, p2bk)
                o = (ci % HC) * 128
                last = (ci == CPB - 1) or (ci == HC - 1)
                for hh in range(2):
                    hb = hh * 64
                    sg_ = last and hh == 1
                    k.mm(P1[hb:hb + 64, o:o + 128], bt[hb:hb + 64, cc], ar[hb:hb + 64, ci, :], True, True, [btk, ark], [P1k], signal=sg_)
                    k.mm(P2[hb:hb + 64, o:o + 128], kt[hb:hb + 64, cc], ar[hb:hb + 64, ci, :], True, True, [ktk, ark], [P2k], signal=sg_)
                    k.mm(p3[hb:hb + 64, cc], ar[hb:hb + 64, ci, 0:64], bt[hb:hb + 64, cc], True, True, [btk, ark], [p3k], signal=(ci == CPB - 1 and hh == 1))
            A1, A1k = R_['A'][0]
            A2, A2k = R_['A'][1]
            HW_ = HC * 128
            mA = k.maskA[d]
            k.tt(A1[:, 0:HW_], p1a[:, 0:HW_], mA[:, 0:HW_], ALU.mult, [p1ak, "maskA"], [A1k])
            k.tt(A2[:, 0:HW_], p2a[:, 0:HW_], mA[:, 0:HW_], ALU.mult, [p2ak, "maskA"], [A2k])
            if CPB > 1:
                k.tt(A1[:, HW_:2 * HW_], p1b[:, 0:HW_], mA[:, 0:HW_], ALU.mult, [p1bk, "maskA"], [A1k])
                k.tt(A2[:, HW_:2 * HW_], p2b[:, 0:HW_], mA[:, 0:HW_], ALU.mult, [p2bk, "maskA"], [A2k])
            A1v = A1[:, 0:CPB * 128].rearrange("p (a b) -> p a b", b=128)
            A2v = A2[:, 0:CPB * 128].rearrange("p (a b) -> p a b", b=128)
            W_ = CPB * 64
            qs = R_['q']
            (Q, Qk), (QT, QTk), (R, Rk) = qs[0], qs[1], qs[2]
            yield
            k.tt(QT[:, 0:W_], p3[:, 0:W_], k.maskN[d][:, 0:W_], ALU.mult, [p3k, "maskN"], [QTk])
            k.tt(R[:, 0:W_].rearrange("p (a b) -> p a b", b=64), A1v[:, :, 0:64],
                 k.eye8[:, 0:W_].rearrange("p (a b) -> p a b", b=64), ALU.add, [A1k, "eye8"], [Rk])
            for lvl in range(5):
                pq, pqk = k.r_ps.next()
                pqt, pqtk = k.r_ps.next()
                need_q = lvl < 4
                for ci in range(CPB):
                    cc = slice(ci * 64, (ci + 1) * 64)
                    for hh in range(2):
                        hb = hh * 64
                        lastm = (ci == CPB - 1 and hh == 1)
                        Qop = A1v[hb:hb + 64, ci, 0:64] if lvl == 0 else Q[hb:hb + 64, cc]
                        Qopk = A1k if lvl == 0 else Qk
                        if need_q:
                            k.mm(pq[hb:hb + 64, cc], QT[hb:hb + 64, cc], Qop, True, True, [Qopk, QTk], [pqk], signal=lastm)
                        k.mm(pqt[hb:hb + 64, cc], Qop, QT[hb:hb + 64, cc], True, True, [Qopk, QTk], [pqtk], signal=lastm)
                o_ = 3 if lvl % 2 == 0 else 0
                (Q2, Q2k), (QT2, QT2k), (R2, R2k) = qs[o_], qs[o_ + 1], qs[o_ + 2]
                if need_q:
                    k.cp(Q2[:, 0:W_], pq[:, 0:W_], [pqk], [Q2k], eng="scalar")
                k.cp(QT2[:, 0:W_], pqt[:, 0:W_], [pqtk], [QT2k], eng="vector")
                yield
                pr, prk = k.r_ps.next()
                for ci in range(CPB):
                    cc = slice(ci * 64, (ci + 1) * 64)
                    for hh in range(2):
                        hb = hh * 64
                        k.mm(pr[hb:hb + 64, cc], QT2[hb:hb + 64, cc], R[hb:hb + 64, cc], True, True, [QT2k, Rk], [prk], signal=(ci == CPB - 1 and hh == 1))
                k.tt(R2[:, 0:W_], pr[:, 0:W_], R[:, 0:W_], ALU.add, [prk, Rk], [R2k])
                Q, Qk, QT, QTk, R, Rk = Q2, Q2k, QT2, QT2k, R2, R2k
                yield
            order = range(CPB) if d == 0 else range(CPB - 1, -1, -1)
            for ci in order:
                cc = slice(ci * 64, (ci + 1) * 64)
                gcol = tb * TB + ci * 64
                pw, pwk_ = k.r_ps.next()
                for hh in range(2):
                    hb = hh * 64
                    k.mm(pw[hb:hb + 64, 0:64], A2v[hb:hb + 64, ci, 0:64], tv(ci)[0][hb:hb + 64, :], True, False, [A2k, tv(ci)[1]], [pwk_], signal=False)
                    k.mm(pw[hb:hb + 64, 0:64], ar[hb:hb + 64, ci, 0:64], Sbf[cur][hb:hb + 64, :], False, True, [ark, Sbfkey[cur]], [pwk_], signal=(hh == 1))
                Wsb, Wk = R_['ch'][0]
                k.cp(Wsb[:, 0:64], pw[:, 0:64], [pwk_], [Wk], eng="scalar")
                yield
                pu, puk = k.r_ps.next()
                for hh in range(2):
                    hb = hh * 64
                    k.mm(pu[hb:hb + 64, 0:64], R[hb:hb + 64, cc], Wsb[hb:hb + 64, 0:64], True, True, [Rk, Wk], [puk], signal=(hh == 1))
                Usb, Uk = R_['ch'][1]
                k.cp(Usb[:, 0:64], pu[:, 0:64], [puk], [Uk], eng="vector")
                yield
                py, pyk = k.r_ps.next()
                pS, pSk = k.r_ps.next()
                for hh in range(2):
                    hb = hh * 64
                    k.mm(py[hb:hb + 64, 0:64], Sbf[cur][hb:hb + 64, :], ar[hb:hb + 64, ci, 64:128], True, False, [Sbfkey[cur], ark], [pyk], signal=False)
                    k.mm(py[hb:hb + 64, 0:64], Usb[hb:hb + 64, 0:64], A1v[hb:hb + 64, ci, 64:128], False, False, [Uk, A1k], [pyk], signal=False)
                    k.mm(py[hb:hb + 64, 0:64], tv(ci)[0][hb:hb + 64, :], A2v[hb:hb + 64, ci, 64:128], False, True, [tv(ci)[1], A2k], [pyk], signal=(hh == 1))
                for hh in range(2):
                    hb = hh * 64
                    k.mm(pS[hb:hb + 64, 0:64], tokB[hb:hb + 64, cc], Usb[hb:hb + 64, 0:64], True, False, [tBk, Uk], [pSk], signal=False)
                    k.mm(pS[hb:hb + 64, 0:64], tokK[hb:hb + 64, cc], tv(ci)[0][hb:hb + 64, :], False, True, [tKk, tv(ci)[1]], [pSk], signal=(hh == 1))
                nxt = 1 - cur
                k.stt(S32[nxt], S32[cur], gC[:, ci:ci + 1], pS[:, 0:64], ALU.mult, ALU.add,
                      [S32key[cur], gCk, pSk], [S32key[nxt]])
                k.cp(Sbf[nxt], S32[nxt], [S32key[nxt]], [Sbfkey[nxt]], eng="scalar")
                cur = nxt
                ykey = (yack, gcol // 64)
                if gcol not in written_y:
                    written_y.add(gcol)
                    k.cp(yac[:, gcol:gcol + 64], py[:, 0:64], [pyk], [ykey], eng="vector")
                else:
                    k.tt(yac[:, gcol:gcol + 64], yac[:, gcol:gcol + 64], py[:, 0:64], ALU.add,
                         [ykey, pyk], [ykey])
                yield


def build_l0_rwkv(k, s, w_in, pv, PC, C):
    T, TB, NTB, NT, NCH = k.T, k.TB, k.NTB, k.NT, k.NCH
    CPB = TB // 64

    def pcol(name, i=0):
        return pv[:, PC[name] + i:PC[name] + i + 1]

    wt, wk = k.r_w128.next()
    k.dma(wt[:, :, 0:64], w_in[:, 3072:3136].rearrange("(kc p) m -> p kc m", p=128), [], [wk], eng="gpsimd")
    k.dma(wt[:, :, 64:128], w_in[:, 3168:3232].rearrange("(kc p) m -> p kc m", p=128), [], [wk], eng="gpsimd")
    tw, twk = k.big[6]
    shifted_proj(k, wt, wk, 128, pcol("mu_wl"), tw, twk, k.big[4], k.big[5])
    k.act(tw[:, 0:T], tw[:, 0:T], AF.Tanh, [twk], [twk])
    wt, wk = k.r_w128.next()
    k.dma(wt[:, :, 0:32], w_in[:, 3136:3168].rearrange("(kc p) m -> p kc m", p=128), [], [wk], eng="gpsimd")
    k.dma(wt[:, :, 32:64], w_in[:, 3232:3264].rearrange("(kc p) m -> p kc m", p=128), [], [wk], eng="gpsimd")
    als, alk = k.big[7]
    shifted_proj(k, wt, wk, 64, pv[0:64, PC["mu_al"]:PC["mu_al"] + 1], als, alk, k.big[4], k.big[5])

    for c in range(4):
        rT, rk_ = k.big[0]
        kT, kk_ = k.big[1]
        vT, vk_ = k.big[2]
        kkt, kkk = k.big[3]
        for (dst, dk_, col0, mui) in ((rT, rk_, 1536, c), (kT, kk_, 2048, 4 + c), (vT, vk_, 2560, 8 + c)):
            wt, wk = k.load_w(col0 + c * 128, 128)
            shifted_proj(k, wt, wk, 128, pcol("mu_rkv", mui), dst, dk_, k.big[4], k.big[5])
        k.ts(kkt[:, 0:T], kT[:, 0:T], pcol("k_k", c), None, ALU.mult, None, [kk_, "pv"], [kkk])
        for tb in range(NTB):
            sl = slice(tb * TB, (tb + 1) * TB)
            sq, sqk = k.r_tmp.next()
            k.act(sq[:, 0:TB], kkt[:, sl], AF.Square, [kkk], [sqk])
            ps, pk = k.r_ps.next()
            k.mm(ps[:, 0:TB], C["blk64"][:, :], sq[:, 0:TB], True, True, ["blk64", sqk], [pk])
            sd, sdk = k.r_tmp.next()
            k.act(sd[:, 0:TB], ps[:, 0:TB], AF.Ln, [pk, "eps"], [sdk], bias=k.eps_kk[:, 0:1])
            rn, rnk = k.r_tmp.next()
            k.act(rn[:, 0:TB], sd[:, 0:TB], AF.Exp, [sdk], [rnk], scale=-0.5)
            k.tt(kkt[:, sl], kkt[:, sl], rn[:, 0:TB], ALU.mult, [kkk, rnk], [kkk])
        kms, kmsk = k.big[4]
        yac, yack = k.big[5]
        bfq = k.r_sc.bufs + k.r_sc2.bufs + k.r_q.bufs + k.r_rf.bufs
        trans = MiniRing(bfq[22:25])
        T_ = {"pv": pv, "PC": PC, "rT": (rT, rk_), "kT": (kT, kk_), "vT": (vT, vk_), "kkt": (kkt, kkk),
              "kms": (kms, kmsk), "yac": (yac, yack), "tw": (tw, twk), "als": (als, alk)}
        tokVall = [(k.r_keep.bufs[i][0].bitcast(BF16), k.r_keep.bufs[i][1]) for i in range(2)]
        T_["tokV"] = tokVall
        for tb in range(NTB):
            vb, vbk = trans.next()
            k.cp(vb[:, 0:TB], vT[:, tb * TB:(tb + 1) * TB], [vk_], [vbk], eng="vector")
            pt_, ptk = k.r_ps.next()
            pt = pt_.bitcast(BF16)
            for ci in range(CPB):
                for hh in range(2):
                    hb = hh * 64
                    k.tr(pt[hb:hb + 64, ci * 64:(ci + 1) * 64], vb[hb:hb + 64, ci * 64:(ci + 1) * 64],
                         k.identb[hb:hb + 64, hb:hb + 64], [vbk, "identb"], [ptk],
                         signal=(ci == CPB - 1 and hh == 1))
            g0 = tb * CPB
            tv_, tvk = tokVall[g0 // 16]
            k.cp(tv_[:, (g0 % 16) * 64:(g0 % 16) * 64 + TB], pt[:, 0:TB], [ptk], [tvk], eng="scalar")
        finek = [(kmsk, j) for j in range(NTB)] + [(yack, j) for j in range(NCH)]
        k.fence([kmsk, yack], finek + [kmsk, yack])
        written_y, written_k = set(), set()
        gens = []
        for d in range(2):
            R_ = {"slots": bfq[11 * d:11 * d + 5], "q": bfq[11 * d + 5:11 * d + 11], "trans": trans,
                  "A": k.r_am.bufs[2 * d:2 * d + 2], "ar": k.r_ar.bufs[d], "gC": k.r_gc.bufs[d],
                  "ch": k.r_ch.bufs[2 * d:2 * d + 2],
                  "S32": [k.S32[:, 2 * d, :], k.S32[:, 2 * d + 1, :]],
                  "Sbf": [k.Sbf[:, 2 * d, :], k.Sbf[:, 2 * d + 1, :]],
                  "S32key": [("S32", 2 * d), ("S32", 2 * d + 1)], "Sbfkey": [("Sbf", 2 * d), ("Sbf", 2 * d + 1)]}
            gens.append(rwkv_stream(k, c, d, R_, C, T_, written_y, written_k))
        interleave(gens, stagger=RWKV_STAGGER)
        k.fence(finek + [kmsk, yack], [kmsk, yack])
        if k.dbg == "rwkv":
            k.dump(f"dbg_y{s}_{c}", yac[:, 0:T], [128, T], [yack])
            k.dump(f"dbg_kk{s}_{c}", kkt[:, 0:T], [128, T], [kkk])
            k.dump(f"dbg_r{s}_{c}", rT[:, 0:T], [128, T], [rk_])
        wt, wk = k.load_w(3264 + c * 128, 128)
        for tb in range(NTB):
            sl = slice(tb * TB, (tb + 1) * TB)
            t0, t0k = k.r_tmp.next()
            k.stt(t0[:, 0:TB], kms[:, sl], k.rkh[:, c:c + 1], rT[:, sl], ALU.mult, ALU.mult,
                  [kmsk, "rkh", rk_], [t0k])
            psb, pbk = k.r_ps.next()
            k.mm(psb[:, 0:TB], C["blk64"][:, :], t0[:, 0:TB], True, True, ["blk64", t0k], [pbk])
            bon, bonk = k.r_keep.next()
            k.tt(bon[:, 0:TB], psb[:, 0:TB], vT[:, sl], ALU.mult, [pbk, vk_], [bonk])
            psm, pmk = k.r_ps.next()
            k.mm(psm[:, 0:TB], k.blk64s[:, :], yac[:, sl], True, True, ["blk64s", yack], [pmk])
            sq, sqk = k.r_tmp.next()
            k.act(sq[:, 0:TB], yac[:, sl], AF.Square, [yack], [sqk])
            psq, pqk_ = k.r_ps.next()
            k.mm(psq[:, 0:TB], k.blk64s[:, :], sq[:, 0:TB], True, True, ["blk64s", sqk], [pqk_])
            msq, msqk = k.r_tmp.next()
            k.act(msq[:, 0:TB], psm[:, 0:TB], AF.Square, [pmk], [msqk])
            var, vark = k.r_tmp.next()
            k.tt(var[:, 0:TB], psq[:, 0:TB], msq[:, 0:TB], ALU.subtract, [pqk_, msqk], [vark])
            sd, sdk = k.r_tmp.next()
            k.act(sd[:, 0:TB], var[:, 0:TB], AF.Ln, [vark, "eps"], [sdk], bias=k.eps_gn[:, 0:1])
            rstd, rsk = k.r_tmp.next()
            k.act(rstd[:, 0:TB], sd[:, 0:TB], AF.Exp, [sdk], [rsk], scale=-0.5)
            t1, t1k = k.r_tmp.next()
            k.tt(t1[:, 0:TB], yac[:, sl], psm[:, 0:TB], ALU.subtract, [yack, pmk], [t1k])
            t2, t2k = k.r_tmp.next()
            k.tt(t2[:, 0:TB], t1[:, 0:TB], rstd[:, 0:TB], ALU.mult, [t1k, rsk], [t2k])
            t3, t3k = k.r_tmp.next()
            k.ts(t3[:, 0:TB], t2[:, 0:TB], pcol("gn_g", c), pcol("gn_b", c), ALU.mult, ALU.add, [t2k, "pv"], [t3k])
            t4, t4k = k.r_tmp.next()
            k.tt(t4[:, 0:TB], t3[:, 0:TB], bon[:, 0:TB], ALU.add, [t3k, bonk], [t4k])
            psg, pgk = k.proj(3264 + c * 128, 128, tb, wt, wk)
            sg, sgk = k.r_tmp.next()
            k.act(sg[:, 0:TB], psg[:, 0:TB], AF.Silu, [pgk], [sgk])
            k.emit_y(s, 4 + c, tb, t4, sg, [t4k, sgk])


def emit_y(k, s, c, tb, a, b, keys):
    TB = k.TB
    yb, ybk = k.r_sc2.next()
    k.tt(yb[:, 0:TB], a[:, 0:TB], b[:, 0:TB], ALU.mult, keys, [ybk])
    row0 = (s * 8 + c) * 128
    k.dma(k.yscr[row0:row0 + 128, tb * TB:(tb + 1) * TB], yb[:, 0:TB], [ybk], [("yscr", s, c, tb)])


K.emit_y = emit_y


def out_tile_stream(k, s, tt_, slot, x_src, dst, dkey, srckey, wo, gb, gbk, ncy):
    T = k.T
    xT = k.xT
    tok0 = s * T + tt_ * 128
    tcol = slice(tt_ * 128, (tt_ + 1) * 128)
    xb, xk0 = k.big[4 + slot // 2]
    xin = xb[:, (slot % 2) * 1024:(slot % 2 + 1) * 1024]
    xk = (xk0, slot % 2)
    rbb, rk0 = k.big[6 + slot // 2]
    rb = rbb[:, (slot % 2) * 1024:(slot % 2 + 1) * 1024]
    rbk = (rk0, slot % 2)
    st_, stk = (k.r_st.bufs + k.r_gc.bufs)[slot]
    k.dma(xin, x_src[tok0:tok0 + 128, :], [(srckey, s, tt_)] if srckey else [], [xk])
    for h in range(2):
        ps, pk = k.r_ps.next()
        for c in range(ncy):
            k.mm(ps[:, 0:512], xT[:, c, tcol], wo[h][0][:, c * 512:(c + 1) * 512], c == 0, c == ncy - 1,
                 [("xT", tt_, c), wo[h][1]], [pk])
        k.stt(rb[:, h * 512:(h + 1) * 512], xin[:, h * 512:(h + 1) * 512], ALPHA, ps[:, 0:512],
              ALU.mult, ALU.add, [xk, pk], [rbk])
    yield
    junk, jk = k.r_tmp.next()
    k.p.op("scalar", lambda e, junk=junk, rb=rb, st_=st_: e.activation(
        out=junk[:, 0:512], in_=rb[:, 0:512], func=AF.Identity, accum_out=st_[:, 0:1]), [rbk], [jk, stk])
    k.p.op("scalar", lambda e, junk=junk, rb=rb, st_=st_: e.activation(
        out=junk[:, 0:512], in_=rb[:, 512:1024], func=AF.Identity, accum_out=st_[:, 1:2]), [rbk], [jk, stk])
    k.p.op("scalar", lambda e, junk=junk, rb=rb, st_=st_: e.activation(
        out=junk[:, 0:512], in_=rb[:, 0:512], func=AF.Square, accum_out=st_[:, 2:3]), [rbk], [jk, stk])
    k.p.op("scalar", lambda e, junk=junk, rb=rb, st_=st_: e.activation(
        out=junk[:, 0:512], in_=rb[:, 512:1024], func=AF.Square, accum_out=st_[:, 3:4]), [rbk], [jk, stk])
    yield
    sk = [stk]
    k.tt(st_[:, 4:5], st_[:, 0:1], st_[:, 1:2], ALU.add, sk, sk)
    k.tt(st_[:, 5:6], st_[:, 2:3], st_[:, 3:4], ALU.add, sk, sk)
    k.ts(st_[:, 4:6], st_[:, 4:6], 1.0 / 1024.0, None, ALU.mult, None, sk, sk)
    k.tt(st_[:, 6:7], st_[:, 4:5], st_[:, 4:5], ALU.mult, sk, sk)
    k.tt(st_[:, 5:6], st_[:, 5:6], st_[:, 6:7], ALU.subtract, sk, sk)
    k.act(st_[:, 6:7], st_[:, 5:6], AF.Sqrt, sk + ["eps"], sk, bias=k.eps_ln[:, 0:1])
    k.recip(st_[:, 6:7], st_[:, 6:7], sk, sk)
    k.stt(st_[:, 7:8], st_[:, 4:5], -1.0, st_[:, 6:7], ALU.mult, ALU.mult, sk, sk)
    yield
    k.p.op("scalar", lambda e, rb=rb, st_=st_: e.activation(
        out=rb, in_=rb, func=AF.Identity, bias=st_[:, 7:8], scale=st_[:, 6:7]), [rbk] + sk, [rbk])
    k.tt(rb, rb, gb[:, 0:1024], ALU.mult, [rbk, gbk], [rbk])
    k.tt(rb, rb, gb[:, 1024:2048], ALU.add, [rbk, gbk], [rbk])
    t = k.dma(dst[tok0:tok0 + 128, :], rb, [rbk], [(dkey, s, tt_)])
    k.last_out.append(t)
    yield


def build_out_ln(k, s, x_src, li, gb_dram, dst, dkey, srckey=None, ncy=8):
    T, TB, NTB, NT = k.T, k.TB, k.NTB, k.NT
    wo = []
    for h in range(2):
        bt_, bk = k.big[1 + h]
        wv = bt_.bitcast(BF16)
        k.dma(wv[:, 0:4096].rearrange("p (c m) -> p c m", c=8),
              k.wobf[li][:, h * 512:(h + 1) * 512].rearrange("(c p) m -> p c m", p=128),
              [("wobf", li, q_) for q_ in range(4)], [bk])
        wo.append((wv, bk))
    gb, gbk = k.big[3]
    k.dma(gb[:, 0:2048], gb_dram.partition_broadcast(128), [], [gbk])
    fine = [(("big", i), j) for i in range(4, 8) for j in range(2)]
    coarse = [("big", i) for i in range(4, 8)]
    k.fence(coarse, fine + coarse)
    xkeys = [("xT", i) for i in range(NT)]
    xfine = [("xT", i, c) for i in range(NT) for c in range(ncy)]
    k.fence(xkeys, xfine + xkeys)
    for tb in range(NTB):
        for c in range(ncy):
            row0 = (s * 8 + c) * 128
            k.dma(k.xT[:, c, tb * TB:(tb + 1) * TB], k.yscr[row0:row0 + 128, tb * TB:(tb + 1) * TB],
                  [("yscr", s, c, tb)], [("xT", tb * (TB // 128) + i, c) for i in range(TB // 128)])
    gens = [out_tile_stream(k, s, tt_, tt_ % 4, x_src, dst, dkey, srckey, wo, gb, gbk, ncy) for tt_ in range(NT)]
    active = []
    pend = list(gens)
    while pend or active:
        if pend and len(active) < 4:
            active.append(pend.pop(0))
        for g in list(active):
            try:
                next(g)
            except StopIteration:
                active.remove(g)
    k.fence(fine + coarse + xfine + xkeys, coarse + xkeys)


def build(T, NSEQ, dbg=None, layers=(0, 1)):
    k = K(T, NSEQ, dbg)
    BW = 2080
    x_dram = k.din("x", [NSEQ * T, D])
    w_in0 = k.din("l0_w_in", [D, EVEN_COLS])
    w_out0 = k.din("l0_w_out", [D, D])
    gb0 = k.din("l0_gb", [1, 2048])
    PC = pack_l0_cols()
    NPV = PC["_n"]
    pv0_d = k.din("pv0", [128, NPV])
    wup_d = k.din("l0_wup", [128, 512])
    aup_d = k.din("l0_aup", [64, 512])
    w0x_d = k.din("l0_w0x", [128, 512])
    w_in1 = k.din("l1_w_in", [D, ODD_COLS])
    w_out1 = k.din("l1_w_out", [D, D])
    gb1 = k.din("l1_gb", [1, 2048])
    gup_d = k.din("l1_gup", [64, 512])
    gbx_d = k.din("l1_gbx", [128, 512])
    ng_d = k.din("l1_ng", [1, 1024])
    consts = host_consts()
    cd = {n: k.din("c_" + n, list(a.shape)) for n, a in consts.items()}
    out_d = k.dout("out", [NSEQ * T, D])
    k.yscr = k.dscr("yscr", [NSEQ * 8 * 128, T], BF16)
    x1scr = k.dscr("x1scr", [NSEQ * T, D])
    k.last_out = []
    k.wobf = {}
    for li, wsrc in ((0, w_out0), (1, w_out1)):
        if li in layers:
            scr = k.dscr(f"wobf{li}", [D, D], BF16)
            for q_ in range(4):
                k.dma(scr[q_ * 256:(q_ + 1) * 256, :], wsrc[q_ * 256:(q_ + 1) * 256, :], [], [("wobf", li, q_)],
                      eng="gpsimd")
            k.wobf[li] = scr
    k.xT = k.sb("xT", [128, 8, T], BF16)
    k.r_ps = k.ring("ps", 8, [128, 512], F32, psum=True)
    k.r_w128 = k.ring("w128_", 6, [128, 8, 128], BF16)
    k.r_tmp = k.ring("tmp", 12, [128, 512])
    k.r_keep = k.ring("keep", 4, [128, 512])
    k.big = [(k.sb(f"big{i}", [128, BW]), ("big", i)) for i in range(8)]
    k.r_xin = Ring.__new__(Ring)
    k.r_xin.bufs = [k.big[4], k.big[5]]
    k.r_xin.i = 0
    k.r_ar = k.ring("ar", 2, [128, 8, 128], BF16)
    k.r_sc = k.ring("sc", 10, [128, 512], BF16)
    k.r_sc2 = k.ring("scb", 5, [128, 512], BF16)
    k.r_am = k.ring("am", 4, [128, 1024], BF16)
    k.r_q = k.ring("q", 8, [128, 512], BF16)
    k.r_rf = k.ring("rf", 2, [128, 512], BF16)
    k.r_ch = k.ring("ch", 4, [128, 64], BF16)
    k.r_gc = k.ring("gc", 3, [128, 8])
    k.r_st = k.ring("st", 3, [128, 8])
    k.S32 = k.sb("S32", [128, 4, 64])
    k.Sbf = k.sb("Sbf", [128, 4, 64], BF16)
    k.wga = {}
    pv = k.sb("pv", [128, NPV])
    k.dma(pv[:, :], pv0_d[:, :], [], ["pv"])
    C = {}
    for n, a in consts.items():
        if n in BF_CONSTS:
            C[n] = k.sb("C_" + n, list(a.shape), BF16)
            k.dma(C[n][:, :], cd[n][:, :], [], [n], eng="gpsimd")
        else:
            C[n] = k.sb("C_" + n, list(a.shape))
            k.dma(C[n][:, :], cd[n][:, :], [], [n])
    k.maskA = [C["maskA0"], C["maskA1"]]
    k.maskN = [C["maskN0"], C["maskN1"]]
    k.eye8 = C["eye8"]
    k.identb = k.sb("identb", [128, 128], BF16)
    k.dma(k.identb[:, :], cd["ident"][:, :], [], ["identb"], eng="gpsimd")
    k.wup = k.sb("wup", [128, 512])
    k.dma(k.wup[:, :], wup_d[:, :], [], ["wup"])
    k.aup = k.sb("aup", [64, 512])
    k.dma(k.aup[:, :], aup_d[:, :], [], ["aup"])
    k.w0x = k.sb("w0x", [128, 512])
    k.dma(k.w0x[:, :], w0x_d[:, :], [], ["w0x"])
    k.gup = k.sb("gup", [64, 512])
    k.dma(k.gup[:, :], gup_d[:, :], [], ["gup"])
    k.gbx = k.sb("gbx", [128, 512])
    k.dma(k.gbx[:, :], gbx_d[:, :], [], ["gbx"])
    k.ng_d = ng_d
    k.Sg32 = k.sb("Sg32", [128, 256])
    k.Sgbf = k.sb("Sgbf", [128, 256], BF16)
    k.ssq = k.sb("ssq", [128, 48])
    k.onec = k.sb("onec", [128, 2])
    k.memset(k.onec[:, 0:1], 1.0, ["onec"])
    k.memset(k.onec[:, 1:2], 1e-6, ["onec"])
    eps = k.sb("epsv", [128, 4])
    k.memset(eps[:, 0:1], 1e-5, ["eps"])
    k.memset(eps[:, 1:2], 1e-12, ["eps"])
    k.memset(eps[:, 2:3], 64e-5, ["eps"])
    k.eps_ln, k.eps_kk, k.eps_gn = eps[:, 0:1], eps[:, 1:2], eps[:, 2:3]
    k.blk64s = k.sb("blk64s", [128, 128])
    k.ts(k.blk64s[:, :], C["blk64"][:, :], 1.0 / 64.0, None, ALU.mult, None, ["blk64"], ["blk64s"])
    k.omk = k.sb("omk", [128, 4])
    k.ts(k.omk[:, :], pv[:, PC["k_a"]:PC["k_a"] + 4], -0.5, 1.0, ALU.mult, ALU.add, ["pv"], ["omk"])
    k.hka = k.sb("hka", [128, 4])
    k.ts(k.hka[:, :], pv[:, PC["k_a"]:PC["k_a"] + 4], 0.5, None, ALU.mult, None, ["pv"], ["hka"])
    k.ha0 = k.sb("ha0", [128, 8])
    k.ts(k.ha0[:, :], pv[:, PC["a0_0"]:PC["a0_0"] + 8], 0.5, None, ALU.mult, None, ["pv"], ["ha0"])
    k.rkh = k.sb("rkh", [128, 4])
    k.ts(k.rkh[:, :], pv[:, PC["r_k"]:PC["r_k"] + 4], 0.5, None, ALU.mult, None, ["pv"], ["rkh"])

    for s in range(NSEQ):
        if 0 in layers:
            build_xT(k, s, x_dram, w_in0, C)
            build_l0_front(k, s, x_dram, w_in0, pv, PC, C)
            build_l0_rwkv(k, s, w_in0, pv, PC, C)
            if 1 in layers:
                build_out_ln(k, s, x_dram, 0, gb0, x1scr, "x1")
            else:
                build_out_ln(k, s, x_dram, 0, gb0, out_d, "out")
                k.finals.extend(k.last_out)
            k.last_out = []
        if 1 in layers:
            src, sk = (x1scr, "x1") if 0 in layers else (x_dram, None)
            build_xT(k, s, src, w_in1, C, sk)
            build_l1_gla(k, s, w_in1, C)
            build_out_ln(k, s, src, 1, gb1, out_d, "out", sk)
            k.finals.extend(k.last_out)
            k.last_out = []
    return k.finish()


def host_inputs(inp, x_core):
    pv0, _ = pack_l0(inp)
    m = {"x": np.ascontiguousarray(x_core.reshape(-1, D)), "l0_w_in": np.asarray(inp["l0_w_in"]),
         "l0_w_out": np.asarray(inp["l0_w_out"]), "pv0": pv0,
         "l0_gb": np.concatenate([inp["l0_ln_g"], inp["l0_ln_b"]])[None, :].astype(np.float32),
         "l0_wup": np.ascontiguousarray(np.asarray(inp["l0_w_up"]).reshape(128, 512)),
         "l0_aup": np.ascontiguousarray(np.asarray(inp["l0_a_up"]).reshape(64, 512))}
    w0x = np.zeros((128, 512), np.float32)
    w0x[0] = inp["l0_w0"][0]
    w0x[64] = inp["l0_w0"][1]
    m["l0_w0x"] = w0x
    m["l1_w_in"] = np.asarray(inp["l1_w_in"])
    m["l1_w_out"] = np.asarray(inp["l1_w_out"])
    m["l1_gb"] = np.concatenate([inp["l1_ln_g"], inp["l1_ln_b"]])[None, :].astype(np.float32)
    gup = np.zeros((64, 512), np.float32)
    gup[0:16] = inp["l1_g_up"][0]
    gup[32:48] = inp["l1_g_up"][1]
    m["l1_gup"] = gup
    gbx = np.zeros((128, 512), np.float32)
    gbx[0] = inp["l1_g_bias"][0]
    gbx[32] = inp["l1_g_bias"][1]
    m["l1_gbx"] = gbx
    m["l1_ng"] = np.asarray(inp["l1_norm_g"])[None, :].astype(np.float32)
    for n, a in host_consts().items():
        m["c_" + n] = a
    return m


CG = -1.0 / 16.0
SEQ_STREAMS = False
RWKV_STAGGER = 5
GLA_STAGGER = 3


def interleave(gens, stagger=0):
    gens = list(gens)
    if SEQ_STREAMS:
        for g in gens:
            for _ in g:
                pass
        return
    active = []
    rnd = 0
    pending = list(enumerate(gens))
    while pending or active:
        while pending and pending[0][0] * stagger <= rnd:
            active.append(pending.pop(0)[1])
        for g in list(active):
            try:
                next(g)
            except StopIteration:
                active.remove(g)
        rnd += 1


class MiniRing:
    def __init__(self, bufs):
        self.bufs = list(bufs)
        self.i = 0

    def next(self):
        b = self.bufs[self.i % len(self.bufs)]
        self.i += 1
        return b


def gla_stream(k, s, h, d, R, C, written):
    T, TB, NTB, NT = k.T, k.TB, k.NTB, k.NT
    CPB = TB // 64
    TPB = TB // 128
    xT, xT_keys = k.xT, k.xT_keys
    lr, lrk = R["lr"]
    (wq, wqk), (wkk, wkkk), (wv0, wv0k), (wv1, wv1k) = R["w"]
    (qt, qtk), (kt, ktk), (tokK, tKk) = R["slots"]
    tokV, tVk = R["tokV"]
    S32t, S32k = R["S32"]
    Sbft, Sbfk = R["Sbf"]
    gC, gCk = R["gC"]
    oview = R["oview"]
    trans = R["trans"]
    S32 = [S32t[:, 0:256], S32t[:, 256:512]]
    Sbf = [Sbft[:, 0:256], Sbft[:, 256:512]]
    S32key = [(S32k, 0), (S32k, 1)]
    Sbfkey = [(Sbfk, 0), (Sbfk, 1)]
    cur = 0
    k.memset(S32[0], 0.0, [S32key[0]], eng="vector")
    k.memset(Sbf[0], 0.0, [Sbfkey[0]], eng="vector")
    blocks = range(NTB) if d == 0 else range(NTB - 1, -1, -1)
    for tb in blocks:
        psl, plk = k.r_ps.next()
        for i in range(TPB):
            tcol = slice(tb * TB + i * 128, tb * TB + (i + 1) * 128)
            k.mm(psl[:, i * 128:(i + 1) * 128], lr[d * 32:d * 32 + 16, tcol],
                 k.gup[d * 32:d * 32 + 16, h * 128:(h + 1) * 128], True, False, [lrk, "gup"], [plk])
            k.mm(psl[:, i * 128:(i + 1) * 128], C["ones128"][d * 32:d * 32 + 1, :],
                 k.gbx[d * 32:d * 32 + 1, h * 128:(h + 1) * 128], False, True, ["ones128", "gbx"], [plk])
        e1, e1k = k.r_tmp.next()
        k.act(e1[:, 0:TB], psl[:, 0:TB], AF.Exp, [plk], [e1k], scale=-1.0)
        sp, spk = k.r_tmp.next()
        k.act(sp[:, 0:TB], e1[:, 0:TB], AF.Ln, [e1k, "onec"], [spk], bias=k.onec[:, 0:1])
        pcs = []
        for x in (1, 2):
            pc_, pck = k.r_ps.next()
            for i in range(TPB):
                k.mm(pc_[:, i * 128:(i + 1) * 128], sp[:, i * 128:(i + 1) * 128],
                     C["cmask"][:, (d * 3 + x) * 128:(d * 3 + x + 1) * 128], True, True,
                     [spk, "cmask"], [pck])
            pcs.append((pc_, pck))
        Gi, Gik = k.r_tmp.next()
        k.act(Gi[:, 0:TB], pcs[0][0][:, 0:TB], AF.Exp, [pcs[0][1]], [Gik], scale=CG)
        Gn, Gnk = k.r_tmp.next()
        k.act(Gn[:, 0:TB], pcs[0][0][:, 0:TB], AF.Exp, [pcs[0][1]], [Gnk], scale=-CG)
        Gr, Grk = k.r_tmp.next()
        k.act(Gr[:, 0:TB], pcs[1][0][:, 0:TB], AF.Exp, [pcs[1][1]], [Grk], scale=CG)
        lastcol = 63 if d == 0 else 0
        k.act(gC[:, 0:CPB], pcs[0][0][:, 0:TB].rearrange("p (a b) -> p a b", b=64)[:, :, lastcol],
              AF.Exp, [pcs[0][1]], [gCk], scale=CG)
        psq, pqk = k.proj(h * 128, 128, tb, wq, wqk)
        k.stt(qt[:, 0:TB], psq[:, 0:TB], float(128 ** -0.5), Gi[:, 0:TB], ALU.mult, ALU.mult,
              [pqk, Gik], [qtk])
        psk, pkk = k.proj(512 + h * 128, 128, tb, wkk, wkkk)
        k.tt(kt[:, 0:TB], psk[:, 0:TB], Gn[:, 0:TB], ALU.mult, [pkk, Gnk], [ktk])
        kg, kgk = trans.next()
        k.tt(kg[:, 0:TB], psk[:, 0:TB], Gr[:, 0:TB], ALU.mult, [pkk, Grk], [kgk])
        pt_, ptk = k.r_ps.next()
        pt = pt_.bitcast(BF16)
        for i in range(TPB):
            k.tr(pt[:, i * 128:(i + 1) * 128], kg[:, i * 128:(i + 1) * 128], k.identb[:, :],
                 [kgk, "identb"], [ptk], signal=(i == TPB - 1))
        k.cp(tokK[:, 0:TB], pt[:, 0:TB], [ptk], [tKk], eng="scalar")
        yield
        pss, pssk = k.r_ps.next()
        for i in range(TPB):
            cc = slice(i * 128, (i + 1) * 128)
            k.mm(pss[:, cc], kt[:, cc], qt[:, cc], True, True, [ktk, qtk], [pssk])
        ST, STk = trans.next()
        k.tt(ST[:, 0:TB], pss[:, 0:TB], C[f"gmask{d}"][:, 0:TB], ALU.mult, [pssk, f"gmask{d}"], [STk])
        for i in range(TPB):
            tt_ = tb * TPB + i
            po, pok = k.r_ps.next()
            k.mm(po[:, 0:256], ST[:, i * 128:(i + 1) * 128], tokV[:, tt_ * 256:(tt_ + 1) * 256], True, True,
                 [STk, tVk], [pok])
            ov, ovk = oview(tt_)
            if (h, tt_) not in written:
                written.add((h, tt_))
                k.cp(ov, po[:, 0:256], [pok], [ovk], eng="scalar")
            else:
                k.tt(ov, ov, po[:, 0:256], ALU.add, [ovk, pok], [ovk])
        yield
        order = range(CPB) if d == 0 else range(CPB - 1, -1, -1)
        for ci in order:
            i, hp = ci // 2, (ci % 2) * 64
            tt_ = tb * TPB + i
            cc = slice(ci * 64, (ci + 1) * 64)
            nxt = 1 - cur
            po, pok = k.r_ps.next()
            k.mm(po[hp:hp + 64, 0:256], qt[:, cc], Sbf[cur], True, True, [qtk, Sbfkey[cur]], [pok])
            pS, pSk = k.r_ps.next()
            k.mm(pS[:, 0:256], tokK[hp:hp + 64, i * 128:(i + 1) * 128],
                 tokV[hp:hp + 64, tt_ * 256:(tt_ + 1) * 256], True, True, [tKk, tVk], [pSk])
            k.stt(S32[nxt], S32[cur], gC[:, ci:ci + 1], pS[:, 0:256], ALU.mult, ALU.add,
                  [S32key[cur], gCk, pSk], [S32key[nxt]])
            k.cp(Sbf[nxt], S32[nxt], [S32key[nxt]], [Sbfkey[nxt]], eng="scalar")
            ov, ovk = oview(tt_)
            k.tt(ov[hp:hp + 64, :], ov[hp:hp + 64, :], po[hp:hp + 64, 0:256], ALU.add, [ovk, pok], [ovk])
            cur = nxt
            yield


def build_l1_gla(k, s, w_in, C):
    T, TB, NTB, NT = k.T, k.TB, k.NTB, k.NT
    TPB = TB // 128
    xT, xT_keys = k.xT, k.xT_keys
    wt, wk = k.r_w128.next()
    k.memset(wt[:, :, 0:64], 0.0, [wk])
    k.dma(wt[:, :, 0:16], w_in[:, 3072:3088].rearrange("(kc p) m -> p kc m", p=128), [], [wk], eng="gpsimd")
    k.dma(wt[:, :, 32:48], w_in[:, 3088:3104].rearrange("(kc p) m -> p kc m", p=128), [], [wk], eng="gpsimd")
    lr, lrk = k.big[4]
    for tb in range(NTB):
        ps, pk = k.r_ps.next()
        for kc in range(8):
            k.mm(ps[0:64, 0:TB], wt[:, kc, 0:64], xT[:, kc, tb * TB:(tb + 1) * TB], kc == 0, kc == 7,
                 [wk] + xT_keys, [pk])
        k.cp(lr[0:64, tb * TB:(tb + 1) * TB], ps[0:64, 0:TB], [pk], [lrk], eng="scalar")
    ngb, ngk = k.big[5]
    k.dma(ngb[:, 0:1024], k.ng_d.partition_broadcast(128), [], [ngk])
    fine = []
    for i in range(4):
        fine += [(("big", i), j) for j in range(8)]
        fine += [(("keep", i), j) for j in range(2)] + [(("q", i), j) for j in range(2)]
    coarse = [("big", i) for i in range(4)] + [("keep", i) for i in range(4)] + [("q", i) for i in range(4)]
    k.fence(coarse, fine + coarse)
    NT = k.NT
    bf512 = k.r_sc.bufs + k.r_sc2.bufs
    trans = MiniRing(bf512[12:15])
    small = k.r_gc.bufs + k.r_st.bufs
    wtiles = k.r_w128.bufs + k.r_ar.bufs
    for pair in range(2):
        heads = (2 * pair, 2 * pair + 1)
        written = set()
        gens = []
        wi = 0

        def loadw(tile, c0):
            wt_, wk_ = tile
            k.dma(wt_[:, :, 0:128], w_in[:, c0:c0 + 128].rearrange("(kc p) m -> p kc m", p=128), [], [wk_],
                  eng="gpsimd")
            return tile

        ovs = {}
        for hi, h in enumerate(heads):
            ws = [loadw(wtiles[hi * 4 + 0], h * 128), loadw(wtiles[hi * 4 + 1], 512 + h * 128),
                  loadw(wtiles[hi * 4 + 2], 1024 + h * 256), loadw(wtiles[hi * 4 + 3], 1024 + h * 256 + 128)]
            oacc = [k.big[2 * hi], k.big[2 * hi + 1]]

            def oview(tt_, oacc=oacc):
                b, bk = oacc[tt_ // 8]
                return b[:, (tt_ % 8) * 256:(tt_ % 8 + 1) * 256], (bk, tt_ % 8)

            ovs[h] = oview
            tvb, tvk_ = k.big[6 + hi]
            tokVh = tvb.bitcast(BF16)
            for tt_ in range(NT):
                tcol = slice(tt_ * 128, (tt_ + 1) * 128)
                pv_, pvk = k.r_ps.next()
                for half in range(2):
                    wv, wvk = ws[2 + half]
                    for kc in range(8):
                        k.mm(pv_[:, half * 128:(half + 1) * 128], xT[:, kc, tcol], wv[:, kc, 0:128],
                             kc == 0, kc == 7, [wvk] + xT_keys, [pvk])
                k.cp(tokVh[:, tt_ * 256:(tt_ + 1) * 256], pv_[:, 0:256], [pvk], [tvk_],
                     eng=("scalar" if tt_ % 2 else "vector"))
            for d in range(2):
                si = hi * 2 + d
                R = {"lr": (lr, lrk), "w": ws, "slots": bf512[3 * si:3 * si + 3], "tokV": (tokVh, tvk_),
                     "S32": k.r_keep.bufs[si], "Sbf": k.r_q.bufs[si], "gC": small[si], "oview": oview,
                     "trans": trans}
                gens.append(gla_stream(k, s, h, d, R, C, written))
        interleave(gens, stagger=GLA_STAGGER)
        for hi, h in enumerate(heads):
            oview = ovs[h]
            wg0, wg0k = loadw(wtiles[hi * 4 + 0], 2048 + h * 256)
            wg1, wg1k = loadw(wtiles[hi * 4 + 1], 2048 + h * 256 + 128)
            ssq = k.ssq
            for tt_ in range(NT):
                ov, ovk = oview(tt_)
                junk, jk = k.r_tmp.next()
                k.p.op("scalar", lambda e, junk=junk, ov=ov, ssq=ssq, tt_=tt_: e.activation(
                    out=junk[:, 0:256], in_=ov, func=AF.Square, accum_out=ssq[:, tt_:tt_ + 1]), [ovk], [jk, "ssq"])
            k.act(ssq[:, 16:16 + NT], ssq[:, 0:NT], AF.Sqrt, ["ssq", "onec"], ["ssq"], bias=k.onec[:, 1:2],
                  scale=1.0 / 256.0)
            k.recip(ssq[:, 32:32 + NT], ssq[:, 16:16 + NT], ["ssq"], ["ssq"])
            for tt_ in range(NT):
                ov, ovk = oview(tt_)
                tcol = slice(tt_ * 128, (tt_ + 1) * 128)
                on, onk = k.r_tmp.next()
                k.stt(on[:, 0:256], ov, ssq[:, 32 + tt_:33 + tt_], ngb[:, h * 256:(h + 1) * 256], ALU.mult, ALU.mult,
                      [ovk, "ssq", ngk], [onk])
                pg, pgk = k.r_ps.next()
                for half, (wg, wgk) in enumerate(((wg0, wg0k), (wg1, wg1k))):
                    for kc in range(8):
                        k.mm(pg[:, half * 128:(half + 1) * 128], xT[:, kc, tcol], wg[:, kc, 0:128],
                             kc == 0, kc == 7, [wgk] + xT_keys, [pgk])
                sg, sgk = k.r_tmp.next()
                k.act(sg[:, 0:256], pg[:, 0:256], AF.Silu, [pgk], [sgk])
                yb, ybk = trans.next()
                k.tt(yb[:, 0:256], on[:, 0:256], sg[:, 0:256], ALU.mult, [onk, sgk], [ybk])
                pt_, ptk = k.r_ps.next()
                pt = pt_.bitcast(BF16)
                for half in range(2):
                    k.tr(pt[:, half * 128:(half + 1) * 128], yb[:, half * 128:(half + 1) * 128], k.identb[:, :],
                         [ybk, "identb"], [ptk], signal=(half == 1))
                yf, yfk = trans.next()
                k.cp(yf[:, 0:256], pt[:, 0:256], [ptk], [yfk], eng="vector")
                tb = tt_ // TPB
                for half in range(2):
                    row0 = (s * 8 + 2 * h + half) * 128
                    k.dma(k.yscr[row0:row0 + 128, tt_ * 128:(tt_ + 1) * 128], yf[:, half * 128:(half + 1) * 128],
                          [yfk], [("yscr", s, 2 * h + half, tb), ("yscrw", s, 2 * h + half, tt_)])
    k.fence(fine + coarse, coarse)


N_CORES = 8
SEQ_LEN = 2048
BATCH = 16


def kernel(**inputs):
    inp = {n: np.asarray(v) for n, v in inputs.items()}
    x = inp["x"]
    nseq = BATCH // N_CORES
    nc = build(SEQ_LEN, nseq, layers=(0, 1))
    in_maps = [host_inputs(inp, x[c * nseq:(c + 1) * nseq]) for c in range(N_CORES)]
    res = run_bass_kernel_spmd(nc, in_maps, core_ids=list(range(N_CORES)))
    outs = [np.asarray(r["out"]).reshape(nseq, SEQ_LEN, D) for r in res.results]
    return np.concatenate(outs, 0).astype(np.float32)
```
